# Optimizing a Trainium2 kernel written in Bass

```python
import jax, jax.numpy as jnp
from jax import lax
import numpy as np

D_MODEL = 1024
BATCH = 8
SEQ = 2048
DEPTH = 2
DEC_BATCH = 128
DEC_SEQ = 4
PAST_LEN = 16384
PAGE_SIZE = 128

N_META = 16
GLA_HEADS = 4
GLA_DK = D_MODEL // 2
GLA_DV = D_MODEL
GLA_DKH = GLA_DK // GLA_HEADS
GLA_DVH = GLA_DV // GLA_HEADS
GATE_RANK = 16
GATE_TAU = 16.0
GLA_CHUNK = 64
POOL_WIDTH = D_MODEL // 2
POOL_WINDOWS = (2, 4, 8, 16)
POOL_GROUPS = len(POOL_WINDOWS)
POOL_GW = POOL_WIDTH // POOL_GROUPS
POOL_BUF = max(POOL_WINDOWS) - 1
N_BRANCH = 2
EPS = 1e-6
IN_SIZES = (GLA_DK, GLA_DK, GLA_DV, GLA_DV, POOL_WIDTH, POOL_WIDTH, GATE_RANK, N_BRANCH * D_MODEL)
IN_OFFSETS = tuple(int(o) for o in np.cumsum(IN_SIZES)[:-1])
IN_COLS = int(sum(IN_SIZES))

kernel_name = "gla_pool_hybrid_step"


def rmsnorm(x, g):
    xf = x.astype(jnp.float32)
    y = xf * lax.rsqrt(jnp.mean(xf * xf, axis=-1, keepdims=True) + EPS)
    return (y * g.astype(jnp.float32)).astype(x.dtype)


def gla_chunk_step(S, chunk):
    q, k, v, la = chunk
    qf = q.astype(jnp.float32)
    kf = k.astype(jnp.float32)
    vf = v.astype(jnp.float32)
    b = jnp.cumsum(la, axis=2)
    C = q.shape[2]
    causal = jnp.tril(jnp.ones((C, C), dtype=bool))
    diff = b[:, :, :, None, :] - b[:, :, None, :, :]
    decay = jnp.exp(jnp.where(causal[None, None, :, :, None], diff, -jnp.inf))
    A = jnp.einsum('bhid,bhjd,bhijd->bhij', qf, kf, decay)
    o = jnp.einsum('bhij,bhjv->bhiv', A, vf) + jnp.einsum('bhid,bhdv->bhiv', qf * jnp.exp(b), S)
    bC = b[:, :, -1:, :]
    kd = kf * jnp.exp(bC - b)
    S_new = jnp.exp(bC[:, :, 0, :])[..., None] * S + jnp.einsum('bhjd,bhjv->bhdv', kd, vf)
    return S_new, o


def gla_scan(q, k, v, la, S0, chunk):
    B, L, H, _ = q.shape
    DV = v.shape[-1]
    n = L // chunk

    def to_chunks(t):
        return t.reshape(B, n, chunk, H, t.shape[-1]).transpose(1, 0, 3, 2, 4)

    S, o = lax.scan(gla_chunk_step, S0.astype(jnp.float32),
                    (to_chunks(q), to_chunks(k), to_chunks(v), to_chunks(la)))
    o = o.transpose(1, 0, 3, 2, 4).reshape(B, L, H, DV)
    return o, S


def pool_mix(u_ext, n_prev, pos0, w_pool, scale):
    B, T, C = u_ext.shape
    L = T - n_prev
    uf = u_ext.astype(jnp.float32)
    cs = jnp.concatenate([jnp.zeros((B, 1, C), jnp.float32), jnp.cumsum(uf, axis=1)], axis=1)
    idx = n_prev + np.arange(L)
    pos = pos0 + np.arange(L)
    hi = cs[:, idx + 1]
    u_new = uf[:, n_prev:]
    outs = []
    for g, w in enumerate(POOL_WINDOWS):
        sl = slice(g * POOL_GW, (g + 1) * POOL_GW)
        lo = cs[:, np.maximum(idx + 1 - w, 0), sl]
        cnt = jnp.asarray(np.minimum(w, pos + 1), jnp.float32)[None, :, None]
        z = (hi[..., sl] - lo) / cnt - u_new[..., sl]
        outs.append(jnp.einsum('blc,cd->bld', z, w_pool[g].astype(jnp.float32)))
    return jnp.concatenate(outs, axis=-1) * scale.astype(jnp.float32)


def run_layer(h, prompt, S0, buf, norm_g, w_in, w_alpha, b_alpha, gla_gain, w_a,
              pool_w, pool_scale, w_b, b_merge, w_out):
    B, L, _ = h.shape
    dt = h.dtype
    xn = rmsnorm(h, norm_g)
    z = xn @ w_in
    q, k, v, ga, u, gb, alow, mg = jnp.split(z, IN_OFFSETS, axis=-1)
    q = q.reshape(B, L, GLA_HEADS, GLA_DKH) * (GLA_DKH ** -0.5)
    k = k.reshape(B, L, GLA_HEADS, GLA_DKH)
    v = v.reshape(B, L, GLA_HEADS, GLA_DVH)
    la = jax.nn.log_sigmoid(alow.astype(jnp.float32) @ w_alpha.astype(jnp.float32)
                            + b_alpha.astype(jnp.float32)) / GATE_TAU
    la = la.reshape(B, L, GLA_HEADS, GLA_DKH)
    if prompt:
        S_meta = jnp.zeros((B, GLA_HEADS, GLA_DKH, GLA_DVH), jnp.float32)
        o_m, S1 = gla_scan(q[:, :N_META], k[:, :N_META], v[:, :N_META], la[:, :N_META], S_meta, N_META)
        o_r, S_new = gla_scan(q[:, N_META:], k[:, N_META:], v[:, N_META:], la[:, N_META:], S1, GLA_CHUNK)
        o = jnp.concatenate([o_m, o_r], axis=1)
        y_pool = pool_mix(u, 0, 0, pool_w, pool_scale)
        buf_new = u[:, -POOL_BUF:]
    else:
        o, S_new = gla_scan(q, k, v, la, S0, L)
        u_ext = jnp.concatenate([buf.astype(u.dtype), u], axis=1)
        y_pool = pool_mix(u_ext, POOL_BUF, PAST_LEN, pool_w, pool_scale)
        buf_new = u_ext[:, -POOL_BUF:]
    o = o * lax.rsqrt(jnp.mean(o * o, axis=-1, keepdims=True) + EPS)
    o = o * gla_gain.astype(jnp.float32).reshape(GLA_HEADS, GLA_DVH)
    o = o.reshape(B, L, GLA_DV) * jax.nn.silu(ga.astype(jnp.float32))
    ya = o.astype(dt) @ w_a
    yb = (y_pool * jax.nn.silu(gb.astype(jnp.float32))).astype(dt) @ w_b
    gates = jax.nn.sigmoid(mg.astype(jnp.float32) + b_merge.astype(jnp.float32))
    s_a, s_b = jnp.split(gates, N_BRANCH, axis=-1)
    merged = (s_a * ya.astype(jnp.float32) + s_b * yb.astype(jnp.float32)).astype(dt)
    return h + merged @ w_out, S_new, buf_new


def setup_inputs(seed: int = 0) -> dict:
    key = jax.random.key(seed)
    ks = jax.random.split(key, 18)
    f32 = jnp.float32
    nrm = lambda k, s, sc: jax.random.normal(k, s, f32) * sc
    return {
        "x_prompt": nrm(ks[0], (BATCH, SEQ, D_MODEL), 1.0),
        "x_sample": nrm(ks[1], (DEC_BATCH, DEC_SEQ, D_MODEL), 1.0),
        "state_gla": nrm(ks[2], (DEPTH, DEC_BATCH, GLA_HEADS, GLA_DKH, GLA_DVH), 1.0),
        "state_pool": nrm(ks[3], (DEPTH, DEC_BATCH, POOL_BUF, POOL_WIDTH), 1.0),
        "meta_tokens": nrm(ks[4], (N_META, D_MODEL), 1.0),
        "norm_g": 1.0 + nrm(ks[5], (DEPTH, D_MODEL), 0.02),
        "w_in": nrm(ks[6], (DEPTH, D_MODEL, IN_COLS), D_MODEL ** -0.5),
        "w_alpha": nrm(ks[7], (DEPTH, GATE_RANK, GLA_DK), GATE_RANK ** -0.5),
        "b_alpha": 2.0 + nrm(ks[8], (DEPTH, GLA_DK), 0.5),
        "gla_gain": 1.0 + nrm(ks[9], (DEPTH, GLA_DV), 0.02),
        "w_a": nrm(ks[10], (DEPTH, GLA_DV, D_MODEL), GLA_DV ** -0.5),
        "pool_w": nrm(ks[11], (DEPTH, POOL_GROUPS, POOL_GW, POOL_GW), POOL_GW ** -0.5),
        "pool_scale": 1.0 + nrm(ks[12], (DEPTH, POOL_WIDTH), 0.02),
        "w_b": nrm(ks[13], (DEPTH, POOL_WIDTH, D_MODEL), POOL_WIDTH ** -0.5),
        "b_merge": nrm(ks[14], (DEPTH, N_BRANCH * D_MODEL), 0.02),
        "w_out": nrm(ks[15], (DEPTH, D_MODEL, D_MODEL), D_MODEL ** -0.5),
        "final_norm_g": 1.0 + nrm(ks[16], (D_MODEL,), 0.02),
    }


def reference(x_prompt, x_sample, state_gla, state_pool, meta_tokens, norm_g, w_in, w_alpha,
              b_alpha, gla_gain, w_a, pool_w, pool_scale, w_b, b_merge, w_out, final_norm_g):
    B = x_prompt.shape[0]
    meta = jnp.broadcast_to(meta_tokens[None].astype(x_prompt.dtype), (B, N_META, D_MODEL))
    hp = jnp.concatenate([meta, x_prompt], axis=1)
    hs = x_sample
    sg_p, sp_p, sg_s, sp_s = [], [], [], []
    for l in range(DEPTH):
        lw = (norm_g[l], w_in[l], w_alpha[l], b_alpha[l], gla_gain[l], w_a[l],
              pool_w[l], pool_scale[l], w_b[l], b_merge[l], w_out[l])
        hp, S_p, buf_p = run_layer(hp, True, None, None, *lw)
        hs, S_s, buf_s = run_layer(hs, False, state_gla[l], state_pool[l], *lw)
        sg_p.append(S_p)
        sp_p.append(buf_p)
        sg_s.append(S_s)
        sp_s.append(buf_s)
    y_prompt = rmsnorm(hp[:, N_META:], final_norm_g)
    y_sample = rmsnorm(hs, final_norm_g)
    return (y_prompt, y_sample, jnp.stack(sg_p), jnp.stack(sp_p), jnp.stack(sg_s), jnp.stack(sp_s))
```

```python
import contextlib
import math

import numpy as np
import concourse.bass as bass
import concourse.mybir as mybir
from concourse.bass_utils import run_bass_kernel_spmd

F32 = mybir.dt.float32
BF16 = mybir.dt.bfloat16
AF = mybir.ActivationFunctionType
ALU = mybir.AluOpType

D = 1024
NL = 2
SEQ = 2048
NCORE = 8
NSS = 16
TS = 4
NMETA = 16
NM = NSS * TS + NMETA
NPT = SEQ // 128
NT = NPT + 1
EPS = 1e-6
IN_COLS = 6160
C_Q, C_K, C_V, C_GA, C_U, C_GB, C_AL, C_MG = 0, 512, 1024, 2048, 3072, 3584, 4096, 4112
P1COLS = 4112
WB_ELEMS = 36864
POOLW_OFF = 8 * P1COLS

CF_IDENT = 0
CF_CAUS = 128
CF_MASKM = 256
CF_RMP = 336
CF_RMM = 848
CF_INVCNT = 1168
CF_ROWMASK = 1232
CF_COLMASK = 1249
NCF = CF_COLMASK
CL_G, CL_BA, CL_GAIN, CL_PS, CL_BM = 0, 8, 12, 20, 24
NCOLS = 80

N_SP_SEMS = 40
STAGED_Q = "pool"
DEFER_GA_AT = 13
STAGE_PER_TILE = 3
N_POOL_SEMS = 8


class Sched:
    ENGS = ("pe", "act", "dve", "pool", "sp")

    debug_tags = False

    def __init__(self):
        self.ops = []
        self.last_w = {}
        self.readers = {}

    def add(self, eng, fn, r=(), w=(), dma=False, after=(), cost=300.0, lat=150.0, attach=False):
        i = len(self.ops)
        deps = {}
        dk = {}
        for k in r:
            d = self.last_w.get(k)
            if d is not None:
                deps[d] = "RAW"
        for k in w:
            d = self.last_w.get(k)
            if d is not None and d not in deps:
                deps[d] = "WAW"
                dk[d] = k
            for d in self.readers.get(k, ()):
                if d not in deps:
                    deps[d] = "WAR"
                    dk[d] = k
        for d in after:
            if d is not None:
                deps[d] = "RAW"
        for k in r:
            self.readers.setdefault(k, []).append(i)
        for k in w:
            self.last_w[k] = i
            self.readers[k] = []
        deps.pop(i, None)
        tag = None
        if self.debug_tags:
            import sys as _sys
            f = _sys._getframe(1)
            while f is not None and f.f_code.co_name in ("mm", "tr", "act", "tt", "ts", "stt", "cp", "memset", "dma",
                                                         "proj", "sigmoid_inplace", "rms_stats", "load_h", "p2_loads"):
                f = f.f_back
            tag = f.f_lineno if f is not None else None
        self.ops.append(dict(eng=eng, fn=fn, deps=deps, dma=dma, token=None, cost=float(cost), lat=float(lat),
                             tag=tag, dk=dk if self.debug_tags else None, ctx=getattr(self, "ctx", None),
                             attach=attach))
        return i

    def list_schedule(self):
        import heapq
        ops = self.ops
        n = len(ops)
        succ = [[] for _ in range(n)]
        npred = [0] * n
        for i, op in enumerate(ops):
            for d, kind in op["deps"].items():
                od = ops[d]
                soft = (not od["dma"]) and (not op["dma"]) and od["eng"] == op["eng"] and \
                    (op["eng"] == "pe" or kind != "RAW")
                succ[d].append((i, soft))
                npred[i] += 1
        ready_t = [0.0] * n
        done_t = [0.0] * n
        start_t = [0.0] * n
        engs = list(self.ENGS)
        pend = {e: [] for e in engs}
        avail = {e: [] for e in engs}
        free_t = {e: 0.0 for e in engs}
        dma_free = [0.0]
        for i in range(n):
            if npred[i] == 0:
                heapq.heappush(pend[ops[i]["eng"]], (0.0, i))
        nsched = 0
        order = []
        last_on = {}
        while nsched < n:
            best = None
            for e in engs:
                T = free_t[e]
                while pend[e] and pend[e][0][0] <= T:
                    heapq.heappush(avail[e], heapq.heappop(pend[e])[1])
                if avail[e]:
                    st_, idx = T, avail[e][0]
                elif pend[e]:
                    st_, idx = pend[e][0]
                else:
                    continue
                if best is None or (st_, idx) < (best[0], best[1]):
                    best = (st_, idx, e)
            st_, idx, e = best
            if avail[e] and avail[e][0] == idx:
                heapq.heappop(avail[e])
            else:
                heapq.heappop(pend[e])
            op = ops[idx]
            start_t[idx] = st_
            if self.debug_tags:
                cd = None
                for d_ in op["deps"]:
                    if cd is None or done_t[d_] > done_t[cd]:
                        cd = d_
                op["st"] = st_
                op["crit"] = cd if (cd is not None and done_t[cd] >= st_ - 1e-6) else ("eng", last_on.get(e))
                last_on[e] = idx
            if op["dma"]:
                issue = 60.0 if e == "sp" else 900.0
                free_t[e] = st_ + issue
                xfer_start = max(st_ + issue, dma_free[0])
                dma_free[0] = xfer_start + op["cost"]
                done_t[idx] = xfer_start + op["cost"] + 1800.0
            else:
                free_t[e] = st_ + op["cost"]
                done_t[idx] = st_ + op["cost"] + op["lat"]
            order.append(idx)
            nsched += 1
            for j, soft in succ[idx]:
                rt_ = free_t[e] if soft else done_t[idx]
                if ready_t[j] < rt_:
                    ready_t[j] = rt_
                npred[j] -= 1
                if npred[j] == 0:
                    heapq.heappush(pend[ops[j]["eng"]], (ready_t[j], j))
        self.sim_end = max(done_t) if n else 0.0
        if self.debug_tags:
            self.sim_done = done_t
            self.sim_ops_old = list(ops)
        remap = {old: new for new, old in enumerate(order)}
        new_ops = []
        for old in order:
            op = ops[old]
            op["deps"] = {remap[d]: k for d, k in op["deps"].items()}
            new_ops.append(op)
        self.ops = new_ops

    def finalize(self):
        self.list_schedule()
        ops = self.ops
        for op in ops:
            kept = []
            for d, kind in op["deps"].items():
                od = ops[d]
                if (not od["dma"]) and (not op["dma"]) and od["eng"] == op["eng"]:
                    if op["eng"] == "pe" or kind != "RAW":
                        continue
                kept.append(d)
            best = {}
            kept2 = []
            for d in kept:
                od = ops[d]
                if od["dma"]:
                    kept2.append(d)
                else:
                    if od["eng"] not in best or best[od["eng"]] < d:
                        best[od["eng"]] = d
            op["kept"] = kept2 + list(best.values())
        slot_last = {}
        gen = {}
        cnt_q = {"sp": 0, "pool": 0}
        nsl = {"sp": N_SP_SEMS, "pool": N_POOL_SEMS}
        for i, op in enumerate(ops):
            if op["dma"]:
                q = op["eng"]
                k = (q, cnt_q[q] % nsl[q])
                cnt_q[q] += 1
                gen[k] = gen.get(k, 0) + 1
                op["token"] = (("dma",) + k, 16 * gen[k])
                if k in slot_last:
                    op["kept"].append(slot_last[k])
                slot_last[k] = i
        self.dma_final = {k: 16 * g for k, g in gen.items()}
        needed = set()
        for op in ops:
            needed.update(op["kept"])
        cnt = {e: 0 for e in self.ENGS}
        for i, op in enumerate(ops):
            if not op["dma"] and i in needed:
                cnt[op["eng"]] += 1
                op["token"] = (("eng", op["eng"]), cnt[op["eng"]])

    def emit(self, nc):
        self.finalize()
        ops = self.ops
        with contextlib.ExitStack() as st:
            sems = {}
            for e in self.ENGS:
                sems[("eng", e)] = st.enter_context(nc.semaphore("s_" + e))
            for k in range(N_SP_SEMS):
                sems[("dma", "sp", k)] = st.enter_context(nc.semaphore("dsp_%d" % k))
            for k in range(N_POOL_SEMS):
                sems[("dma", "pool", k)] = st.enter_context(nc.semaphore("dpl_%d" % k))
            block = st.enter_context(nc.Block())

            def run_engine(ename, eng):
                seen = {}
                for op in ops:
                    if op["eng"] != ename:
                        continue
                    need = {}
                    for d in op["kept"]:
                        sk, val = ops[d]["token"]
                        if need.get(sk, 0) < val:
                            need[sk] = val
                    items = [(sk, val) for sk, val in need.items() if seen.get(sk, 0) < val]
                    attach = items.pop() if (items and op.get("attach")) else None
                    for sk, val in items:
                        eng.wait_ge(sems[sk], val)
                        seen[sk] = val
                    inst = op["fn"](eng)
                    if attach is not None:
                        inst._wait_ge(sems[attach[0]], attach[1])
                        seen[attach[0]] = attach[1]
                    if op["token"] is not None:
                        sk, val = op["token"]
                        inst.then_inc(sems[sk], 16 if op["dma"] else 1)
                if ename == "sp":
                    for k, v in self.dma_final.items():
                        sk = ("dma",) + k
                        if seen.get(sk, 0) < v:
                            eng.wait_ge(sems[sk], v)

            @block.tensor
            def _(e):
                run_engine("pe", e)

            @block.scalar
            def _(e):
                run_engine("act", e)

            @block.vector
            def _(e):
                run_engine("dve", e)

            @block.gpsimd
            def _(e):
                run_engine("pool", e)

            @block.sync
            def _(e):
                run_engine("sp", e)


def make_consts():
    cf = np.zeros((128, NCF), np.float32)
    cf[:, CF_IDENT:CF_IDENT + 128] = np.eye(128, dtype=np.float32)
    j = np.arange(128)[:, None]
    i = np.arange(128)[None, :]
    cf[:, CF_CAUS:CF_CAUS + 128] = (j <= i).astype(np.float32)
    mm = np.zeros((128, NM), np.float32)
    for a in range(NM):
        for b in range(NM):
            if a < 64 and b < 64:
                ok = (a // TS == b // TS) and a <= b
            elif a >= 64 and b >= 64:
                ok = a <= b
            else:
                ok = False
            mm[a, b] = 1.0 if ok else 0.0
    cf[:, CF_MASKM:CF_MASKM + NM] = mm
    rmp = np.ones(512, np.float32)
    rmp[0::128] = 0.0
    cf[:, CF_RMP:CF_RMP + 512] = rmp[None, :]
    rmm = np.ones((4, NM), np.float32)
    for t in range(NM):
        if (t < 64 and t % TS == 0) or t == 64:
            rmm[:, t] = 0.0
    cf[:, CF_RMM:CF_RMM + 4 * NM] = rmm.reshape(1, -1)
    inv = np.zeros((4, NMETA), np.float32)
    for g, w in enumerate((2, 4, 8, 16)):
        for p in range(NMETA):
            inv[g, p] = 1.0 / min(w, p + 1)
    cf[:, CF_INVCNT:CF_INVCNT + 64] = inv.reshape(1, -1)
    rm = np.zeros((128, NSS + 1), np.float32)
    for a in range(NM):
        if a < 64:
            rm[a, a // TS] = 1.0
        else:
            rm[a, NSS] = 1.0
    cf[:, CF_ROWMASK:CF_ROWMASK + NSS + 1] = rm
    return cf


def make_colmask():
    cm = np.zeros((NSS, NM), np.float32)
    for s in range(NSS):
        cm[s, TS * s:TS * s + TS] = 1.0
    return np.ascontiguousarray(np.broadcast_to(cm.reshape(1, -1), (128, NSS * NM)))


def build_nc():
    nc = bass.Bass("TRN2", target_bir_lowering=False)

    def din(name, shape, dt=F32):
        return nc.dram_tensor(name, list(shape), dt, kind="ExternalInput").ap()

    def dout(name, shape, dt=F32):
        return nc.dram_tensor(name, list(shape), dt, kind="ExternalOutput").ap()

    def dscr(name, shape, dt):
        return nc.dram_tensor(name, list(shape), dt, kind="Internal").ap()

    xp = din("xp", [SEQ, D])
    xs = din("xs", [NSS * TS, D])
    meta = din("meta", [NMETA, D])
    sgla = din("sgla", [NL, NSS, 4, 128, 256])
    spool = din("spool", [NL, NSS, 15, 512])
    w_in = din("w_in", [NL, D, IN_COLS])
    w_alpha = din("w_alpha", [NL, 16, 512])
    w_a = din("w_a", [NL, D, D])
    pool_w = din("pool_w", [NL, 4, 128, 128])
    w_b = din("w_b", [NL, 512, D])
    w_out = din("w_out", [NL, D, D])
    cols_d = din("cols", [128, NCOLS])
    fg_d = din("fgfull", [128, D])
    cf_d = din("cf", [128, NCF])
    cm_d = din("cm", [128, NSS * NM])

    y_p = dout("y_p", [SEQ, D])
    y_s = dout("y_s", [NSS * TS, D])
    nsg_p = dout("nsg_p", [NL, 4, 128, 256])
    nsp_p = dout("nsp_p", [NL, 15, 512])
    nsg_s = dout("nsg_s", [NL, NSS, 4, 128, 256])
    nsp_s = dout("nsp_s", [NL, NSS, 15, 512])

    H1 = dscr("H1", [NT * 128, D], F32)
    XN = dscr("XN", [NT, 128, 1024], BF16)
    OT = dscr("OT", [NT, 128, 1024], BF16)
    YP = dscr("YP", [NT, 128, 512], BF16)
    WS = {(0, 2): dscr("WS02", [128, WB_ELEMS], BF16), (1, 1): dscr("WS11", [128, WB_ELEMS], BF16),
          (1, 2): dscr("WS12", [128, WB_ELEMS], BF16)}

    S = Sched()
    with contextlib.ExitStack() as st:
        def sb(name, shape, dt):
            return st.enter_context(nc.sbuf_tensor("sb_" + name, list(shape), dt))

        WB = sb("WB", [128, WB_ELEMS], BF16)
        cf = sb("cf", [128, NCF], F32)
        cols = sb("cols", [128, NCOLS], F32)
        negc = sb("negc", [128, NCOLS], F32)
        walpha = sb("walpha", [16, NL * 512], F32)
        ident_bf = sb("ident_bf", [128, 128], BF16)
        mask4 = sb("mask4", [128, 512], BF16)
        maskm4 = sb("maskm4", [128, 4 * NM], BF16)
        colmask = sb("colmask", [128, NSS * NM], BF16)
        hb = [sb("hb%d" % i, [128, D], F32) for i in range(3)]
        xsbs = [sb("xsb%d" % i, [128, D], BF16) for i in range(2)]
        ss = sb("ss", [128, 16], F32)
        xnT = [sb("xnT%d" % i, [128, 1024], BF16) for i in range(3)]
        alow_sb = sb("alow_sb", [16, 128], F32)
        tmpA = sb("tmpA", [128, 512], F32)
        tmpB = sb("tmpB", [128, 512], F32)
        tmpC = sb("tmpC", [128, 512], F32)
        qT = sb("qT", [128, 512], BF16)
        kT = sb("kT", [128, 512], BF16)
        kdT = sb("kdT", [128, 512], BF16)
        ebc = sb("ebc", [128, 4], F32)
        ebcm = sb("ebcm", [128, 4 * 17], F32)
        v_tok = sb("v_tok", [128, D], BF16)
        kd_tok = sb("kd_tok", [128, 512], BF16)
        ATm = sb("ATm", [128, 512], BF16)
        Sst2 = [sb("Sst%d" % i, [128, 1024], F32) for i in range(2)]
        S_bf = sb("S_bf", [128, 1024], BF16)
        ss4 = sb("ss4", [128, 12], F32)
        on = sb("on", [128, D], BF16)
        eg = sb("eg", [128, 1024], F32)
        fgf = eg
        qT_m = sb("qT_m", [128, 4 * NM], BF16)
        kd_tok_m = sb("kd_tok_m", [128, 512], BF16)
        v_tok_m = sb("v_tok_m", [128, D], BF16)
        oTg = [sb("oTg%d" % i, [128, 1024], BF16) for i in range(2)]
        uext = sb("uext", [128, 4 * 143], F32)
        sA = sb("sA", [128, 4 * 143], F32)
        sB = sb("sB", [128, 4 * 143], F32)
        zT = sb("zT", [128, 512], BF16)
        eb = sb("eb", [128, 512], F32)
        ypTg = [sb("ypTg%d" % i, [128, 512], BF16) for i in range(2)]
        e2 = [sb("e2_%d" % i, [128, 256], F32) for i in range(2)]
        tbuf = [sb("tbuf%d" % i, [128, 256], F32) for i in range(2)]
        mergedT = [sb("mergedT%d" % i, [128, 1024], BF16) for i in range(2)]
        S_in_all = sb("S_in_all", [128, 4096], F32)
        S_in = [S_in_all[:, i * 1024:(i + 1) * 1024] for i in range(4)]
        S_bfs = [sb("S_bfs%d" % i, [128, 1024], BF16) for i in range(2)]
        qm = [sb("qm%d" % i, [128, 4 * NM], BF16) for i in range(2)]
        kdm = [sb("kdm%d" % i, [128, 512], BF16) for i in range(2)]
        sp_tok = sb("sp_tok", [128, 1024], F32)
        uextm = S_in_all[:, 0:1216]
        sAm = S_in_all[:, 1216:2432]
        sBm = S_in_all[:, 2432:3648]
        uextmeta = sb("uextmeta", [128, 4 * 31], F32)
        u_s_perm = sb("u_s_perm", [128, 256], F32)
        utok = sb("utok", [128, 512], F32)
        banks = [st.enter_context(nc.psum_tensor("bank%d" % i, [128, 512], F32)) for i in range(8)]

        free_banks = list(range(8))

        def nfree():
            return len(free_banks)

        def bank():
            i = free_banks.pop(0)
            return banks[i], "B%d" % i

        def free(key):
            i = int(key[1:])
            assert i not in free_banks
            free_banks.append(i)

        def fsz(ap):
            n_ = 1
            for d_ in list(ap.shape)[1:]:
                n_ *= int(d_)
            return n_

        phase_pe_ops = []

        def mm(out, lhsT, rhs, start, stop, r, w, sgc=False):
            N = fsz(rhs)
            c = max(N, 48) / 2.4 + 3.0
            if lhsT.dtype == F32:
                c *= 4.0
            i_ = S.add("pe", lambda e: e.matmul(out, lhsT=lhsT, rhs=rhs, start=start, stop=stop,
                                                skip_group_check=sgc), r=r, w=w, cost=c, lat=250.0, attach=True)
            phase_pe_ops.append(i_)
            return i_

        def tr(out, in_, ident, r, w):
            c = 64.0 if in_.dtype != F32 else 200.0
            return S.add("pe", lambda e: e.transpose(out, in_, ident), r=r, w=w, cost=c, lat=250.0, attach=True)

        def act(out, in_, func, r, w, bias=None, scale=None, accum_out=None):
            kw = {}
            if bias is not None:
                kw["bias"] = bias
            if scale is not None:
                kw["scale"] = scale
            if accum_out is not None:
                kw["accum_out"] = accum_out
            c = 200.0 + 0.75 * fsz(in_) + (200.0 if accum_out is not None else 0.0)
            return S.add("act", lambda e: e.activation(out=out, in_=in_, func=func, **kw), r=r, w=w, cost=c, lat=120.0)

        def ecost(eng, n_, f=1.0):
            if eng == "pool":
                return 220.0 + 1.0 * n_ * f
            return 70.0 + 1.05 * n_ * f

        def tt(eng, out, in0, in1, op, r, w):
            return S.add(eng, lambda e: e.tensor_tensor(out=out, in0=in0, in1=in1, op=op), r=r, w=w,
                         cost=ecost(eng, fsz(out)), lat=120.0)

        def ts(eng, out, in0, s1, r, w, s2=None, op0=ALU.mult, op1=None):
            c = ecost(eng, fsz(out))
            if op1 is None:
                return S.add(eng, lambda e: e.tensor_scalar(out=out, in0=in0, scalar1=s1, scalar2=None, op0=op0),
                             r=r, w=w, cost=c, lat=120.0)
            return S.add(eng, lambda e: e.tensor_scalar(out=out, in0=in0, scalar1=s1, scalar2=s2, op0=op0, op1=op1),
                         r=r, w=w, cost=c, lat=120.0)

        def stt(out, in0, scalar, in1, op0, op1, r, w):
            return S.add("dve", lambda e: e.scalar_tensor_tensor(out=out, in0=in0, scalar=scalar, in1=in1, op0=op0,
                                                                 op1=op1), r=r, w=w, cost=ecost("dve", fsz(out)),
                         lat=120.0)

        def cp(eng, out, in_, r, w):
            if eng == "act":
                return act(out, in_, AF.Copy, r, w)
            f = 3.3 if (eng == "pool" and out.dtype != in_.dtype) else 1.0
            return S.add(eng, lambda e: e.tensor_copy(out=out, in_=in_), r=r, w=w, cost=ecost(eng, fsz(out), f),
                         lat=120.0)

        def memset(eng, ap, val, w):
            return S.add(eng, lambda e: e.memset(ap, val), w=w, cost=ecost(eng, fsz(ap)))

        def dma(out, in_, r, w, eng="sp", after=(), **kw):
            nb = 1
            for d_ in list(out.shape):
                nb *= int(d_)
            nb *= 4 if out.dtype == F32 else 2
            if eng == "pool" and in_.dtype != out.dtype:
                nb *= 2
            return S.add(eng, lambda e: e.dma_start(out=out, in_=in_, **kw), r=r, w=w, dma=True, after=after,
                         cost=nb / 300.0)

        def V(ap2d, nchunk, n):
            return ap2d[:, 0:nchunk * n].rearrange("p (c m) -> p c m", m=n)

        def bfview(b):
            return b[:, 0:512].bitcast(BF16)

        def sigmoid_inplace(buf_ap, key):
            act(buf_ap, buf_ap, AF.Ln, r=[key], w=[key], bias=1.0, scale=1.0)
            act(buf_ap, buf_ap, AF.Exp, r=[key], w=[key], scale=-1.0)

        dma(cf[:, :], cf_d[:, :], r=[], w=["cf"])
        dma(cols[:, :], cols_d[:, :], r=[], w=["cols"])
        dma(walpha[:, :].rearrange("r (l c) -> r l c", l=NL), w_alpha.rearrange("l r c -> r l c"), r=[], w=["walpha"])
        ts("dve", negc[:, :], cols[:, :], -1.0, r=["cols"], w=["negc"])
        cp("dve", ident_bf[:, :], cf[:, CF_IDENT:CF_IDENT + 128], r=["cf"], w=["ident_bf"])
        for h in range(4):
            cp("dve", mask4[:, h * 128:(h + 1) * 128], cf[:, CF_CAUS:CF_CAUS + 128], r=["cf"], w=["mask4"])
            cp("dve", maskm4[:, h * NM:(h + 1) * NM], cf[:, CF_MASKM:CF_MASKM + NM], r=["cf"], w=["maskm4"])
        memset("dve", uextmeta[:, :], 0.0, w=["uextmeta"])
        memset("dve", xnT[2][:, :], 0.0, w=["xnT2"])
        for i in range(2):
            memset("dve", xnT[i][:, :], 0.0, w=["xnT%d" % i])
            memset("dve", oTg[i][:, :], 0.0, w=["oTg%d" % i])
            memset("dve", ypTg[i][:, :], 0.0, w=["ypTg%d" % i])

        ident_f = cf[:, CF_IDENT:CF_IDENT + 128]
        last_pe_of_phase = [None]

        def ntok(t):
            return NM if t == 0 else 128

        def w_groups(l, ph, dst):
            wv = w_in[l].rearrange("(kc p) n -> p kc n", p=128)
            out = []
            if ph == 1:
                W1d = dst[:, 0:8 * P1COLS].rearrange("p (kc n) -> p kc n", n=P1COLS)
                for name, c0, c1 in (("al", C_AL, C_AL + 16), ("qk", 0, 1024), ("v", 1024, 2048),
                                     ("ga", 2048, 3072), ("ugb", 3072, 4096)):
                    out.append(("W1%s_%d" % (name, l), W1d[:, :, c0:c1], wv[:, :, c0:c1]))
                out.append(("W1pw_%d" % l, dst[:, POOLW_OFF:POOLW_OFF + 512].rearrange("p (g d) -> p g d", g=4),
                            pool_w[l].rearrange("g c d -> c g d")))
            else:
                Wmgd = dst[:, 0:16384].rearrange("p (kc n) -> p kc n", n=2048)
                out.append(("W2mga_%d" % l, Wmgd[:, :, 0:1024], wv[:, :, C_MG:C_MG + 1024]))
                out.append(("W2mgb_%d" % l, Wmgd[:, :, 1024:2048], wv[:, :, C_MG + 1024:C_MG + 2048]))
                out.append(("W2a_%d" % l, dst[:, 16384:24576].rearrange("p (kc n) -> p kc n", n=1024),
                            w_a[l].rearrange("(kc p) n -> p kc n", p=128)))
                out.append(("W2b_%d" % l, dst[:, 24576:28672].rearrange("p (kc n) -> p kc n", n=1024),
                            w_b[l].rearrange("(kc p) n -> p kc n", p=128)))
                out.append(("W2o_%d" % l, dst[:, 28672:36864].rearrange("p (kc n) -> p kc n", n=1024),
                            w_out[l].rearrange("(kc p) n -> p kc n", p=128)))
            return out

        def stage_pieces():
            out = []
            for (l_, ph_) in ((0, 2), (1, 1), (1, 2)):
                for key, o_, i_ in w_groups(l_, ph_, WS[(l_, ph_)]):
                    shp = list(o_.shape)
                    if len(shp) == 3 and shp[1] in (4, 8) and not key.startswith("W1pw"):
                        nk = shp[1]
                        for kc in range(nk):
                            out.append((key, "%s_%d" % (key, kc), o_[:, kc, :], i_[:, kc, :]))
                    else:
                        out.append((key, key, o_, i_))
            return out

        stage_keys = {}

        def emit_stage(pieces, k):
            aft = (len(S.ops) - 1,)
            for _ in range(k):
                if not pieces:
                    return
                key, sub, o_, i_ = pieces.pop(0)
                stage_keys.setdefault(key, []).append("S" + sub)
                dma(o_, i_, r=[], w=["S" + sub], eng="pool", after=aft, max_dma_last_dim=4096)

        deferred_w = {}

        def load_w(l, ph, aft=None, only=None):
            if aft is None:
                aft = tuple(phase_pe_ops)
            late = ("W1ga", "W1ugb", "W1pw")
            if ph == 1 and only is None and DEFER_GA_AT >= 0:
                deferred_w[l] = aft
                only = "early"
            if (l, ph) not in WS:
                for key, o_, i_ in w_groups(l, ph, WB[:, :]):
                    il = key.startswith(late)
                    if (only == "early" and il) or (only == "late" and not il):
                        continue
                    dma(o_, i_, r=[], w=[key], eng="pool", after=aft, max_dma_last_dim=4096)
            else:
                src = w_groups(l, ph, WS[(l, ph)])
                for (key, o_, _), (_, so_, _) in zip(w_groups(l, ph, WB[:, :]), src):
                    il = key.startswith(late)
                    if (only == "early" and il) or (only == "late" and not il):
                        continue
                    dma(o_, so_, r=stage_keys.get(key, ["S" + key]), w=[key], after=aft, eng=STAGED_Q)

        def load_w_p1(l):
            load_w(l, 1)

        def load_w_p2(l):
            load_w(l, 2)

        def load_h(l, t):
            hs = t % 3
            n = ntok(t)
            key = "hb%d" % hs
            if l == 0:
                if t == 0:
                    dma(hb[hs][0:64, :], xs[:, :], r=[], w=[key])
                    dma(hb[hs][64:80, :], meta[:, :], r=[], w=[key + "m"])
                else:
                    dma(hb[hs][0:128, :], xp[(t - 1) * 128:t * 128, :], r=[], w=[key, key + "m"])
            else:
                dma(hb[hs][0:n, :], H1[t * 128:t * 128 + n, :], r=["H1_%d" % t], w=[key, key + "m"])

        def hkeys(l, t):
            hs = t % 3
            return ["hb%d" % hs, "hb%dm" % hs]

        def rms_stats(src_ap, n, rkeys, s2=0):
            xb, xbk, so, sk = xsbs[s2], "xsb%d" % s2, 4 * s2, "ss%d" % s2
            act(xb[0:n, :], src_ap, AF.Square, r=rkeys, w=[xbk, sk], accum_out=ss[0:n, so:so + 1])
            act(ss[0:n, so + 1:so + 2], ss[0:n, so:so + 1], AF.Ln, r=[sk], w=[sk], bias=EPS, scale=1.0 / D)
            act(ss[0:n, so + 2:so + 3], ss[0:n, so + 1:so + 2], AF.Exp, r=[sk], w=[sk], scale=-0.5)

        def front(l, t):
            n = ntok(t)
            s3, s2, hs, cb = t % 3, t % 2, t % 3, l * 40
            xk = "xnT%d" % s3
            xn3 = xnT[s3][:, :].rearrange("p (k m) -> p k m", m=128)
            hk = hkeys(l, t)
            xb, xbk, so, sk = xsbs[s2], "xsb%d" % s2, 4 * s2, "ss%d" % s2
            rms_stats(hb[hs][0:n, :], n, hk, s2)
            ts("dve", xb[0:n, :], hb[hs][0:n, :], ss[0:n, so + 2:so + 3], r=hk + [sk, xbk], w=[xbk])
            yield ("need", 1)
            bt, bk = bank()
            btb = bfview(bt)
            for kc in range(8):
                tr(btb[:, kc * n:(kc + 1) * n], xb[0:n, kc * 128:(kc + 1) * 128], ident_bf[0:n, 0:n],
                   r=[xbk, "ident_bf"], w=[bk])
            tt("dve", xn3[:, :, 0:n], V(btb, 8, n),
               cols[:, cb + CL_G:cb + CL_G + 8].unsqueeze(2).to_broadcast([128, 8, n]), ALU.mult,
               r=[bk, "cols"], w=[xk])
            free(bk)
            dma(XN[t], xnT[s3][:, :], r=[xk], w=["XN_%d" % t])

        def phase1_tile(l, t):
            n = ntok(t)
            sl = t % 2
            s3 = t % 3
            cb = l * 40
            W1 = WB[:, 0:8 * P1COLS].rearrange("p (kc n) -> p kc n", n=P1COLS)
            xk = "xnT%d" % s3
            xn3 = xnT[s3][:, :].rearrange("p (k m) -> p k m", m=128)
            if t + 2 < NT:
                load_h(l, t + 2)
            if t == 0:
                yield from front(l, 0)

            def proj(dst, c0, m, wkey, bkey):
                for kc in range(8):
                    mm(dst, W1[:, kc, c0:c0 + m], xn3[:, kc, 0:n], kc == 0, kc == 7, r=[wkey, xk], w=[bkey])

            yield ("need", 1)
            ba, bak = bank()
            proj(ba[0:16, 0:n], C_AL, 16, "W1al_%d" % l, bak)
            cp("act", alow_sb[0:16, 0:n], ba[0:16, 0:n], r=[bak], w=["alow"])
            free(bak)
            yield ("need", 1)
            bl, blk = bank()
            for h in range(4):
                mm(bl[:, h * n:(h + 1) * n], walpha[0:16, l * 512 + h * 128:l * 512 + (h + 1) * 128],
                   alow_sb[0:16, 0:n], True, True, r=["walpha", "alow"], w=[blk])
            for h in range(4):
                act(tmpA[:, h * n:(h + 1) * n], bl[:, h * n:(h + 1) * n], AF.Exp, r=[blk, "negc"], w=["tmpA"],
                    bias=negc[:, cb + CL_BA + h:cb + CL_BA + h + 1], scale=-1.0)
            free(blk)
            act(tmpA[:, 0:4 * n], tmpA[:, 0:4 * n], AF.Ln, r=["tmpA"], w=["tmpA"], bias=1.0, scale=1.0)
            rmc = CF_RMM if t == 0 else CF_RMP
            S.add("dve", lambda e: e.tensor_tensor_scan(out=tmpB[:, 0:4 * n], data0=cf[:, rmc:rmc + 4 * n],
                                                        data1=tmpA[:, 0:4 * n], initial=0.0, op0=ALU.mult,
                                                        op1=ALU.add),
                  r=["cf", "tmpA"], w=["tmpB"], cost=80.0 + 2.1 * 4 * n, lat=120.0)
            act(tmpA[:, 0:4 * n], tmpB[:, 0:4 * n], AF.Exp, r=["tmpB"], w=["tmpA"], scale=-1.0 / 16.0)
            act(tmpC[:, 0:4 * n], tmpB[:, 0:4 * n], AF.Exp, r=["tmpB"], w=["tmpC"], scale=1.0 / 16.0)
            if t == 0:
                tb3 = V(tmpB, 4, NM)
                act(V(ebcm, 4, 17)[:, :, 0:16].unsqueeze(3),
                    tb3[:, :, 0:64].rearrange("p h (s t) -> p h s t", t=TS)[:, :, :, TS - 1:TS],
                    AF.Exp, r=["tmpB"], w=["ebcm"], scale=-1.0 / 16.0)
                act(V(ebcm, 4, 17)[:, :, 16:17], tb3[:, :, NM - 1:NM], AF.Exp, r=["tmpB"], w=["ebcm"],
                    scale=-1.0 / 16.0)
            else:
                act(ebc[:, 0:4].unsqueeze(2), V(tmpB, 4, 128)[:, :, 127:128], AF.Exp, r=["tmpB"], w=["ebc"],
                    scale=-1.0 / 16.0)
            if t + 1 < NT:
                yield from front(l, t + 1)
            yield ("need", 1)
            bq, bqk = bank()
            for h in range(4):
                proj(bq[:, h * n:(h + 1) * n], C_Q + h * 128, 128, "W1qk_%d" % l, bqk)
            stt(qT[:, 0:4 * n], bq[:, 0:4 * n], 128.0 ** -0.5, tmpA[:, 0:4 * n], ALU.mult, ALU.mult,
                r=[bqk, "tmpA"], w=["qT"])
            free(bqk)
            yield ("need", 1)
            bkb, bkk = bank()
            for h in range(4):
                proj(bkb[:, h * n:(h + 1) * n], C_K + h * 128, 128, "W1qk_%d" % l, bkk)
            tt("dve", kT[:, 0:4 * n], bkb[:, 0:4 * n], tmpC[:, 0:4 * n], ALU.mult, r=[bkk, "tmpC"], w=["kT"])
            free(bkk)
            yield ("need", 2)
            bv = [bank(), bank()]
            for b in range(2):
                for kc in range(8):
                    mm(bv[b][0][0:n, 0:512], xn3[:, kc, 0:n], W1[:, kc, C_V + b * 512:C_V + (b + 1) * 512],
                       kc == 0, kc == 7, r=["W1v_%d" % l, xk], w=[bv[b][1]])
                cp("act", v_tok[0:n, b * 512:(b + 1) * 512], bv[b][0][0:n, 0:512], r=[bv[b][1]], w=["v_tok"])
                free(bv[b][1])
            if t == 0:
                k3 = V(kT, 4, NM)
                d3 = V(kdT, 4, NM)
                e3 = V(ebcm, 4, 17)
                tt("pool", d3[:, :, 0:64].rearrange("p h (s t) -> p h s t", t=TS),
                   k3[:, :, 0:64].rearrange("p h (s t) -> p h s t", t=TS),
                   e3[:, :, 0:16].unsqueeze(3).to_broadcast([128, 4, NSS, TS]), ALU.mult,
                   r=["kT", "ebcm"], w=["kdT"])
                tt("pool", d3[:, :, 64:NM], k3[:, :, 64:NM], e3[:, :, 16:17].to_broadcast([128, 4, NMETA]),
                   ALU.mult, r=["kT", "ebcm"], w=["kdT"])
            else:
                tt("pool", V(kdT, 4, 128), V(kT, 4, 128), ebc[:, 0:4].unsqueeze(2).to_broadcast([128, 4, 128]),
                   ALU.mult, r=["kT", "ebc"], w=["kdT"])
            yield ("need", 1)
            bt2, bt2k = bank()
            bt2b = bfview(bt2)
            for h in range(4):
                tr(bt2b[0:n, h * 128:(h + 1) * 128], kdT[:, h * n:(h + 1) * n], ident_bf[:, :],
                   r=["kdT", "ident_bf"], w=[bt2k])
            cp("act", kd_tok[0:n, 0:512], bt2b[0:n, 0:512], r=[bt2k], w=["kd_tok"])
            free(bt2k)
            yield ("need", 1)
            bA, bAk = bank()
            for h in range(4):
                mm(bA[0:n, h * n:(h + 1) * n], kT[:, h * n:(h + 1) * n], qT[:, h * n:(h + 1) * n], True, True,
                   r=["kT", "qT"], w=[bAk])
            mk = maskm4 if t == 0 else mask4
            tt("dve", ATm[0:n, 0:4 * n], bA[0:n, 0:4 * n], mk[0:n, 0:4 * n], ALU.mult,
               r=[bAk, "maskm4" if t == 0 else "mask4"], w=["ATm"])
            free(bAk)

            if t == 0:
                yield ("need", 2)
                bo = [bank(), bank()]
                for h in range(4):
                    b_, bk_ = bo[h // 2]
                    off = (h % 2) * 256
                    mm(b_[0:n, off:off + 256], ATm[0:n, h * n:(h + 1) * n], v_tok[0:n, h * 256:(h + 1) * 256],
                       h % 2 == 0, False, r=["ATm", "v_tok"], w=[bk_], sgc=True)
                cp("pool", qT_m[:, 0:4 * NM], qT[:, 0:4 * NM], r=["qT"], w=["qT_m"])
                cp("pool", kd_tok_m[0:n, :], kd_tok[0:n, :], r=["kd_tok"], w=["kd_tok_m"])
                cp("pool", v_tok_m[0:n, :], v_tok[0:n, :], r=["v_tok"], w=["v_tok_m"])
                ts("pool", kdm[0][0:n, :], kd_tok[0:n, :], cf[0:n, CF_ROWMASK + NSS:CF_ROWMASK + NSS + 1],
                   r=["kd_tok", "cf"], w=["kdm0"], s2=1.0, op1=ALU.mult)
                yield ("need", 2)
                bp = [bank(), bank()]
                for h in range(4):
                    b_, bk_ = bp[h // 2]
                    off = (h % 2) * 256
                    mm(b_[:, off:off + 256], kdm[0][0:n, h * 128:(h + 1) * 128],
                       v_tok[0:n, h * 256:(h + 1) * 256], True, True, r=["kdm0", "v_tok"], w=[bk_])
                for hp in range(2):
                    cp("dve", Sst2[0][:, hp * 512:(hp + 1) * 512], bp[hp][0][:, 0:512], r=[bp[hp][1]], w=["Sst0"])
                    free(bp[hp][1])
                yield "M"
                NSL = len(S_in)

                def ld(sn):
                    dma(S_in[sn % NSL][:, :].rearrange("p (h v) -> p h v", h=4),
                        sgla[l, sn].rearrange("h d v -> d h v"), r=[], w=["S_in%d" % (sn % NSL)])

                for s0 in range(NSL - 1):
                    ld(s0)
                for s in range(NSS):
                    i2 = s % 2
                    i3 = s % NSL
                    sk_ = "S_in%d" % i3
                    if s == DEFER_GA_AT and l in deferred_w:
                        load_w(l, 1, aft=deferred_w.pop(l) + (len(S.ops) - 1,), only="late")
                    if s + NSL - 1 < NSS:
                        ld(s + NSL - 1)
                    ts("pool", kdm[i2][0:n, :], kd_tok_m[0:n, :], cf[0:n, CF_ROWMASK + s:CF_ROWMASK + s + 1],
                       r=["kd_tok_m", "cf"], w=["kdm%d" % i2], s2=1.0, op1=ALU.mult)
                    cp("act", S_bfs[i2][:, :], S_in[i3][:, :], r=[sk_], w=["S_bfs%d" % i2])
                    tt("pool", V(qm[i2], 4, NM), V(qT_m, 4, NM),
                       colmask[:, s * NM:(s + 1) * NM].unsqueeze(1).to_broadcast([128, 4, NM]), ALU.mult,
                       r=["qT_m", "colmask"], w=["qm%d" % i2])
                    for h in range(4):
                        b_, bk_ = bo[h // 2]
                        off = (h % 2) * 256
                        mm(b_[0:n, off:off + 256], qm[i2][:, h * NM:(h + 1) * NM],
                           S_bfs[i2][:, h * 256:(h + 1) * 256], False, s == NSS - 1,
                           r=["qm%d" % i2, "S_bfs%d" % i2], w=[bk_], sgc=True)
                    for hp in range(2):
                        yield ("need", 1)
                        b_, bk_ = bank()
                        for h in (2 * hp, 2 * hp + 1):
                            off = (h % 2) * 256
                            mm(b_[:, off:off + 256], kdm[i2][0:n, h * 128:(h + 1) * 128],
                               v_tok_m[0:n, h * 256:(h + 1) * 256], True, True, r=["kdm%d" % i2, "v_tok_m"], w=[bk_])
                        for h in (2 * hp, 2 * hp + 1):
                            off = (h % 2) * 256
                            stt(S_in[i3][:, h * 256:(h + 1) * 256], S_in[i3][:, h * 256:(h + 1) * 256],
                                ebcm[:, h * 17 + s:h * 17 + s + 1], b_[:, off:off + 256], ALU.mult, ALU.add,
                                r=[sk_, "ebcm", bk_], w=[sk_])
                        free(bk_)
                    dma(nsg_s[l, s].rearrange("h d v -> d h v"),
                        S_in[i3][:, :].rearrange("p (h v) -> p h v", h=4), r=[sk_], w=[])
                if l in deferred_w:
                    load_w(l, 1, aft=deferred_w.pop(l) + (len(S.ops) - 1,), only="late")
                obanks = [(bo[h // 2][0][0:n, (h % 2) * 256:(h % 2) * 256 + 256], bo[h // 2][1]) for h in range(4)]
                obk = [bo[0][1], bo[1][1]]
            else:
                So, Sn = Sst2[(t - 1) % 2], Sst2[t % 2]
                Sok, Snk = "Sst%d" % ((t - 1) % 2), "Sst%d" % (t % 2)
                cp("pool", S_bf[:, :], So[:, :], r=[Sok], w=["S_bf"])
                yield ("need", 2)
                bo = [bank(), bank()]
                for h in range(4):
                    b_, bk_ = bo[h // 2]
                    off = (h % 2) * 256
                    mm(b_[0:n, off:off + 256], ATm[0:n, h * n:(h + 1) * n], v_tok[0:n, h * 256:(h + 1) * 256],
                       True, False, r=["ATm", "v_tok"], w=[bk_])
                    mm(b_[0:n, off:off + 256], qT[:, h * n:(h + 1) * n], S_bf[:, h * 256:(h + 1) * 256],
                       False, True, r=["qT", "S_bf"], w=[bk_])
                yield ("need", 2)
                bp = [bank(), bank()]
                for h in range(4):
                    b_, bk_ = bp[h // 2]
                    off = (h % 2) * 256
                    mm(b_[:, off:off + 256], kd_tok[0:n, h * 128:(h + 1) * 128], v_tok[0:n, h * 256:(h + 1) * 256],
                       True, True, r=["kd_tok", "v_tok"], w=[bk_])
                for h in range(4):
                    b_, bk_ = bp[h // 2]
                    off = (h % 2) * 256
                    stt(Sn[:, h * 256:(h + 1) * 256], So[:, h * 256:(h + 1) * 256], ebc[:, h:h + 1],
                        b_[:, off:off + 256], ALU.mult, ALU.add, r=[Sok, "ebc", bk_], w=[Snk])
                free(bp[0][1])
                free(bp[1][1])
                if t == NT - 1:
                    dma(nsg_p[l].rearrange("h d v -> d h v"), Sn[:, :].rearrange("p (h v) -> p h v", h=4),
                        r=[Snk], w=[])
                obanks = [(bo[h // 2][0][0:n, (h % 2) * 256:(h % 2) * 256 + 256], bo[h // 2][1]) for h in range(4)]
                obk = [bo[0][1], bo[1][1]]
            if t != 0:
                yield "M"

            for h in range(4):
                oap, ok_ = obanks[h]
                act(on[0:n, h * 256:(h + 1) * 256], oap, AF.Square, r=[ok_], w=["on", "ss4"],
                    accum_out=ss4[0:n, h:h + 1])
            act(ss4[0:n, 4:8], ss4[0:n, 0:4], AF.Ln, r=["ss4"], w=["ss4"], bias=EPS, scale=1.0 / 256.0)
            act(ss4[0:n, 8:12], ss4[0:n, 4:8], AF.Exp, r=["ss4"], w=["ss4"], scale=-0.5)
            yield
            for h in range(4):
                oap, ok_ = obanks[h]
                act(on[0:n, h * 256:(h + 1) * 256], oap, AF.Copy, r=[ok_, "ss4", "on"], w=["on"],
                    scale=ss4[0:n, 8 + h:9 + h])
            for k_ in obk:
                free(k_)
            gcols = cols[:, cb + CL_GAIN:cb + CL_GAIN + 8]
            for i in range(2):
                yield ("need", 1)
                b_, bk_ = bank()
                for c4 in range(4):
                    proj(b_[:, c4 * n:(c4 + 1) * n], C_GA + (4 * i + c4) * 128, 128, "W1ga_%d" % l, bk_)
                egi = eg[:, i * 4 * n:(i + 1) * 4 * n]
                act(egi, b_[:, 0:4 * n], AF.Exp, r=[bk_], w=["eg%d" % i], scale=-1.0)
                sigmoid_inplace(egi, "eg%d" % i)
                tt("pool", V(egi, 4, n), V(egi, 4, n),
                   gcols[:, 4 * i:4 * i + 4].unsqueeze(2).to_broadcast([128, 4, n]), ALU.mult,
                   r=["eg%d" % i, "cols"], w=["eg%d" % i])
                tt("dve", egi, b_[:, 0:4 * n], egi, ALU.mult, r=[bk_, "eg%d" % i], w=["eg%d" % i])
                free(bk_)
            yield ("need", 1)
            bT, bTk = bank()
            bTb = bfview(bT)
            for c in range(8):
                tr(bTb[:, c * n:(c + 1) * n], on[0:n, c * 128:(c + 1) * 128], ident_bf[0:n, 0:n],
                   r=["on", "ident_bf"], w=[bTk])
            ok2 = "oTg%d" % sl
            tt("dve", V(oTg[sl], 8, 128)[:, :, 0:n], V(bTb, 8, n), V(eg, 8, n), ALU.mult, r=[bTk, "eg0", "eg1"],
               w=[ok2])
            free(bTk)
            dma(OT[t], oTg[sl][:, :], r=[ok2], w=["OT_%d" % t])

            yield ("need", 1)
            bu, buk = bank()
            for g in range(4):
                proj(bu[:, g * n:(g + 1) * n], C_U + g * 128, 128, "W1ugb_%d" % l, buk)
            z3 = V(zT, 4, n)
            WIN = (2, 4, 8, 16)
            if t == 0:
                for rt in range(2):
                    dma(sp_tok[0:120, rt * 512:(rt + 1) * 512],
                        spool[l].rearrange("s p c -> (s p) c")[rt * 120:(rt + 1) * 120, :], r=[], w=["sp_tok%d" % rt])
                dma(nsp_s[l, :, 0:11, :], spool[l, :, 4:15, :], r=[], w=[])
                um = uextm[:, :].rearrange("p (g s e) -> p g s e", g=4, s=NSS)
                am = sAm[:, :].rearrange("p (g s e) -> p g s e", g=4, s=NSS)
                bm_ = sBm[:, :].rearrange("p (g s e) -> p g s e", g=4, s=NSS)
                for rt in range(2):
                    yield ("need", 1)
                    bs, bsk = bank()
                    for g in range(4):
                        tr(bs[:, g * 120:(g + 1) * 120], sp_tok[0:120, rt * 512 + g * 128:rt * 512 + (g + 1) * 128],
                           ident_f[0:120, 0:120], r=["sp_tok%d" % rt, "cf"], w=[bsk])
                    cp("act", um[:, :, rt * 8:(rt + 1) * 8, 0:15],
                       bs[:, 0:480].rearrange("p (g s e) -> p g s e", g=4, s=8), r=[bsk], w=["S_in0", "S_in1"])
                    free(bsk)
                u3 = V(bu, 4, NM)
                cp("act", um[:, :, :, 15:19], u3[:, :, 0:64].rearrange("p g (s t) -> p g s t", t=TS), r=[buk],
                   w=["S_in0", "S_in1"])
                ume = V(uextmeta, 4, 31)
                cp("act", ume[:, :, 15:31], u3[:, :, 64:NM], r=[buk], w=["uextmeta"])
                cp("act", u_s_perm[:, :].rearrange("p (g t s) -> p g s t", g=4, t=TS),
                   u3[:, :, 0:64].rearrange("p g (s t) -> p g s t", t=TS), r=[buk], w=["u_s_perm"])
                free(buk)
                yield
                tt("pool", am[:, :, :, 1:19], um[:, :, :, 1:19], um[:, :, :, 0:18], ALU.add, r=["S_in0", "S_in1"], w=["S_in1", "S_in2"])
                tt("pool", bm_[:, 1:4, :, 3:19], am[:, 1:4, :, 3:19], am[:, 1:4, :, 1:17], ALU.add, r=["S_in1", "S_in2"],
                   w=["S_in2", "S_in3"])
                tt("pool", am[:, 2:4, :, 7:19], bm_[:, 2:4, :, 7:19], bm_[:, 2:4, :, 3:15], ALU.add, r=["S_in2", "S_in3"],
                   w=["S_in1", "S_in2"])
                tt("pool", bm_[:, 3:4, :, 15:19], am[:, 3:4, :, 15:19], am[:, 3:4, :, 7:11], ALU.add, r=["S_in1", "S_in2"],
                   w=["S_in2", "S_in3"])
                for g in range(4):
                    src = am if g % 2 == 0 else bm_
                    stt(z3[:, g, 0:64].rearrange("p (s t) -> p s t", t=TS), src[:, g, :, 15:19], 1.0 / WIN[g],
                        um[:, g, :, 15:19], ALU.mult, ALU.subtract,
                        r=["S_in0", "S_in1", "S_in2", "S_in3"], w=["zT"])
                yield
                a3 = V(sA, 4, 143)
                b3 = V(sB, 4, 143)
                tt("pool", a3[:, :, 1:31], ume[:, :, 1:31], ume[:, :, 0:30], ALU.add, r=["uextmeta"], w=["sA"])
                tt("pool", b3[:, 1:4, 3:31], a3[:, 1:4, 3:31], a3[:, 1:4, 1:29], ALU.add, r=["sA"], w=["sB"])
                tt("pool", a3[:, 2:4, 7:31], b3[:, 2:4, 7:31], b3[:, 2:4, 3:27], ALU.add, r=["sB"], w=["sA"])
                tt("pool", b3[:, 3:4, 15:31], a3[:, 3:4, 15:31], a3[:, 3:4, 7:23], ALU.add, r=["sA"], w=["sB"])
                icn = cf[:, CF_INVCNT:CF_INVCNT + 64].rearrange("p (g e) -> p g e", g=4)
                for g in range(4):
                    src = a3 if g % 2 == 0 else b3
                    tt("pool", src[:, g, 15:31], src[:, g, 15:31], icn[:, g, :], ALU.mult, r=["sA", "sB", "cf"],
                       w=["sA", "sB"])
                    tt("dve", z3[:, g, 64:NM], src[:, g, 15:31], ume[:, g, 15:31], ALU.subtract,
                       r=["sA", "sB", "uextmeta"], w=["zT"])
                cp("pool", V(uext, 4, 143)[:, :, 0:15], ume[:, :, 16:31], r=["uextmeta"], w=["uext"])
                yield ("need", 1)
                bx, bxk = bank()
                for g in range(4):
                    tr(bx[0:64, g * 128:(g + 1) * 128], u_s_perm[:, g * 64:(g + 1) * 64], ident_f[:, :],
                       r=["u_s_perm", "cf"], w=[bxk])
                cp("act", utok[0:64, :], bx[0:64, :], r=[bxk], w=["utok"])
                free(bxk)
                for tt_ in range(TS):
                    dma(nsp_s[l, :, 11 + tt_, :], utok[tt_ * NSS:(tt_ + 1) * NSS, :], r=["utok"], w=[])
            else:
                ue = V(uext, 4, 143)
                a3 = V(sA, 4, 143)
                b3 = V(sB, 4, 143)
                cp("act", ue[:, :, 15:143], V(bu, 4, 128), r=[buk], w=["uext"])
                free(buk)
                yield
                tt("pool", a3[:, :, 1:143], ue[:, :, 1:143], ue[:, :, 0:142], ALU.add, r=["uext"], w=["sA"])
                tt("pool", b3[:, 1:4, 3:143], a3[:, 1:4, 3:143], a3[:, 1:4, 1:141], ALU.add, r=["sA"], w=["sB"])
                tt("pool", a3[:, 2:4, 7:143], b3[:, 2:4, 7:143], b3[:, 2:4, 3:139], ALU.add, r=["sB"], w=["sA"])
                tt("pool", b3[:, 3:4, 15:143], a3[:, 3:4, 15:143], a3[:, 3:4, 7:135], ALU.add, r=["sA"], w=["sB"])
                for g in range(4):
                    src = a3 if g % 2 == 0 else b3
                    stt(z3[:, g, :], src[:, g, 15:143], 1.0 / WIN[g], ue[:, g, 15:143], ALU.mult, ALU.subtract,
                        r=["sA", "sB", "uext"], w=["zT"])
                if t == NT - 1:
                    yield ("need", 1)
                    bx, bxk = bank()
                    for g in range(4):
                        tr(bx[0:15, g * 128:(g + 1) * 128], ue[:, g, 128:143], ident_f[:, :], r=["uext", "cf"],
                           w=[bxk])
                    cp("act", utok[0:15, :], bx[0:15, :], r=[bxk], w=["utok"])
                    free(bxk)
                    dma(nsp_p[l], utok[0:15, :], r=["utok"], w=[])
                else:
                    cp("pool", ue[:, :, 0:15], ue[:, :, 128:143], r=["uext"], w=["uext"])
            yield ("need", 1)
            bgb, bgbk = bank()
            for g in range(4):
                proj(bgb[:, g * n:(g + 1) * n], C_GB + g * 128, 128, "W1ugb_%d" % l, bgbk)
            act(eb[:, 0:4 * n], bgb[:, 0:4 * n], AF.Exp, r=[bgbk], w=["eb"], scale=-1.0)
            sigmoid_inplace(eb[:, 0:4 * n], "eb")
            tt("pool", V(eb, 4, n), V(eb, 4, n),
               cols[:, cb + CL_PS:cb + CL_PS + 4].unsqueeze(2).to_broadcast([128, 4, n]), ALU.mult,
               r=["eb", "cols"], w=["eb"])
            tt("dve", eb[:, 0:4 * n], bgb[:, 0:4 * n], eb[:, 0:4 * n], ALU.mult, r=[bgbk, "eb"], w=["eb"])
            free(bgbk)
            yield ("need", 1)
            by, byk = bank()
            pw = WB[:, POOLW_OFF:POOLW_OFF + 512].rearrange("p (g d) -> p g d", g=4)
            for g in range(4):
                mm(by[:, g * n:(g + 1) * n], pw[:, g, :], zT[:, g * n:(g + 1) * n], True, True,
                   r=["W1pw_%d" % l, "zT"], w=[byk])
            yk = "ypTg%d" % sl
            tt("dve", V(ypTg[sl], 4, 128)[:, :, 0:n], V(by, 4, n), V(eb, 4, n), ALU.mult, r=[byk, "eb"], w=[yk])
            free(byk)
            dma(YP[t], ypTg[sl][:, :], r=[yk], w=["YP_%d" % t])

        def p2_loads(l, t):
            sl = t % 2
            dma(xnT[sl][:, :], XN[t], r=["XN_%d" % t], w=["xnT%d" % sl])
            dma(oTg[sl][:, :], OT[t], r=["OT_%d" % t], w=["oTg%d" % sl])
            dma(ypTg[sl][:, :], YP[t], r=["YP_%d" % t], w=["ypTg%d" % sl])
            load_h(l, t)

        def phase2_tile(l, t):
            n = ntok(t)
            sl = t % 2
            hs = t % 3
            cb = l * 40
            hk = hkeys(l, t)
            xk, ok2, yk = "xnT%d" % sl, "oTg%d" % sl, "ypTg%d" % sl
            xn3 = V(xnT[sl], 8, 128)
            o3 = V(oTg[sl], 8, 128)
            y3 = V(ypTg[sl], 4, 128)
            Wmg = WB[:, 0:16384].rearrange("p (kc n) -> p kc n", n=2048)
            Wa = WB[:, 16384:24576].rearrange("p (kc n) -> p kc n", n=1024)
            Wb = WB[:, 24576:28672].rearrange("p (kc n) -> p kc n", n=1024)
            Wo = WB[:, 28672:36864].rearrange("p (kc n) -> p kc n", n=1024)
            if t + 1 < NT:
                p2_loads(l, t + 1)
            m3 = V(mergedT[sl], 8, 128)
            mk_ = "mergedT%d" % sl
            for c in range(8):
                yield ("need", 1)
                b_, bk_ = bank()
                es = c % 2
                for kc in range(8):
                    mm(b_[:, 0:n], Wmg[:, kc, c * 128:(c + 1) * 128], xn3[:, kc, 0:n], kc == 0, kc == 7,
                       r=["W2mga_%d" % l, xk], w=[bk_])
                for kc in range(8):
                    mm(b_[:, n:2 * n], Wmg[:, kc, 1024 + c * 128:1024 + (c + 1) * 128], xn3[:, kc, 0:n], kc == 0,
                       kc == 7, r=["W2mgb_%d" % l, xk], w=[bk_])
                for kc in range(8):
                    mm(b_[:, 2 * n:3 * n], Wa[:, kc, c * 128:(c + 1) * 128], o3[:, kc, 0:n], kc == 0, kc == 7,
                       r=["W2a_%d" % l, ok2], w=[bk_])
                for kc in range(4):
                    mm(b_[:, 3 * n:4 * n], Wb[:, kc, c * 128:(c + 1) * 128], y3[:, kc, 0:n], kc == 0, kc == 3,
                       r=["W2b_%d" % l, yk], w=[bk_])
                ek = "e2_%d" % es
                act(e2[es][:, 0:n], b_[:, 0:n], AF.Exp, r=[bk_, "negc"], w=[ek],
                    bias=negc[:, cb + CL_BM + c:cb + CL_BM + c + 1], scale=-1.0)
                act(e2[es][:, n:2 * n], b_[:, n:2 * n], AF.Exp, r=[bk_, "negc"], w=[ek],
                    bias=negc[:, cb + CL_BM + 8 + c:cb + CL_BM + 8 + c + 1], scale=-1.0)
                sigmoid_inplace(e2[es][:, 0:2 * n], ek)
                tk = "tbuf%d" % es
                tt("dve", tbuf[es][:, 0:2 * n], b_[:, 2 * n:4 * n], e2[es][:, 0:2 * n], ALU.mult, r=[bk_, ek], w=[tk])
                free(bk_)
                tt("pool", m3[:, c, 0:n], tbuf[es][:, 0:n], tbuf[es][:, n:2 * n], ALU.add, r=[tk], w=[mk_])
            yield "M"
            yield
            for b in range(2):
                yield ("need", 1)
                b_, bk_ = bank()
                for kc in range(8):
                    mm(b_[0:n, 0:512], m3[:, kc, 0:n], Wo[:, kc, b * 512:(b + 1) * 512], kc == 0, kc == 7,
                       r=["W2o_%d" % l, mk_], w=[bk_])
                tt("dve", hb[hs][0:n, b * 512:(b + 1) * 512], b_[0:n, 0:512], hb[hs][0:n, b * 512:(b + 1) * 512],
                   ALU.add, r=[bk_] + hk, w=hk)
                free(bk_)
            yield
            if l == 0:
                dma(H1[t * 128:t * 128 + n, :], hb[hs][0:n, :], r=hk, w=["H1_%d" % t])
            else:
                rms_stats(hb[hs][0:n, :], n, hk, t % 2)
                so_ = 4 * (t % 2)
                stt(hb[hs][0:n, :], hb[hs][0:n, :], ss[0:n, so_ + 2:so_ + 3], fgf[0:n, :], ALU.mult, ALU.mult,
                    r=hk + ["ss%d" % (t % 2), "eg0", "eg1"], w=hk)
                if t == 0:
                    dma(y_s[:, :], hb[hs][0:64, :], r=hk, w=[])
                else:
                    dma(y_p[(t - 1) * 128:t * 128, :], hb[hs][0:128, :], r=hk, w=[])

        def zipper(gens, tag=None):
            state = {}

            def step(g, must):
                pend = state.get(id(g))
                if pend is not None:
                    if nfree() < pend:
                        if must:
                            raise RuntimeError("PSUM banks exhausted")
                        return "blocked"
                    state[id(g)] = None
                while True:
                    try:
                        S.ctx = gctx.get(id(g))
                        v = next(g)
                    except StopIteration:
                        return "done"
                    if isinstance(v, tuple) and v[0] == "need":
                        if nfree() >= v[1]:
                            continue
                        state[id(g)] = v[1]
                        if must:
                            raise RuntimeError("PSUM banks exhausted (need %d, free %d)" % (v[1], nfree()))
                        return "blocked"
                    return "M" if v == "M" else "ok"

            cur = None
            gctx = {id(g): (tag, gi) for gi, g in enumerate(gens)}
            for g in gens:
                done_cur = cur is None
                while True:
                    if not done_cur:
                        done_cur = step(cur, True) == "done"
                    r = step(g, done_cur)
                    if r == "M":
                        break
                while not done_cur:
                    done_cur = step(cur, True) == "done"
                cur = g
            while step(cur, True) != "done":
                pass

        for l in range(NL):
            load_w_p1(l)
            del phase_pe_ops[:]
            if l == 0:
                dma(colmask[:, :], cm_d[:, :], r=[], w=["colmask"], eng="pool", max_dma_last_dim=4096)
            load_h(l, 0)
            load_h(l, 1)
            gens1 = [phase1_tile(l, t) for t in range(NT)]
            if l == 0:
                st_pieces = stage_pieces()

                def with_staging(g_, k_):
                    emit_stage(st_pieces, k_)
                    yield from g_

                for t_ in range(2, NT):
                    gens1[t_] = with_staging(gens1[t_], STAGE_PER_TILE + (2 if t_ < 12 else 0))
            zipper(gens1, tag="L%dP1" % l)
            for i in range(len(S.ops) - 1, -1, -1):
                if S.ops[i]["eng"] == "pe":
                    last_pe_of_phase[0] = i
                    break
            load_w_p2(l)
            del phase_pe_ops[:]
            if l == NL - 1:
                dma(fgf[:, :], fg_d[:, :], r=[], w=["eg0", "eg1"])
            p2_loads(l, 0)
            gens2 = [phase2_tile(l, t) for t in range(NT)]
            if l == 0:
                for t_ in range(1, NT):
                    gens2[t_] = with_staging(gens2[t_], STAGE_PER_TILE)
            zipper(gens2, tag="L%dP2" % l)
            if l == 0:
                emit_stage(st_pieces, 1000)
            for i in range(len(S.ops) - 1, -1, -1):
                if S.ops[i]["eng"] == "pe":
                    last_pe_of_phase[0] = i
                    break
        S.emit(nc)
    return nc


def _cols_layout(norm_g, b_alpha, gla_gain, pool_scale, b_merge):
    cols = np.zeros((128, NCOLS), np.float32)
    for l in range(NL):
        cb = l * 40
        cols[:, cb + CL_G:cb + CL_G + 8] = norm_g[l].reshape(8, 128).T
        cols[:, cb + CL_BA:cb + CL_BA + 4] = b_alpha[l].reshape(4, 128).T
        cols[:, cb + CL_GAIN:cb + CL_GAIN + 8] = gla_gain[l].reshape(8, 128).T
        cols[:, cb + CL_PS:cb + CL_PS + 4] = pool_scale[l].reshape(4, 128).T
        cols[:, cb + CL_BM:cb + CL_BM + 16] = b_merge[l].reshape(16, 128).T
    return cols


def kernel(x_prompt, x_sample, state_gla, state_pool, meta_tokens, norm_g, w_in, w_alpha, b_alpha, gla_gain,
           w_a, pool_w, pool_scale, w_b, b_merge, w_out, final_norm_g):
    f = lambda a: np.ascontiguousarray(np.asarray(a, dtype=np.float32))
    x_prompt, x_sample, state_gla, state_pool = f(x_prompt), f(x_sample), f(state_gla), f(state_pool)
    meta_tokens, w_in, w_alpha, w_a, pool_w, w_b, w_out = map(f, (meta_tokens, w_in, w_alpha, w_a, pool_w, w_b, w_out))
    cols = _cols_layout(f(norm_g), f(b_alpha), f(gla_gain), f(pool_scale), f(b_merge))
    fgfull = np.ascontiguousarray(np.broadcast_to(f(final_norm_g)[None, :], (128, D)))
    cf = make_consts()
    cm = make_colmask()
    nc = build_nc()
    in_maps = []
    for c in range(NCORE):
        in_maps.append({
            "xp": x_prompt[c],
            "xs": np.ascontiguousarray(x_sample[c * NSS:(c + 1) * NSS].reshape(NSS * TS, D)),
            "meta": meta_tokens,
            "sgla": np.ascontiguousarray(state_gla[:, c * NSS:(c + 1) * NSS]),
            "spool": np.ascontiguousarray(state_pool[:, c * NSS:(c + 1) * NSS]),
            "w_in": w_in, "w_alpha": w_alpha, "w_a": w_a, "pool_w": pool_w, "w_b": w_b, "w_out": w_out,
            "cols": cols, "fgfull": fgfull, "cf": cf, "cm": cm,
        })
    res = run_bass_kernel_spmd(nc, in_maps, core_ids=list(range(NCORE)))
    R = res.results
    y_prompt = np.stack([R[c]["y_p"] for c in range(NCORE)], axis=0)
    y_sample = np.concatenate([R[c]["y_s"].reshape(NSS, TS, D) for c in range(NCORE)], axis=0)
    nsg_p = np.stack([R[c]["nsg_p"] for c in range(NCORE)], axis=1)
    nsp_p = np.stack([R[c]["nsp_p"] for c in range(NCORE)], axis=1)
    nsg_s = np.concatenate([R[c]["nsg_s"] for c in range(NCORE)], axis=1)
    nsp_s = np.concatenate([R[c]["nsp_s"] for c in range(NCORE)], axis=1)
    return (y_prompt.astype(np.float32), y_sample.astype(np.float32), nsg_p.astype(np.float32),
            nsp_p.astype(np.float32), nsg_s.astype(np.float32), nsp_s.astype(np.float32))
```

```python
import contextlib
import math

import numpy as np
import concourse.bass as bass
import concourse.mybir as mybir
from concourse.bass_utils import run_bass_kernel_spmd

F32 = mybir.dt.float32
BF16 = mybir.dt.bfloat16
AF = mybir.ActivationFunctionType
ALU = mybir.AluOpType

D = 1024
NL = 2
SEQ = 2048
NCORE = 8
NSS = 16
TS = 4
NMETA = 16
NM = NSS * TS + NMETA
NPT = SEQ // 128
NT = NPT + 1
EPS = 1e-6
IN_COLS = 6160
C_Q, C_K, C_V, C_GA, C_U, C_GB, C_AL, C_MG = 0, 512, 1024, 2048, 3072, 3584, 4096, 4112
P1COLS = 4112
WB_ELEMS = 36864
POOLW_OFF = 8 * P1COLS

CF_IDENT = 0
CF_CAUS = 128
CF_MASKM = 256
CF_RMP = 336
CF_RMM = 848
CF_INVCNT = 1168
CF_ROWMASK = 1232
CF_COLMASK = 1249
NCF = CF_COLMASK
CL_G, CL_BA, CL_GAIN, CL_PS, CL_BM = 0, 8, 12, 20, 24
NCOLS = 80

N_SP_SEMS = 40
STAGED_Q = "pool"
DEFER_GA_AT = 12
STAGE_PER_TILE = 3
N_POOL_SEMS = 8


class Sched:
    ENGS = ("pe", "act", "dve", "pool", "sp")

    debug_tags = False

    def __init__(self):
        self.ops = []
        self.last_w = {}
        self.readers = {}

    def add(self, eng, fn, r=(), w=(), dma=False, after=(), cost=300.0, lat=150.0, attach=False):
        i = len(self.ops)
        deps = {}
        dk = {}
        for k in r:
            d = self.last_w.get(k)
            if d is not None:
                deps[d] = "RAW"
        for k in w:
            d = self.last_w.get(k)
            if d is not None and d not in deps:
                deps[d] = "WAW"
                dk[d] = k
            for d in self.readers.get(k, ()):
                if d not in deps:
                    deps[d] = "WAR"
                    dk[d] = k
        for d in after:
            if d is not None:
                deps[d] = "RAW"
        for k in r:
            self.readers.setdefault(k, []).append(i)
        for k in w:
            self.last_w[k] = i
            self.readers[k] = []
        deps.pop(i, None)
        tag = None
        if self.debug_tags:
            import sys as _sys
            f = _sys._getframe(1)
            while f is not None and f.f_code.co_name in ("mm", "tr", "act", "tt", "ts", "stt", "cp", "memset", "dma",
                                                         "proj", "sigmoid_inplace", "rms_stats", "load_h", "p2_loads"):
                f = f.f_back
            tag = f.f_lineno if f is not None else None
        self.ops.append(dict(eng=eng, fn=fn, deps=deps, dma=dma, token=None, cost=float(cost), lat=float(lat),
                             tag=tag, dk=dk if self.debug_tags else None, ctx=getattr(self, "ctx", None),
                             attach=attach))
        return i

    def list_schedule(self):
        import heapq
        ops = self.ops
        n = len(ops)
        succ = [[] for _ in range(n)]
        npred = [0] * n
        for i, op in enumerate(ops):
            for d, kind in op["deps"].items():
                od = ops[d]
                soft = (not od["dma"]) and (not op["dma"]) and od["eng"] == op["eng"] and \
                    (op["eng"] == "pe" or kind != "RAW")
                succ[d].append((i, soft))
                npred[i] += 1
        ready_t = [0.0] * n
        done_t = [0.0] * n
        start_t = [0.0] * n
        engs = list(self.ENGS)
        pend = {e: [] for e in engs}
        avail = {e: [] for e in engs}
        free_t = {e: 0.0 for e in engs}
        dma_free = [0.0]
        for i in range(n):
            if npred[i] == 0:
                heapq.heappush(pend[ops[i]["eng"]], (0.0, i))
        nsched = 0
        order = []
        last_on = {}
        while nsched < n:
            best = None
            for e in engs:
                T = free_t[e]
                while pend[e] and pend[e][0][0] <= T:
                    heapq.heappush(avail[e], heapq.heappop(pend[e])[1])
                if avail[e]:
                    st_, idx = T, avail[e][0]
                elif pend[e]:
                    st_, idx = pend[e][0]
                else:
                    continue
                if best is None or (st_, idx) < (best[0], best[1]):
                    best = (st_, idx, e)
            st_, idx, e = best
            if avail[e] and avail[e][0] == idx:
                heapq.heappop(avail[e])
            else:
                heapq.heappop(pend[e])
            op = ops[idx]
            start_t[idx] = st_
            if self.debug_tags:
                cd = None
                for d_ in op["deps"]:
                    if cd is None or done_t[d_] > done_t[cd]:
                        cd = d_
                op["st"] = st_
                op["crit"] = cd if (cd is not None and done_t[cd] >= st_ - 1e-6) else ("eng", last_on.get(e))
                last_on[e] = idx
            if op["dma"]:
                issue = 60.0 if e == "sp" else 900.0
                free_t[e] = st_ + issue
                xfer_start = max(st_ + issue, dma_free[0])
                dma_free[0] = xfer_start + op["cost"]
                done_t[idx] = xfer_start + op["cost"] + 1800.0
            else:
                free_t[e] = st_ + op["cost"]
                done_t[idx] = st_ + op["cost"] + op["lat"]
            order.append(idx)
            nsched += 1
            for j, soft in succ[idx]:
                rt_ = free_t[e] if soft else done_t[idx]
                if ready_t[j] < rt_:
                    ready_t[j] = rt_
                npred[j] -= 1
                if npred[j] == 0:
                    heapq.heappush(pend[ops[j]["eng"]], (ready_t[j], j))
        self.sim_end = max(done_t) if n else 0.0
        if self.debug_tags:
            self.sim_done = done_t
            self.sim_ops_old = list(ops)
        remap = {old: new for new, old in enumerate(order)}
        new_ops = []
        for old in order:
            op = ops[old]
            op["deps"] = {remap[d]: k for d, k in op["deps"].items()}
            new_ops.append(op)
        self.ops = new_ops

    def finalize(self):
        self.list_schedule()
        ops = self.ops
        for op in ops:
            kept = []
            for d, kind in op["deps"].items():
                od = ops[d]
                if (not od["dma"]) and (not op["dma"]) and od["eng"] == op["eng"]:
                    if op["eng"] == "pe" or kind != "RAW":
                        continue
                kept.append(d)
            best = {}
            kept2 = []
            for d in kept:
                od = ops[d]
                if od["dma"]:
                    kept2.append(d)
                else:
                    if od["eng"] not in best or best[od["eng"]] < d:
                        best[od["eng"]] = d
            op["kept"] = kept2 + list(best.values())
        slot_last = {}
        gen = {}
        cnt_q = {"sp": 0, "pool": 0}
        nsl = {"sp": N_SP_SEMS, "pool": N_POOL_SEMS}
        for i, op in enumerate(ops):
            if op["dma"]:
                q = op["eng"]
                k = (q, cnt_q[q] % nsl[q])
                cnt_q[q] += 1
                gen[k] = gen.get(k, 0) + 1
                op["token"] = (("dma",) + k, 16 * gen[k])
                if k in slot_last:
                    op["kept"].append(slot_last[k])
                slot_last[k] = i
        self.dma_final = {k: 16 * g for k, g in gen.items()}
        needed = set()
        for op in ops:
            needed.update(op["kept"])
        cnt = {e: 0 for e in self.ENGS}
        for i, op in enumerate(ops):
            if not op["dma"] and i in needed:
                cnt[op["eng"]] += 1
                op["token"] = (("eng", op["eng"]), cnt[op["eng"]])

    def emit(self, nc):
        self.finalize()
        ops = self.ops
        with contextlib.ExitStack() as st:
            sems = {}
            for e in self.ENGS:
                sems[("eng", e)] = st.enter_context(nc.semaphore("s_" + e))
            for k in range(N_SP_SEMS):
                sems[("dma", "sp", k)] = st.enter_context(nc.semaphore("dsp_%d" % k))
            for k in range(N_POOL_SEMS):
                sems[("dma", "pool", k)] = st.enter_context(nc.semaphore("dpl_%d" % k))
            block = st.enter_context(nc.Block())

            def run_engine(ename, eng):
                seen = {}
                for op in ops:
                    if op["eng"] != ename:
                        continue
                    need = {}
                    for d in op["kept"]:
                        sk, val = ops[d]["token"]
                        if need.get(sk, 0) < val:
                            need[sk] = val
                    items = [(sk, val) for sk, val in need.items() if seen.get(sk, 0) < val]
                    attach = items.pop() if (items and op.get("attach")) else None
                    for sk, val in items:
                        eng.wait_ge(sems[sk], val)
                        seen[sk] = val
                    inst = op["fn"](eng)
                    if attach is not None:
                        inst._wait_ge(sems[attach[0]], attach[1])
                        seen[attach[0]] = attach[1]
                    if op["token"] is not None:
                        sk, val = op["token"]
                        inst.then_inc(sems[sk], 16 if op["dma"] else 1)
                if ename == "sp":
                    for k, v in self.dma_final.items():
                        sk = ("dma",) + k
                        if seen.get(sk, 0) < v:
                            eng.wait_ge(sems[sk], v)

            @block.tensor
            def _(e):
                run_engine("pe", e)

            @block.scalar
            def _(e):
                run_engine("act", e)

            @block.vector
            def _(e):
                run_engine("dve", e)

            @block.gpsimd
            def _(e):
                run_engine("pool", e)

            @block.sync
            def _(e):
                run_engine("sp", e)


def make_consts():
    cf = np.zeros((128, NCF), np.float32)
    cf[:, CF_IDENT:CF_IDENT + 128] = np.eye(128, dtype=np.float32)
    j = np.arange(128)[:, None]
    i = np.arange(128)[None, :]
    cf[:, CF_CAUS:CF_CAUS + 128] = (j <= i).astype(np.float32)
    mm = np.zeros((128, NM), np.float32)
    for a in range(NM):
        for b in range(NM):
            if a < 64 and b < 64:
                ok = (a // TS == b // TS) and a <= b
            elif a >= 64 and b >= 64:
                ok = a <= b
            else:
                ok = False
            mm[a, b] = 1.0 if ok else 0.0
    cf[:, CF_MASKM:CF_MASKM + NM] = mm
    rmp = np.ones(512, np.float32)
    rmp[0::128] = 0.0
    cf[:, CF_RMP:CF_RMP + 512] = rmp[None, :]
    rmm = np.ones((4, NM), np.float32)
    for t in range(NM):
        if (t < 64 and t % TS == 0) or t == 64:
            rmm[:, t] = 0.0
    cf[:, CF_RMM:CF_RMM + 4 * NM] = rmm.reshape(1, -1)
    inv = np.zeros((4, NMETA), np.float32)
    for g, w in enumerate((2, 4, 8, 16)):
        for p in range(NMETA):
            inv[g, p] = 1.0 / min(w, p + 1)
    cf[:, CF_INVCNT:CF_INVCNT + 64] = inv.reshape(1, -1)
    rm = np.zeros((128, NSS + 1), np.float32)
    for a in range(NM):
        if a < 64:
            rm[a, a // TS] = 1.0
        else:
            rm[a, NSS] = 1.0
    cf[:, CF_ROWMASK:CF_ROWMASK + NSS + 1] = rm
    return cf


def make_colmask():
    cm = np.zeros((NSS, NM), np.float32)
    for s in range(NSS):
        cm[s, TS * s:TS * s + TS] = 1.0
    return np.ascontiguousarray(np.broadcast_to(cm.reshape(1, -1), (128, NSS * NM)))


def build_nc():
    nc = bass.Bass("TRN2", target_bir_lowering=False)

    def din(name, shape, dt=F32):
        return nc.dram_tensor(name, list(shape), dt, kind="ExternalInput").ap()

    def dout(name, shape, dt=F32):
        return nc.dram_tensor(name, list(shape), dt, kind="ExternalOutput").ap()

    def dscr(name, shape, dt):
        return nc.dram_tensor(name, list(shape), dt, kind="Internal").ap()

    xp = din("xp", [SEQ, D])
    xs = din("xs", [NSS * TS, D])
    meta = din("meta", [NMETA, D])
    sgla = din("sgla", [NL, NSS, 4, 128, 256])
    spool = din("spool", [NL, NSS, 15, 512])
    w_in = din("w_in", [NL, D, IN_COLS])
    w_alpha = din("w_alpha", [NL, 16, 512])
    w_a = din("w_a", [NL, D, D])
    pool_w = din("pool_w", [NL, 4, 128, 128])
    w_b = din("w_b", [NL, 512, D])
    w_out = din("w_out", [NL, D, D])
    cols_d = din("cols", [128, NCOLS])
    fg_d = din("fgfull", [128, D])
    cf_d = din("cf", [128, NCF])
    cm_d = din("cm", [128, NSS * NM])

    y_p = dout("y_p", [SEQ, D])
    y_s = dout("y_s", [NSS * TS, D])
    nsg_p = dout("nsg_p", [NL, 4, 128, 256])
    nsp_p = dout("nsp_p", [NL, 15, 512])
    nsg_s = dout("nsg_s", [NL, NSS, 4, 128, 256])
    nsp_s = dout("nsp_s", [NL, NSS, 15, 512])

    H1 = dscr("H1", [NT * 128, D], F32)
    XN = dscr("XN", [NT, 128, 1024], BF16)
    OT = dscr("OT", [NT, 128, 1024], BF16)
    YP = dscr("YP", [NT, 128, 512], BF16)
    WS = {(0, 2): dscr("WS02", [128, WB_ELEMS], BF16), (1, 1): dscr("WS11", [128, WB_ELEMS], BF16),
          (1, 2): dscr("WS12", [128, WB_ELEMS], BF16)}

    S = Sched()
    with contextlib.ExitStack() as st:
        def sb(name, shape, dt):
            return st.enter_context(nc.sbuf_tensor("sb_" + name, list(shape), dt))

        WB = sb("WB", [128, WB_ELEMS], BF16)
        cf = sb("cf", [128, NCF], F32)
        cols = sb("cols", [128, NCOLS], F32)
        negc = sb("negc", [128, NCOLS], F32)
        walpha = sb("walpha", [16, NL * 512], F32)
        ident_bf = sb("ident_bf", [128, 128], BF16)
        mask4 = sb("mask4", [128, 512], BF16)
        maskm4 = sb("maskm4", [128, 4 * NM], BF16)
        colmask = sb("colmask", [128, NSS * NM], BF16)
        hb = [sb("hb%d" % i, [128, D], F32) for i in range(3)]
        xsbs = [sb("xsb%d" % i, [128, D], BF16) for i in range(2)]
        ss = sb("ss", [128, 16], F32)
        xnT = [sb("xnT%d" % i, [128, 1024], BF16) for i in range(3)]
        alow_sb = sb("alow_sb", [16, 128], F32)
        tmpA = sb("tmpA", [128, 512], F32)
        tmpB = sb("tmpB", [128, 512], F32)
        tmpC = sb("tmpC", [128, 512], F32)
        qT = sb("qT", [128, 512], BF16)
        kT = sb("kT", [128, 512], BF16)
        kdT = sb("kdT", [128, 512], BF16)
        ebc = sb("ebc", [128, 4], F32)
        ebcm = sb("ebcm", [128, 4 * 17], F32)
        v_tok = sb("v_tok", [128, D], BF16)
        kd_tok = sb("kd_tok", [128, 512], BF16)
        ATm = sb("ATm", [128, 512], BF16)
        Sst2 = [sb("Sst%d" % i, [128, 1024], F32) for i in range(2)]
        S_bf = sb("S_bf", [128, 1024], BF16)
        ss4 = sb("ss4", [128, 12], F32)
        on = sb("on", [128, D], BF16)
        eg = sb("eg", [128, 1024], F32)
        fgf = eg
        qT_m = sb("qT_m", [128, 4 * NM], BF16)
        kd_tok_m = sb("kd_tok_m", [128, 512], BF16)
        v_tok_m = sb("v_tok_m", [128, D], BF16)
        oTg = [sb("oTg%d" % i, [128, 1024], BF16) for i in range(2)]
        uext = sb("uext", [128, 4 * 143], F32)
        sA = sb("sA", [128, 4 * 143], F32)
        sB = sb("sB", [128, 4 * 143], F32)
        zT = sb("zT", [128, 512], BF16)
        eb = sb("eb", [128, 512], F32)
        ypTg = [sb("ypTg%d" % i, [128, 512], BF16) for i in range(2)]
        e2 = [sb("e2_%d" % i, [128, 256], F32) for i in range(2)]
        tbuf = [sb("tbuf%d" % i, [128, 256], F32) for i in range(2)]
        mergedT = [sb("mergedT%d" % i, [128, 1024], BF16) for i in range(2)]
        S_in_all = sb("S_in_all", [128, 4096], F32)
        S_in = [S_in_all[:, i * 1024:(i + 1) * 1024] for i in range(4)]
        S_bfs = [sb("S_bfs%d" % i, [128, 1024], BF16) for i in range(2)]
        qm = [sb("qm%d" % i, [128, 4 * NM], BF16) for i in range(2)]
        kdm = [sb("kdm%d" % i, [128, 512], BF16) for i in range(2)]
        sp_tok = sb("sp_tok", [128, 1024], F32)
        uextm = S_in_all[:, 0:1216]
        sAm = S_in_all[:, 1216:2432]
        sBm = S_in_all[:, 2432:3648]
        uextmeta = sb("uextmeta", [128, 4 * 31], F32)
        u_s_perm = sb("u_s_perm", [128, 256], F32)
        utok = sb("utok", [128, 512], F32)
        banks = [st.enter_context(nc.psum_tensor("bank%d" % i, [128, 512], F32)) for i in range(8)]

        free_banks = list(range(8))

        def nfree():
            return len(free_banks)

        def bank():
            i = free_banks.pop(0)
            return banks[i], "B%d" % i

        def free(key):
            i = int(key[1:])
            assert i not in free_banks
            free_banks.append(i)

        def fsz(ap):
            n_ = 1
            for d_ in list(ap.shape)[1:]:
                n_ *= int(d_)
            return n_

        phase_pe_ops = []

        def mm(out, lhsT, rhs, start, stop, r, w, sgc=False):
            N = fsz(rhs)
            c = max(N, 48) / 2.4 + 3.0
            if lhsT.dtype == F32:
                c *= 4.0
            i_ = S.add("pe", lambda e: e.matmul(out, lhsT=lhsT, rhs=rhs, start=start, stop=stop,
                                                skip_group_check=sgc), r=r, w=w, cost=c, lat=250.0, attach=True)
            phase_pe_ops.append(i_)
            return i_

        def tr(out, in_, ident, r, w):
            c = 64.0 if in_.dtype != F32 else 200.0
            return S.add("pe", lambda e: e.transpose(out, in_, ident), r=r, w=w, cost=c, lat=250.0, attach=True)

        def act(out, in_, func, r, w, bias=None, scale=None, accum_out=None):
            kw = {}
            if bias is not None:
                kw["bias"] = bias
            if scale is not None:
                kw["scale"] = scale
            if accum_out is not None:
                kw["accum_out"] = accum_out
            c = 200.0 + 0.75 * fsz(in_) + (200.0 if accum_out is not None else 0.0)
            return S.add("act", lambda e: e.activation(out=out, in_=in_, func=func, **kw), r=r, w=w, cost=c, lat=120.0)

        def ecost(eng, n_, f=1.0):
            if eng == "pool":
                return 220.0 + 1.0 * n_ * f
            return 70.0 + 1.05 * n_ * f

        def tt(eng, out, in0, in1, op, r, w):
            return S.add(eng, lambda e: e.tensor_tensor(out=out, in0=in0, in1=in1, op=op), r=r, w=w,
                         cost=ecost(eng, fsz(out)), lat=120.0)

        def ts(eng, out, in0, s1, r, w, s2=None, op0=ALU.mult, op1=None):
            c = ecost(eng, fsz(out))
            if op1 is None:
                return S.add(eng, lambda e: e.tensor_scalar(out=out, in0=in0, scalar1=s1, scalar2=None, op0=op0),
                             r=r, w=w, cost=c, lat=120.0)
            return S.add(eng, lambda e: e.tensor_scalar(out=out, in0=in0, scalar1=s1, scalar2=s2, op0=op0, op1=op1),
                         r=r, w=w, cost=c, lat=120.0)

        def stt(out, in0, scalar, in1, op0, op1, r, w):
            return S.add("dve", lambda e: e.scalar_tensor_tensor(out=out, in0=in0, scalar=scalar, in1=in1, op0=op0,
                                                                 op1=op1), r=r, w=w, cost=ecost("dve", fsz(out)),
                         lat=120.0)

        def cp(eng, out, in_, r, w):
            if eng == "act":
                return act(out, in_, AF.Copy, r, w)
            f = 3.3 if (eng == "pool" and out.dtype != in_.dtype) else 1.0
            return S.add(eng, lambda e: e.tensor_copy(out=out, in_=in_), r=r, w=w, cost=ecost(eng, fsz(out), f),
                         lat=120.0)

        def memset(eng, ap, val, w):
            return S.add(eng, lambda e: e.memset(ap, val), w=w, cost=ecost(eng, fsz(ap)))

        def dma(out, in_, r, w, eng="sp", after=(), **kw):
            nb = 1
            for d_ in list(out.shape):
                nb *= int(d_)
            nb *= 4 if out.dtype == F32 else 2
            if eng == "pool" and in_.dtype != out.dtype:
                nb *= 2
            return S.add(eng, lambda e: e.dma_start(out=out, in_=in_, **kw), r=r, w=w, dma=True, after=after,
                         cost=nb / 300.0)

        def V(ap2d, nchunk, n):
            return ap2d[:, 0:nchunk * n].rearrange("p (c m) -> p c m", m=n)

        def bfview(b):
            return b[:, 0:512].bitcast(BF16)

        def sigmoid_inplace(buf_ap, key):
            act(buf_ap, buf_ap, AF.Ln, r=[key], w=[key], bias=1.0, scale=1.0)
            act(buf_ap, buf_ap, AF.Exp, r=[key], w=[key], scale=-1.0)

        dma(cf[:, :], cf_d[:, :], r=[], w=["cf"])
        dma(cols[:, :], cols_d[:, :], r=[], w=["cols"])
        dma(walpha[:, :].rearrange("r (l c) -> r l c", l=NL), w_alpha.rearrange("l r c -> r l c"), r=[], w=["walpha"])
        ts("dve", negc[:, :], cols[:, :], -1.0, r=["cols"], w=["negc"])
        cp("dve", ident_bf[:, :], cf[:, CF_IDENT:CF_IDENT + 128], r=["cf"], w=["ident_bf"])
        for h in range(4):
            cp("dve", mask4[:, h * 128:(h + 1) * 128], cf[:, CF_CAUS:CF_CAUS + 128], r=["cf"], w=["mask4"])
            cp("dve", maskm4[:, h * NM:(h + 1) * NM], cf[:, CF_MASKM:CF_MASKM + NM], r=["cf"], w=["maskm4"])
        memset("dve", uextmeta[:, :], 0.0, w=["uextmeta"])
        memset("dve", xnT[2][:, :], 0.0, w=["xnT2"])
        for i in range(2):
            memset("dve", xnT[i][:, :], 0.0, w=["xnT%d" % i])
            memset("dve", oTg[i][:, :], 0.0, w=["oTg%d" % i])
            memset("dve", ypTg[i][:, :], 0.0, w=["ypTg%d" % i])

        ident_f = cf[:, CF_IDENT:CF_IDENT + 128]
        last_pe_of_phase = [None]

        def ntok(t):
            return NM if t == 0 else 128

        def w_groups(l, ph, dst):
            wv = w_in[l].rearrange("(kc p) n -> p kc n", p=128)
            out = []
            if ph == 1:
                W1d = dst[:, 0:8 * P1COLS].rearrange("p (kc n) -> p kc n", n=P1COLS)
                for name, c0, c1 in (("al", C_AL, C_AL + 16), ("qk", 0, 1024), ("v", 1024, 2048),
                                     ("ga", 2048, 3072), ("ugb", 3072, 4096)):
                    out.append(("W1%s_%d" % (name, l), W1d[:, :, c0:c1], wv[:, :, c0:c1]))
                out.append(("W1pw_%d" % l, dst[:, POOLW_OFF:POOLW_OFF + 512].rearrange("p (g d) -> p g d", g=4),
                            pool_w[l].rearrange("g c d -> c g d")))
            else:
                Wmgd = dst[:, 0:16384].rearrange("p (kc n) -> p kc n", n=2048)
                out.append(("W2mga_%d" % l, Wmgd[:, :, 0:1024], wv[:, :, C_MG:C_MG + 1024]))
                out.append(("W2mgb_%d" % l, Wmgd[:, :, 1024:2048], wv[:, :, C_MG + 1024:C_MG + 2048]))
                out.append(("W2a_%d" % l, dst[:, 16384:24576].rearrange("p (kc n) -> p kc n", n=1024),
                            w_a[l].rearrange("(kc p) n -> p kc n", p=128)))
                out.append(("W2b_%d" % l, dst[:, 24576:28672].rearrange("p (kc n) -> p kc n", n=1024),
                            w_b[l].rearrange("(kc p) n -> p kc n", p=128)))
                out.append(("W2o_%d" % l, dst[:, 28672:36864].rearrange("p (kc n) -> p kc n", n=1024),
                            w_out[l].rearrange("(kc p) n -> p kc n", p=128)))
            return out

        def stage_pieces():
            out = []
            for (l_, ph_) in ((0, 2), (1, 1), (1, 2)):
                for key, o_, i_ in w_groups(l_, ph_, WS[(l_, ph_)]):
                    shp = list(o_.shape)
                    if len(shp) == 3 and shp[1] in (4, 8) and not key.startswith("W1pw"):
                        nk = shp[1]
                        for kc in range(nk):
                            out.append((key, "%s_%d" % (key, kc), o_[:, kc, :], i_[:, kc, :]))
                    else:
                        out.append((key, key, o_, i_))
            return out

        stage_keys = {}

        def emit_stage(pieces, k):
            aft = (len(S.ops) - 1,)
            for _ in range(k):
                if not pieces:
                    return
                key, sub, o_, i_ = pieces.pop(0)
                stage_keys.setdefault(key, []).append("S" + sub)
                dma(o_, i_, r=[], w=["S" + sub], eng="pool", after=aft, max_dma_last_dim=4096)

        deferred_w = {}

        def load_w(l, ph, aft=None, only=None):
            if aft is None:
                aft = tuple(phase_pe_ops)
            late = ("W1ga", "W1ugb", "W1pw")
            if ph == 1 and only is None and DEFER_GA_AT >= 0:
                deferred_w[l] = aft
                only = "early"
            if (l, ph) not in WS:
                for key, o_, i_ in w_groups(l, ph, WB[:, :]):
                    il = key.startswith(late)
                    if (only == "early" and il) or (only == "late" and not il):
                        continue
                    dma(o_, i_, r=[], w=[key], eng="pool", after=aft, max_dma_last_dim=4096)
            else:
                src = w_groups(l, ph, WS[(l, ph)])
                for (key, o_, _), (_, so_, _) in zip(w_groups(l, ph, WB[:, :]), src):
                    il = key.startswith(late)
                    if (only == "early" and il) or (only == "late" and not il):
                        continue
                    dma(o_, so_, r=stage_keys.get(key, ["S" + key]), w=[key], after=aft, eng=STAGED_Q)

        def load_w_p1(l):
            load_w(l, 1)

        def load_w_p2(l):
            load_w(l, 2)

        def load_h(l, t):
            hs = t % 3
            n = ntok(t)
            key = "hb%d" % hs
            if l == 0:
                if t == 0:
                    dma(hb[hs][0:64, :], xs[:, :], r=[], w=[key])
                    dma(hb[hs][64:80, :], meta[:, :], r=[], w=[key + "m"])
                else:
                    dma(hb[hs][0:128, :], xp[(t - 1) * 128:t * 128, :], r=[], w=[key, key + "m"])
            else:
                dma(hb[hs][0:n, :], H1[t * 128:t * 128 + n, :], r=["H1_%d" % t], w=[key, key + "m"])

        def hkeys(l, t):
            hs = t % 3
            return ["hb%d" % hs, "hb%dm" % hs]

        def rms_stats(src_ap, n, rkeys, s2=0):
            xb, xbk, so, sk = xsbs[s2], "xsb%d" % s2, 4 * s2, "ss%d" % s2
            act(xb[0:n, :], src_ap, AF.Square, r=rkeys, w=[xbk, sk], accum_out=ss[0:n, so:so + 1])
            act(ss[0:n, so + 1:so + 2], ss[0:n, so:so + 1], AF.Ln, r=[sk], w=[sk], bias=EPS, scale=1.0 / D)
            act(ss[0:n, so + 2:so + 3], ss[0:n, so + 1:so + 2], AF.Exp, r=[sk], w=[sk], scale=-0.5)

        def front(l, t):
            n = ntok(t)
            s3, s2, hs, cb = t % 3, t % 2, t % 3, l * 40
            xk = "xnT%d" % s3
            xn3 = xnT[s3][:, :].rearrange("p (k m) -> p k m", m=128)
            hk = hkeys(l, t)
            xb, xbk, so, sk = xsbs[s2], "xsb%d" % s2, 4 * s2, "ss%d" % s2
            rms_stats(hb[hs][0:n, :], n, hk, s2)
            ts("dve", xb[0:n, :], hb[hs][0:n, :], ss[0:n, so + 2:so + 3], r=hk + [sk, xbk], w=[xbk])
            yield ("need", 1)
            bt, bk = bank()
            btb = bfview(bt)
            for kc in range(8):
                tr(btb[:, kc * n:(kc + 1) * n], xb[0:n, kc * 128:(kc + 1) * 128], ident_bf[0:n, 0:n],
                   r=[xbk, "ident_bf"], w=[bk])
            tt("dve", xn3[:, :, 0:n], V(btb, 8, n),
               cols[:, cb + CL_G:cb + CL_G + 8].unsqueeze(2).to_broadcast([128, 8, n]), ALU.mult,
               r=[bk, "cols"], w=[xk])
            free(bk)
            dma(XN[t], xnT[s3][:, :], r=[xk], w=["XN_%d" % t])

        def phase1_tile(l, t):
            n = ntok(t)
            sl = t % 2
            s3 = t % 3
            cb = l * 40
            W1 = WB[:, 0:8 * P1COLS].rearrange("p (kc n) -> p kc n", n=P1COLS)
            xk = "xnT%d" % s3
            xn3 = xnT[s3][:, :].rearrange("p (k m) -> p k m", m=128)
            if t + 2 < NT:
                load_h(l, t + 2)
            if t == 0:
                yield from front(l, 0)

            def proj(dst, c0, m, wkey, bkey):
                for kc in range(8):
                    mm(dst, W1[:, kc, c0:c0 + m], xn3[:, kc, 0:n], kc == 0, kc == 7, r=[wkey, xk], w=[bkey])

            yield ("need", 1)
            ba, bak = bank()
            proj(ba[0:16, 0:n], C_AL, 16, "W1al_%d" % l, bak)
            cp("act", alow_sb[0:16, 0:n], ba[0:16, 0:n], r=[bak], w=["alow"])
            free(bak)
            yield ("need", 1)
            bl, blk = bank()
            for h in range(4):
                mm(bl[:, h * n:(h + 1) * n], walpha[0:16, l * 512 + h * 128:l * 512 + (h + 1) * 128],
                   alow_sb[0:16, 0:n], True, True, r=["walpha", "alow"], w=[blk])
            for h in range(4):
                act(tmpA[:, h * n:(h + 1) * n], bl[:, h * n:(h + 1) * n], AF.Exp, r=[blk, "negc"], w=["tmpA"],
                    bias=negc[:, cb + CL_BA + h:cb + CL_BA + h + 1], scale=-1.0)
            free(blk)
            act(tmpA[:, 0:4 * n], tmpA[:, 0:4 * n], AF.Ln, r=["tmpA"], w=["tmpA"], bias=1.0, scale=1.0)
            rmc = CF_RMM if t == 0 else CF_RMP
            S.add("dve", lambda e: e.tensor_tensor_scan(out=tmpB[:, 0:4 * n], data0=cf[:, rmc:rmc + 4 * n],
                                                        data1=tmpA[:, 0:4 * n], initial=0.0, op0=ALU.mult,
                                                        op1=ALU.add),
                  r=["cf", "tmpA"], w=["tmpB"], cost=80.0 + 2.1 * 4 * n, lat=120.0)
            act(tmpA[:, 0:4 * n], tmpB[:, 0:4 * n], AF.Exp, r=["tmpB"], w=["tmpA"], scale=-1.0 / 16.0)
            act(tmpC[:, 0:4 * n], tmpB[:, 0:4 * n], AF.Exp, r=["tmpB"], w=["tmpC"], scale=1.0 / 16.0)
            if t == 0:
                tb3 = V(tmpB, 4, NM)
                act(V(ebcm, 4, 17)[:, :, 0:16].unsqueeze(3),
                    tb3[:, :, 0:64].rearrange("p h (s t) -> p h s t", t=TS)[:, :, :, TS - 1:TS],
                    AF.Exp, r=["tmpB"], w=["ebcm"], scale=-1.0 / 16.0)
                act(V(ebcm, 4, 17)[:, :, 16:17], tb3[:, :, NM - 1:NM], AF.Exp, r=["tmpB"], w=["ebcm"],
                    scale=-1.0 / 16.0)
            else:
                act(ebc[:, 0:4].unsqueeze(2), V(tmpB, 4, 128)[:, :, 127:128], AF.Exp, r=["tmpB"], w=["ebc"],
                    scale=-1.0 / 16.0)
            if t + 1 < NT:
                yield from front(l, t + 1)
            yield ("need", 1)
            bq, bqk = bank()
            for h in range(4):
                proj(bq[:, h * n:(h + 1) * n], C_Q + h * 128, 128, "W1qk_%d" % l, bqk)
            stt(qT[:, 0:4 * n], bq[:, 0:4 * n], 128.0 ** -0.5, tmpA[:, 0:4 * n], ALU.mult, ALU.mult,
                r=[bqk, "tmpA"], w=["qT"])
            free(bqk)
            yield ("need", 1)
            bkb, bkk = bank()
            for h in range(4):
                proj(bkb[:, h * n:(h + 1) * n], C_K + h * 128, 128, "W1qk_%d" % l, bkk)
            tt("dve", kT[:, 0:4 * n], bkb[:, 0:4 * n], tmpC[:, 0:4 * n], ALU.mult, r=[bkk, "tmpC"], w=["kT"])
            free(bkk)
            yield ("need", 2)
            bv = [bank(), bank()]
            for b in range(2):
                for kc in range(8):
                    mm(bv[b][0][0:n, 0:512], xn3[:, kc, 0:n], W1[:, kc, C_V + b * 512:C_V + (b + 1) * 512],
                       kc == 0, kc == 7, r=["W1v_%d" % l, xk], w=[bv[b][1]])
                cp("act", v_tok[0:n, b * 512:(b + 1) * 512], bv[b][0][0:n, 0:512], r=[bv[b][1]], w=["v_tok"])
                free(bv[b][1])
            if t == 0:
                k3 = V(kT, 4, NM)
                d3 = V(kdT, 4, NM)
                e3 = V(ebcm, 4, 17)
                tt("pool", d3[:, :, 0:64].rearrange("p h (s t) -> p h s t", t=TS),
                   k3[:, :, 0:64].rearrange("p h (s t) -> p h s t", t=TS),
                   e3[:, :, 0:16].unsqueeze(3).to_broadcast([128, 4, NSS, TS]), ALU.mult,
                   r=["kT", "ebcm"], w=["kdT"])
                tt("pool", d3[:, :, 64:NM], k3[:, :, 64:NM], e3[:, :, 16:17].to_broadcast([128, 4, NMETA]),
                   ALU.mult, r=["kT", "ebcm"], w=["kdT"])
            else:
                tt("pool", V(kdT, 4, 128), V(kT, 4, 128), ebc[:, 0:4].unsqueeze(2).to_broadcast([128, 4, 128]),
                   ALU.mult, r=["kT", "ebc"], w=["kdT"])
            yield ("need", 1)
            bt2, bt2k = bank()
            bt2b = bfview(bt2)
            for h in range(4):
                tr(bt2b[0:n, h * 128:(h + 1) * 128], kdT[:, h * n:(h + 1) * n], ident_bf[:, :],
                   r=["kdT", "ident_bf"], w=[bt2k])
            cp("act", kd_tok[0:n, 0:512], bt2b[0:n, 0:512], r=[bt2k], w=["kd_tok"])
            free(bt2k)
            yield ("need", 1)
            bA, bAk = bank()
            for h in range(4):
                mm(bA[0:n, h * n:(h + 1) * n], kT[:, h * n:(h + 1) * n], qT[:, h * n:(h + 1) * n], True, True,
                   r=["kT", "qT"], w=[bAk])
            mk = maskm4 if t == 0 else mask4
            tt("dve", ATm[0:n, 0:4 * n], bA[0:n, 0:4 * n], mk[0:n, 0:4 * n], ALU.mult,
               r=[bAk, "maskm4" if t == 0 else "mask4"], w=["ATm"])
            free(bAk)

            if t == 0:
                yield ("need", 2)
                bo = [bank(), bank()]
                for h in range(4):
                    b_, bk_ = bo[h // 2]
                    off = (h % 2) * 256
                    mm(b_[0:n, off:off + 256], ATm[0:n, h * n:(h + 1) * n], v_tok[0:n, h * 256:(h + 1) * 256],
                       h % 2 == 0, False, r=["ATm", "v_tok"], w=[bk_], sgc=True)
                cp("pool", qT_m[:, 0:4 * NM], qT[:, 0:4 * NM], r=["qT"], w=["qT_m"])
                cp("pool", kd_tok_m[0:n, :], kd_tok[0:n, :], r=["kd_tok"], w=["kd_tok_m"])
                cp("pool", v_tok_m[0:n, :], v_tok[0:n, :], r=["v_tok"], w=["v_tok_m"])
                ts("pool", kdm[0][0:n, :], kd_tok[0:n, :], cf[0:n, CF_ROWMASK + NSS:CF_ROWMASK + NSS + 1],
                   r=["kd_tok", "cf"], w=["kdm0"], s2=1.0, op1=ALU.mult)
                yield ("need", 2)
                bp = [bank(), bank()]
                for h in range(4):
                    b_, bk_ = bp[h // 2]
                    off = (h % 2) * 256
                    mm(b_[:, off:off + 256], kdm[0][0:n, h * 128:(h + 1) * 128],
                       v_tok[0:n, h * 256:(h + 1) * 256], True, True, r=["kdm0", "v_tok"], w=[bk_])
                for hp in range(2):
                    cp("dve", Sst2[0][:, hp * 512:(hp + 1) * 512], bp[hp][0][:, 0:512], r=[bp[hp][1]], w=["Sst0"])
                    free(bp[hp][1])
                yield "M"
                NSL = len(S_in)

                def ld(sn):
                    dma(S_in[sn % NSL][:, :].rearrange("p (h v) -> p h v", h=4),
                        sgla[l, sn].rearrange("h d v -> d h v"), r=[], w=["S_in%d" % (sn % NSL)])

                for s0 in range(NSL - 1):
                    ld(s0)
                for s in range(NSS):
                    i2 = s % 2
                    i3 = s % NSL
                    sk_ = "S_in%d" % i3
                    if s == DEFER_GA_AT and l in deferred_w:
                        load_w(l, 1, aft=deferred_w.pop(l) + (len(S.ops) - 1,), only="late")
                    if s + NSL - 1 < NSS:
                        ld(s + NSL - 1)
                    ts("pool", kdm[i2][0:n, :], kd_tok_m[0:n, :], cf[0:n, CF_ROWMASK + s:CF_ROWMASK + s + 1],
                       r=["kd_tok_m", "cf"], w=["kdm%d" % i2], s2=1.0, op1=ALU.mult)
                    cp("act", S_bfs[i2][:, :], S_in[i3][:, :], r=[sk_], w=["S_bfs%d" % i2])
                    tt("pool", V(qm[i2], 4, NM), V(qT_m, 4, NM),
                       colmask[:, s * NM:(s + 1) * NM].unsqueeze(1).to_broadcast([128, 4, NM]), ALU.mult,
                       r=["qT_m", "colmask"], w=["qm%d" % i2])
                    for h in range(4):
                        b_, bk_ = bo[h // 2]
                        off = (h % 2) * 256
                        mm(b_[0:n, off:off + 256], qm[i2][:, h * NM:(h + 1) * NM],
                           S_bfs[i2][:, h * 256:(h + 1) * 256], False, s == NSS - 1,
                           r=["qm%d" % i2, "S_bfs%d" % i2], w=[bk_], sgc=True)
                    for hp in range(2):
                        yield ("need", 1)
                        b_, bk_ = bank()
                        for h in (2 * hp, 2 * hp + 1):
                            off = (h % 2) * 256
                            mm(b_[:, off:off + 256], kdm[i2][0:n, h * 128:(h + 1) * 128],
                               v_tok_m[0:n, h * 256:(h + 1) * 256], True, True, r=["kdm%d" % i2, "v_tok_m"], w=[bk_])
                        for h in (2 * hp, 2 * hp + 1):
                            off = (h % 2) * 256
                            stt(S_in[i3][:, h * 256:(h + 1) * 256], S_in[i3][:, h * 256:(h + 1) * 256],
                                ebcm[:, h * 17 + s:h * 17 + s + 1], b_[:, off:off + 256], ALU.mult, ALU.add,
                                r=[sk_, "ebcm", bk_], w=[sk_])
                        free(bk_)
                    dma(nsg_s[l, s].rearrange("h d v -> d h v"),
                        S_in[i3][:, :].rearrange("p (h v) -> p h v", h=4), r=[sk_], w=[])
                if l in deferred_w:
                    load_w(l, 1, aft=deferred_w.pop(l) + (len(S.ops) - 1,), only="late")
                obanks = [(bo[h // 2][0][0:n, (h % 2) * 256:(h % 2) * 256 + 256], bo[h // 2][1]) for h in range(4)]
                obk = [bo[0][1], bo[1][1]]
            else:
                So, Sn = Sst2[(t - 1) % 2], Sst2[t % 2]
                Sok, Snk = "Sst%d" % ((t - 1) % 2), "Sst%d" % (t % 2)
                cp("pool", S_bf[:, :], So[:, :], r=[Sok], w=["S_bf"])
                yield ("need", 2)
                bo = [bank(), bank()]
                for h in range(4):
                    b_, bk_ = bo[h // 2]
                    off = (h % 2) * 256
                    mm(b_[0:n, off:off + 256], ATm[0:n, h * n:(h + 1) * n], v_tok[0:n, h * 256:(h + 1) * 256],
                       True, False, r=["ATm", "v_tok"], w=[bk_])
                    mm(b_[0:n, off:off + 256], qT[:, h * n:(h + 1) * n], S_bf[:, h * 256:(h + 1) * 256],
                       False, True, r=["qT", "S_bf"], w=[bk_])
                yield ("need", 2)
                bp = [bank(), bank()]
                for h in range(4):
                    b_, bk_ = bp[h // 2]
                    off = (h % 2) * 256
                    mm(b_[:, off:off + 256], kd_tok[0:n, h * 128:(h + 1) * 128], v_tok[0:n, h * 256:(h + 1) * 256],
                       True, True, r=["kd_tok", "v_tok"], w=[bk_])
                for h in range(4):
                    b_, bk_ = bp[h // 2]
                    off = (h % 2) * 256
                    stt(Sn[:, h * 256:(h + 1) * 256], So[:, h * 256:(h + 1) * 256], ebc[:, h:h + 1],
                        b_[:, off:off + 256], ALU.mult, ALU.add, r=[Sok, "ebc", bk_], w=[Snk])
                free(bp[0][1])
                free(bp[1][1])
                if t == NT - 1:
                    dma(nsg_p[l].rearrange("h d v -> d h v"), Sn[:, :].rearrange("p (h v) -> p h v", h=4),
                        r=[Snk], w=[])
                obanks = [(bo[h // 2][0][0:n, (h % 2) * 256:(h % 2) * 256 + 256], bo[h // 2][1]) for h in range(4)]
                obk = [bo[0][1], bo[1][1]]
            if t != 0:
                yield "M"

            for h in range(4):
                oap, ok_ = obanks[h]
                act(on[0:n, h * 256:(h + 1) * 256], oap, AF.Square, r=[ok_], w=["on", "ss4"],
                    accum_out=ss4[0:n, h:h + 1])
            act(ss4[0:n, 4:8], ss4[0:n, 0:4], AF.Ln, r=["ss4"], w=["ss4"], bias=EPS, scale=1.0 / 256.0)
            act(ss4[0:n, 8:12], ss4[0:n, 4:8], AF.Exp, r=["ss4"], w=["ss4"], scale=-0.5)
            yield
            for h in range(4):
                oap, ok_ = obanks[h]
                act(on[0:n, h * 256:(h + 1) * 256], oap, AF.Copy, r=[ok_, "ss4", "on"], w=["on"],
                    scale=ss4[0:n, 8 + h:9 + h])
            for k_ in obk:
                free(k_)
            gcols = cols[:, cb + CL_GAIN:cb + CL_GAIN + 8]
            for i in range(2):
                yield ("need", 1)
                b_, bk_ = bank()
                for c4 in range(4):
                    proj(b_[:, c4 * n:(c4 + 1) * n], C_GA + (4 * i + c4) * 128, 128, "W1ga_%d" % l, bk_)
                egi = eg[:, i * 4 * n:(i + 1) * 4 * n]
                act(egi, b_[:, 0:4 * n], AF.Exp, r=[bk_], w=["eg%d" % i], scale=-1.0)
                sigmoid_inplace(egi, "eg%d" % i)
                tt("pool", V(egi, 4, n), V(egi, 4, n),
                   gcols[:, 4 * i:4 * i + 4].unsqueeze(2).to_broadcast([128, 4, n]), ALU.mult,
                   r=["eg%d" % i, "cols"], w=["eg%d" % i])
                tt("dve", egi, b_[:, 0:4 * n], egi, ALU.mult, r=[bk_, "eg%d" % i], w=["eg%d" % i])
                free(bk_)
            yield ("need", 1)
            bT, bTk = bank()
            bTb = bfview(bT)
            for c in range(8):
                tr(bTb[:, c * n:(c + 1) * n], on[0:n, c * 128:(c + 1) * 128], ident_bf[0:n, 0:n],
                   r=["on", "ident_bf"], w=[bTk])
            ok2 = "oTg%d" % sl
            tt("dve", V(oTg[sl], 8, 128)[:, :, 0:n], V(bTb, 8, n), V(eg, 8, n), ALU.mult, r=[bTk, "eg0", "eg1"],
               w=[ok2])
            free(bTk)
            dma(OT[t], oTg[sl][:, :], r=[ok2], w=["OT_%d" % t])

            yield ("need", 1)
            bu, buk = bank()
            for g in range(4):
                proj(bu[:, g * n:(g + 1) * n], C_U + g * 128, 128, "W1ugb_%d" % l, buk)
            z3 = V(zT, 4, n)
            WIN = (2, 4, 8, 16)
            if t == 0:
                for rt in range(2):
                    dma(sp_tok[0:120, rt * 512:(rt + 1) * 512],
                        spool[l].rearrange("s p c -> (s p) c")[rt * 120:(rt + 1) * 120, :], r=[], w=["sp_tok%d" % rt])
                dma(nsp_s[l, :, 0:11, :], spool[l, :, 4:15, :], r=[], w=[])
                um = uextm[:, :].rearrange("p (g s e) -> p g s e", g=4, s=NSS)
                am = sAm[:, :].rearrange("p (g s e) -> p g s e", g=4, s=NSS)
                bm_ = sBm[:, :].rearrange("p (g s e) -> p g s e", g=4, s=NSS)
                for rt in range(2):
                    yield ("need", 1)
                    bs, bsk = bank()
                    for g in range(4):
                        tr(bs[:, g * 120:(g + 1) * 120], sp_tok[0:120, rt * 512 + g * 128:rt * 512 + (g + 1) * 128],
                           ident_f[0:120, 0:120], r=["sp_tok%d" % rt, "cf"], w=[bsk])
                    cp("act", um[:, :, rt * 8:(rt + 1) * 8, 0:15],
                       bs[:, 0:480].rearrange("p (g s e) -> p g s e", g=4, s=8), r=[bsk], w=["S_in0", "S_in1"])
                    free(bsk)
                u3 = V(bu, 4, NM)
                cp("act", um[:, :, :, 15:19], u3[:, :, 0:64].rearrange("p g (s t) -> p g s t", t=TS), r=[buk],
                   w=["S_in0", "S_in1"])
                ume = V(uextmeta, 4, 31)
                cp("act", ume[:, :, 15:31], u3[:, :, 64:NM], r=[buk], w=["uextmeta"])
                cp("act", u_s_perm[:, :].rearrange("p (g t s) -> p g s t", g=4, t=TS),
                   u3[:, :, 0:64].rearrange("p g (s t) -> p g s t", t=TS), r=[buk], w=["u_s_perm"])
                free(buk)
                yield
                tt("pool", am[:, :, :, 1:19], um[:, :, :, 1:19], um[:, :, :, 0:18], ALU.add, r=["S_in0", "S_in1"], w=["S_in1", "S_in2"])
                tt("pool", bm_[:, 1:4, :, 3:19], am[:, 1:4, :, 3:19], am[:, 1:4, :, 1:17], ALU.add, r=["S_in1", "S_in2"],
                   w=["S_in2", "S_in3"])
                tt("pool", am[:, 2:4, :, 7:19], bm_[:, 2:4, :, 7:19], bm_[:, 2:4, :, 3:15], ALU.add, r=["S_in2", "S_in3"],
                   w=["S_in1", "S_in2"])
                tt("pool", bm_[:, 3:4, :, 15:19], am[:, 3:4, :, 15:19], am[:, 3:4, :, 7:11], ALU.add, r=["S_in1", "S_in2"],
                   w=["S_in2", "S_in3"])
                for g in range(4):
                    src = am if g % 2 == 0 else bm_
                    stt(z3[:, g, 0:64].rearrange("p (s t) -> p s t", t=TS), src[:, g, :, 15:19], 1.0 / WIN[g],
                        um[:, g, :, 15:19], ALU.mult, ALU.subtract,
                        r=["S_in0", "S_in1", "S_in2", "S_in3"], w=["zT"])
                yield
                a3 = V(sA, 4, 143)
                b3 = V(sB, 4, 143)
                tt("pool", a3[:, :, 1:31], ume[:, :, 1:31], ume[:, :, 0:30], ALU.add, r=["uextmeta"], w=["sA"])
                tt("pool", b3[:, 1:4, 3:31], a3[:, 1:4, 3:31], a3[:, 1:4, 1:29], ALU.add, r=["sA"], w=["sB"])
                tt("pool", a3[:, 2:4, 7:31], b3[:, 2:4, 7:31], b3[:, 2:4, 3:27], ALU.add, r=["sB"], w=["sA"])
                tt("pool", b3[:, 3:4, 15:31], a3[:, 3:4, 15:31], a3[:, 3:4, 7:23], ALU.add, r=["sA"], w=["sB"])
                icn = cf[:, CF_INVCNT:CF_INVCNT + 64].rearrange("p (g e) -> p g e", g=4)
                for g in range(4):
                    src = a3 if g % 2 == 0 else b3
                    tt("pool", src[:, g, 15:31], src[:, g, 15:31], icn[:, g, :], ALU.mult, r=["sA", "sB", "cf"],
                       w=["sA", "sB"])
                    tt("dve", z3[:, g, 64:NM], src[:, g, 15:31], ume[:, g, 15:31], ALU.subtract,
                       r=["sA", "sB", "uextmeta"], w=["zT"])
                cp("pool", V(uext, 4, 143)[:, :, 0:15], ume[:, :, 16:31], r=["uextmeta"], w=["uext"])
                yield ("need", 1)
                bx, bxk = bank()
                for g in range(4):
                    tr(bx[0:64, g * 128:(g + 1) * 128], u_s_perm[:, g * 64:(g + 1) * 64], ident_f[:, :],
                       r=["u_s_perm", "cf"], w=[bxk])
                cp("act", utok[0:64, :], bx[0:64, :], r=[bxk], w=["utok"])
                free(bxk)
                for tt_ in range(TS):
                    dma(nsp_s[l, :, 11 + tt_, :], utok[tt_ * NSS:(tt_ + 1) * NSS, :], r=["utok"], w=[])
            else:
                ue = V(uext, 4, 143)
                a3 = V(sA, 4, 143)
                b3 = V(sB, 4, 143)
                cp("act", ue[:, :, 15:143], V(bu, 4, 128), r=[buk], w=["uext"])
                free(buk)
                yield
                tt("pool", a3[:, :, 1:143], ue[:, :, 1:143], ue[:, :, 0:142], ALU.add, r=["uext"], w=["sA"])
                tt("pool", b3[:, 1:4, 3:143], a3[:, 1:4, 3:143], a3[:, 1:4, 1:141], ALU.add, r=["sA"], w=["sB"])
                tt("pool", a3[:, 2:4, 7:143], b3[:, 2:4, 7:143], b3[:, 2:4, 3:139], ALU.add, r=["sB"], w=["sA"])
                tt("pool", b3[:, 3:4, 15:143], a3[:, 3:4, 15:143], a3[:, 3:4, 7:135], ALU.add, r=["sA"], w=["sB"])
                for g in range(4):
                    src = a3 if g % 2 == 0 else b3
                    stt(z3[:, g, :], src[:, g, 15:143], 1.0 / WIN[g], ue[:, g, 15:143], ALU.mult, ALU.subtract,
                        r=["sA", "sB", "uext"], w=["zT"])
                if t == NT - 1:
                    yield ("need", 1)
                    bx, bxk = bank()
                    for g in range(4):
                        tr(bx[0:15, g * 128:(g + 1) * 128], ue[:, g, 128:143], ident_f[:, :], r=["uext", "cf"],
                           w=[bxk])
                    cp("act", utok[0:15, :], bx[0:15, :], r=[bxk], w=["utok"])
                    free(bxk)
                    dma(nsp_p[l], utok[0:15, :], r=["utok"], w=[])
                else:
                    cp("pool", ue[:, :, 0:15], ue[:, :, 128:143], r=["uext"], w=["uext"])
            yield ("need", 1)
            bgb, bgbk = bank()
            for g in range(4):
                proj(bgb[:, g * n:(g + 1) * n], C_GB + g * 128, 128, "W1ugb_%d" % l, bgbk)
            act(eb[:, 0:4 * n], bgb[:, 0:4 * n], AF.Exp, r=[bgbk], w=["eb"], scale=-1.0)
            sigmoid_inplace(eb[:, 0:4 * n], "eb")
            tt("pool", V(eb, 4, n), V(eb, 4, n),
               cols[:, cb + CL_PS:cb + CL_PS + 4].unsqueeze(2).to_broadcast([128, 4, n]), ALU.mult,
               r=["eb", "cols"], w=["eb"])
            tt("dve", eb[:, 0:4 * n], bgb[:, 0:4 * n], eb[:, 0:4 * n], ALU.mult, r=[bgbk, "eb"], w=["eb"])
            free(bgbk)
            yield ("need", 1)
            by, byk = bank()
            pw = WB[:, POOLW_OFF:POOLW_OFF + 512].rearrange("p (g d) -> p g d", g=4)
            for g in range(4):
                mm(by[:, g * n:(g + 1) * n], pw[:, g, :], zT[:, g * n:(g + 1) * n], True, True,
                   r=["W1pw_%d" % l, "zT"], w=[byk])
            yk = "ypTg%d" % sl
            tt("dve", V(ypTg[sl], 4, 128)[:, :, 0:n], V(by, 4, n), V(eb, 4, n), ALU.mult, r=[byk, "eb"], w=[yk])
            free(byk)
            dma(YP[t], ypTg[sl][:, :], r=[yk], w=["YP_%d" % t])

        def p2_loads(l, t):
            sl = t % 2
            dma(xnT[sl][:, :], XN[t], r=["XN_%d" % t], w=["xnT%d" % sl])
            dma(oTg[sl][:, :], OT[t], r=["OT_%d" % t], w=["oTg%d" % sl])
            dma(ypTg[sl][:, :], YP[t], r=["YP_%d" % t], w=["ypTg%d" % sl])
            load_h(l, t)

        def phase2_tile(l, t):
            n = ntok(t)
            sl = t % 2
            hs = t % 3
            cb = l * 40
            hk = hkeys(l, t)
            xk, ok2, yk = "xnT%d" % sl, "oTg%d" % sl, "ypTg%d" % sl
            xn3 = V(xnT[sl], 8, 128)
            o3 = V(oTg[sl], 8, 128)
            y3 = V(ypTg[sl], 4, 128)
            Wmg = WB[:, 0:16384].rearrange("p (kc n) -> p kc n", n=2048)
            Wa = WB[:, 16384:24576].rearrange("p (kc n) -> p kc n", n=1024)
            Wb = WB[:, 24576:28672].rearrange("p (kc n) -> p kc n", n=1024)
            Wo = WB[:, 28672:36864].rearrange("p (kc n) -> p kc n", n=1024)
            if t + 1 < NT:
                p2_loads(l, t + 1)
            m3 = V(mergedT[sl], 8, 128)
            mk_ = "mergedT%d" % sl
            for c in range(8):
                yield ("need", 1)
                b_, bk_ = bank()
                es = c % 2
                for kc in range(8):
                    mm(b_[:, 0:n], Wmg[:, kc, c * 128:(c + 1) * 128], xn3[:, kc, 0:n], kc == 0, kc == 7,
                       r=["W2mga_%d" % l, xk], w=[bk_])
                for kc in range(8):
                    mm(b_[:, n:2 * n], Wmg[:, kc, 1024 + c * 128:1024 + (c + 1) * 128], xn3[:, kc, 0:n], kc == 0,
                       kc == 7, r=["W2mgb_%d" % l, xk], w=[bk_])
                for kc in range(8):
                    mm(b_[:, 2 * n:3 * n], Wa[:, kc, c * 128:(c + 1) * 128], o3[:, kc, 0:n], kc == 0, kc == 7,
                       r=["W2a_%d" % l, ok2], w=[bk_])
                for kc in range(4):
                    mm(b_[:, 3 * n:4 * n], Wb[:, kc, c * 128:(c + 1) * 128], y3[:, kc, 0:n], kc == 0, kc == 3,
                       r=["W2b_%d" % l, yk], w=[bk_])
                ek = "e2_%d" % es
                act(e2[es][:, 0:n], b_[:, 0:n], AF.Exp, r=[bk_, "negc"], w=[ek],
                    bias=negc[:, cb + CL_BM + c:cb + CL_BM + c + 1], scale=-1.0)
                act(e2[es][:, n:2 * n], b_[:, n:2 * n], AF.Exp, r=[bk_, "negc"], w=[ek],
                    bias=negc[:, cb + CL_BM + 8 + c:cb + CL_BM + 8 + c + 1], scale=-1.0)
                sigmoid_inplace(e2[es][:, 0:2 * n], ek)
                tk = "tbuf%d" % es
                tt("dve", tbuf[es][:, 0:2 * n], b_[:, 2 * n:4 * n], e2[es][:, 0:2 * n], ALU.mult, r=[bk_, ek], w=[tk])
                free(bk_)
                tt("pool", m3[:, c, 0:n], tbuf[es][:, 0:n], tbuf[es][:, n:2 * n], ALU.add, r=[tk], w=[mk_])
            yield "M"
            yield
            for b in range(2):
                yield ("need", 1)
                b_, bk_ = bank()
                for kc in range(8):
                    mm(b_[0:n, 0:512], m3[:, kc, 0:n], Wo[:, kc, b * 512:(b + 1) * 512], kc == 0, kc == 7,
                       r=["W2o_%d" % l, mk_], w=[bk_])
                tt("dve", hb[hs][0:n, b * 512:(b + 1) * 512], b_[0:n, 0:512], hb[hs][0:n, b * 512:(b + 1) * 512],
                   ALU.add, r=[bk_] + hk, w=hk)
                free(bk_)
            yield
            if l == 0:
                dma(H1[t * 128:t * 128 + n, :], hb[hs][0:n, :], r=hk, w=["H1_%d" % t])
            else:
                rms_stats(hb[hs][0:n, :], n, hk, t % 2)
                so_ = 4 * (t % 2)
                stt(hb[hs][0:n, :], hb[hs][0:n, :], ss[0:n, so_ + 2:so_ + 3], fgf[0:n, :], ALU.mult, ALU.mult,
                    r=hk + ["ss%d" % (t % 2), "eg0", "eg1"], w=hk)
                if t == 0:
                    dma(y_s[:, :], hb[hs][0:64, :], r=hk, w=[])
                else:
                    dma(y_p[(t - 1) * 128:t * 128, :], hb[hs][0:128, :], r=hk, w=[])

        def zipper(gens, tag=None):
            state = {}

            def step(g, must):
                pend = state.get(id(g))
                if pend is not None:
                    if nfree() < pend:
                        if must:
                            raise RuntimeError("PSUM banks exhausted")
                        return "blocked"
                    state[id(g)] = None
                while True:
                    try:
                        S.ctx = gctx.get(id(g))
                        v = next(g)
                    except StopIteration:
                        return "done"
                    if isinstance(v, tuple) and v[0] == "need":
                        if nfree() >= v[1]:
                            continue
                        state[id(g)] = v[1]
                        if must:
                            raise RuntimeError("PSUM banks exhausted (need %d, free %d)" % (v[1], nfree()))
                        return "blocked"
                    return "M" if v == "M" else "ok"

            cur = None
            gctx = {id(g): (tag, gi) for gi, g in enumerate(gens)}
            for g in gens:
                done_cur = cur is None
                while True:
                    if not done_cur:
                        done_cur = step(cur, True) == "done"
                    r = step(g, done_cur)
                    if r == "M":
                        break
                while not done_cur:
                    done_cur = step(cur, True) == "done"
                cur = g
            while step(cur, True) != "done":
                pass

        for l in range(NL):
            load_w_p1(l)
            del phase_pe_ops[:]
            if l == 0:
                dma(colmask[:, :], cm_d[:, :], r=[], w=["colmask"], eng="pool", max_dma_last_dim=4096)
            load_h(l, 0)
            load_h(l, 1)
            gens1 = [phase1_tile(l, t) for t in range(NT)]
            if l == 0:
                st_pieces = stage_pieces()

                def with_staging(g_, k_):
                    emit_stage(st_pieces, k_)
                    yield from g_

                for t_ in range(2, NT):
                    gens1[t_] = with_staging(gens1[t_], STAGE_PER_TILE + (2 if t_ < 12 else 0))
            zipper(gens1, tag="L%dP1" % l)
            for i in range(len(S.ops) - 1, -1, -1):
                if S.ops[i]["eng"] == "pe":
                    last_pe_of_phase[0] = i
                    break
            load_w_p2(l)
            del phase_pe_ops[:]
            if l == NL - 1:
                dma(fgf[:, :], fg_d[:, :], r=[], w=["eg0", "eg1"])
            p2_loads(l, 0)
            gens2 = [phase2_tile(l, t) for t in range(NT)]
            if l == 0:
                for t_ in range(1, NT):
                    gens2[t_] = with_staging(gens2[t_], STAGE_PER_TILE)
            zipper(gens2, tag="L%dP2" % l)
            if l == 0:
                emit_stage(st_pieces, 1000)
            for i in range(len(S.ops) - 1, -1, -1):
                if S.ops[i]["eng"] == "pe":
                    last_pe_of_phase[0] = i
                    break
        S.emit(nc)
    return nc


def _cols_layout(norm_g, b_alpha, gla_gain, pool_scale, b_merge):
    cols = np.zeros((128, NCOLS), np.float32)
    for l in range(NL):
        cb = l * 40
        cols[:, cb + CL_G:cb + CL_G + 8] = norm_g[l].reshape(8, 128).T
        cols[:, cb + CL_BA:cb + CL_BA + 4] = b_alpha[l].reshape(4, 128).T
        cols[:, cb + CL_GAIN:cb + CL_GAIN + 8] = gla_gain[l].reshape(8, 128).T
        cols[:, cb + CL_PS:cb + CL_PS + 4] = pool_scale[l].reshape(4, 128).T
        cols[:, cb + CL_BM:cb + CL_BM + 16] = b_merge[l].reshape(16, 128).T
    return cols


def kernel(x_prompt, x_sample, state_gla, state_pool, meta_tokens, norm_g, w_in, w_alpha, b_alpha, gla_gain,
           w_a, pool_w, pool_scale, w_b, b_merge, w_out, final_norm_g):
    f = lambda a: np.ascontiguousarray(np.asarray(a, dtype=np.float32))
    x_prompt, x_sample, state_gla, state_pool = f(x_prompt), f(x_sample), f(state_gla), f(state_pool)
    meta_tokens, w_in, w_alpha, w_a, pool_w, w_b, w_out = map(f, (meta_tokens, w_in, w_alpha, w_a, pool_w, w_b, w_out))
    cols = _cols_layout(f(norm_g), f(b_alpha), f(gla_gain), f(pool_scale), f(b_merge))
    fgfull = np.ascontiguousarray(np.broadcast_to(f(final_norm_g)[None, :], (128, D)))
    cf = make_consts()
    cm = make_colmask()
    nc = build_nc()
    in_maps = []
    for c in range(NCORE):
        in_maps.append({
            "xp": x_prompt[c],
            "xs": np.ascontiguousarray(x_sample[c * NSS:(c + 1) * NSS].reshape(NSS * TS, D)),
            "meta": meta_tokens,
            "sgla": np.ascontiguousarray(state_gla[:, c * NSS:(c + 1) * NSS]),
            "spool": np.ascontiguousarray(state_pool[:, c * NSS:(c + 1) * NSS]),
            "w_in": w_in, "w_alpha": w_alpha, "w_a": w_a, "pool_w": pool_w, "w_b": w_b, "w_out": w_out,
            "cols": cols, "fgfull": fgfull, "cf": cf, "cm": cm,
        })
    res = run_bass_kernel_spmd(nc, in_maps, core_ids=list(range(NCORE)))
    R = res.results
    y_prompt = np.stack([R[c]["y_p"] for c in range(NCORE)], axis=0)
    y_sample = np.concatenate([R[c]["y_s"].reshape(NSS, TS, D) for c in range(NCORE)], axis=0)
    nsg_p = np.stack([R[c]["nsg_p"] for c in range(NCORE)], axis=1)
    nsp_p = np.stack([R[c]["nsp_p"] for c in range(NCORE)], axis=1)
    nsg_s = np.concatenate([R[c]["nsg_s"] for c in range(NCORE)], axis=1)
    nsp_s = np.concatenate([R[c]["nsp_s"] for c in range(NCORE)], axis=1)
    return (y_prompt.astype(np.float32), y_sample.astype(np.float32), nsg_p.astype(np.float32),
            nsp_p.astype(np.float32), nsg_s.astype(np.float32), nsp_s.astype(np.float32))
```

```python
import contextlib
import math

import numpy as np
import concourse.bass as bass
import concourse.mybir as mybir
from concourse.bass_utils import run_bass_kernel_spmd

F32 = mybir.dt.float32
BF16 = mybir.dt.bfloat16
AF = mybir.ActivationFunctionType
ALU = mybir.AluOpType

D = 1024
NL = 2
SEQ = 2048
NCORE = 8
NSS = 16
TS = 4
NMETA = 16
NM = NSS * TS + NMETA
NPT = SEQ // 128
NT = NPT + 1
EPS = 1e-6
IN_COLS = 6160
C_Q, C_K, C_V, C_GA, C_U, C_GB, C_AL, C_MG = 0, 512, 1024, 2048, 3072, 3584, 4096, 4112
P1COLS = 4112
WB_ELEMS = 36864
POOLW_OFF = 8 * P1COLS

CF_IDENT = 0
CF_CAUS = 128
CF_MASKM = 256
CF_RMP = 336
CF_RMM = 848
CF_INVCNT = 1168
CF_ROWMASK = 1232
CF_COLMASK = 1249
NCF = CF_COLMASK
CL_G, CL_BA, CL_GAIN, CL_PS, CL_BM = 0, 8, 12, 20, 24
NCOLS = 80

N_SP_SEMS = 40
STAGED_Q = "pool"
DEFER_GA_AT = 13
STAGE_PER_TILE = 4
N_POOL_SEMS = 8


class Sched:
    ENGS = ("pe", "act", "dve", "pool", "sp")

    debug_tags = False

    def __init__(self):
        self.ops = []
        self.last_w = {}
        self.readers = {}

    def add(self, eng, fn, r=(), w=(), dma=False, after=(), cost=300.0, lat=150.0, attach=False):
        i = len(self.ops)
        deps = {}
        dk = {}
        for k in r:
            d = self.last_w.get(k)
            if d is not None:
                deps[d] = "RAW"
        for k in w:
            d = self.last_w.get(k)
            if d is not None and d not in deps:
                deps[d] = "WAW"
                dk[d] = k
            for d in self.readers.get(k, ()):
                if d not in deps:
                    deps[d] = "WAR"
                    dk[d] = k
        for d in after:
            if d is not None:
                deps[d] = "RAW"
        for k in r:
            self.readers.setdefault(k, []).append(i)
        for k in w:
            self.last_w[k] = i
            self.readers[k] = []
        deps.pop(i, None)
        tag = None
        if self.debug_tags:
            import sys as _sys
            f = _sys._getframe(1)
            while f is not None and f.f_code.co_name in ("mm", "tr", "act", "tt", "ts", "stt", "cp", "memset", "dma",
                                                         "proj", "sigmoid_inplace", "rms_stats", "load_h", "p2_loads"):
                f = f.f_back
            tag = f.f_lineno if f is not None else None
        self.ops.append(dict(eng=eng, fn=fn, deps=deps, dma=dma, token=None, cost=float(cost), lat=float(lat),
                             tag=tag, dk=dk if self.debug_tags else None, ctx=getattr(self, "ctx", None),
                             attach=attach))
        return i

    def list_schedule(self):
        import heapq
        ops = self.ops
        n = len(ops)
        succ = [[] for _ in range(n)]
        npred = [0] * n
        for i, op in enumerate(ops):
            for d, kind in op["deps"].items():
                od = ops[d]
                soft = (not od["dma"]) and (not op["dma"]) and od["eng"] == op["eng"] and \
                    (op["eng"] == "pe" or kind != "RAW")
                succ[d].append((i, soft))
                npred[i] += 1
        ready_t = [0.0] * n
        done_t = [0.0] * n
        start_t = [0.0] * n
        engs = list(self.ENGS)
        pend = {e: [] for e in engs}
        avail = {e: [] for e in engs}
        free_t = {e: 0.0 for e in engs}
        dma_free = [0.0]
        for i in range(n):
            if npred[i] == 0:
                heapq.heappush(pend[ops[i]["eng"]], (0.0, i))
        nsched = 0
        order = []
        last_on = {}
        while nsched < n:
            best = None
            for e in engs:
                T = free_t[e]
                while pend[e] and pend[e][0][0] <= T:
                    heapq.heappush(avail[e], heapq.heappop(pend[e])[1])
                if avail[e]:
                    st_, idx = T, avail[e][0]
                elif pend[e]:
                    st_, idx = pend[e][0]
                else:
                    continue
                if best is None or (st_, idx) < (best[0], best[1]):
                    best = (st_, idx, e)
            st_, idx, e = best
            if avail[e] and avail[e][0] == idx:
                heapq.heappop(avail[e])
            else:
                heapq.heappop(pend[e])
            op = ops[idx]
            start_t[idx] = st_
            if self.debug_tags:
                cd = None
                for d_ in op["deps"]:
                    if cd is None or done_t[d_] > done_t[cd]:
                        cd = d_
                op["st"] = st_
                op["crit"] = cd if (cd is not None and done_t[cd] >= st_ - 1e-6) else ("eng", last_on.get(e))
                last_on[e] = idx
            if op["dma"]:
                issue = 60.0 if e == "sp" else 900.0
                free_t[e] = st_ + issue
                xfer_start = max(st_ + issue, dma_free[0])
                dma_free[0] = xfer_start + op["cost"]
                done_t[idx] = xfer_start + op["cost"] + 1800.0
            else:
                free_t[e] = st_ + op["cost"]
                done_t[idx] = st_ + op["cost"] + op["lat"]
            order.append(idx)
            nsched += 1
            for j, soft in succ[idx]:
                rt_ = free_t[e] if soft else done_t[idx]
                if ready_t[j] < rt_:
                    ready_t[j] = rt_
                npred[j] -= 1
                if npred[j] == 0:
                    heapq.heappush(pend[ops[j]["eng"]], (ready_t[j], j))
        self.sim_end = max(done_t) if n else 0.0
        if self.debug_tags:
            self.sim_done = done_t
            self.sim_ops_old = list(ops)
        remap = {old: new for new, old in enumerate(order)}
        new_ops = []
        for old in order:
            op = ops[old]
            op["deps"] = {remap[d]: k for d, k in op["deps"].items()}
            new_ops.append(op)
        self.ops = new_ops

    def finalize(self):
        self.list_schedule()
        ops = self.ops
        for op in ops:
            kept = []
            for d, kind in op["deps"].items():
                od = ops[d]
                if (not od["dma"]) and (not op["dma"]) and od["eng"] == op["eng"]:
                    if op["eng"] == "pe" or kind != "RAW":
                        continue
                kept.append(d)
            best = {}
            kept2 = []
            for d in kept:
                od = ops[d]
                if od["dma"]:
                    kept2.append(d)
                else:
                    if od["eng"] not in best or best[od["eng"]] < d:
                        best[od["eng"]] = d
            op["kept"] = kept2 + list(best.values())
        slot_last = {}
        gen = {}
        cnt_q = {"sp": 0, "pool": 0}
        nsl = {"sp": N_SP_SEMS, "pool": N_POOL_SEMS}
        for i, op in enumerate(ops):
            if op["dma"]:
                q = op["eng"]
                k = (q, cnt_q[q] % nsl[q])
                cnt_q[q] += 1
                gen[k] = gen.get(k, 0) + 1
                op["token"] = (("dma",) + k, 16 * gen[k])
                if k in slot_last:
                    op["kept"].append(slot_last[k])
                slot_last[k] = i
        self.dma_final = {k: 16 * g for k, g in gen.items()}
        needed = set()
        for op in ops:
            needed.update(op["kept"])
        cnt = {e: 0 for e in self.ENGS}
        for i, op in enumerate(ops):
            if not op["dma"] and i in needed:
                cnt[op["eng"]] += 1
                op["token"] = (("eng", op["eng"]), cnt[op["eng"]])

    def emit(self, nc):
        self.finalize()
        ops = self.ops
        with contextlib.ExitStack() as st:
            sems = {}
            for e in self.ENGS:
                sems[("eng", e)] = st.enter_context(nc.semaphore("s_" + e))
            for k in range(N_SP_SEMS):
                sems[("dma", "sp", k)] = st.enter_context(nc.semaphore("dsp_%d" % k))
            for k in range(N_POOL_SEMS):
                sems[("dma", "pool", k)] = st.enter_context(nc.semaphore("dpl_%d" % k))
            block = st.enter_context(nc.Block())

            def run_engine(ename, eng):
                seen = {}
                for op in ops:
                    if op["eng"] != ename:
                        continue
                    need = {}
                    for d in op["kept"]:
                        sk, val = ops[d]["token"]
                        if need.get(sk, 0) < val:
                            need[sk] = val
                    items = [(sk, val) for sk, val in need.items() if seen.get(sk, 0) < val]
                    attach = items.pop() if (items and op.get("attach")) else None
                    for sk, val in items:
                        eng.wait_ge(sems[sk], val)
                        seen[sk] = val
                    inst = op["fn"](eng)
                    if attach is not None:
                        inst._wait_ge(sems[attach[0]], attach[1])
                        seen[attach[0]] = attach[1]
                    if op["token"] is not None:
                        sk, val = op["token"]
                        inst.then_inc(sems[sk], 16 if op["dma"] else 1)
                if ename == "sp":
                    for k, v in self.dma_final.items():
                        sk = ("dma",) + k
                        if seen.get(sk, 0) < v:
                            eng.wait_ge(sems[sk], v)

            @block.tensor
            def _(e):
                run_engine("pe", e)

            @block.scalar
            def _(e):
                run_engine("act", e)

            @block.vector
            def _(e):
                run_engine("dve", e)

            @block.gpsimd
            def _(e):
                run_engine("pool", e)

            @block.sync
            def _(e):
                run_engine("sp", e)


def make_consts():
    cf = np.zeros((128, NCF), np.float32)
    cf[:, CF_IDENT:CF_IDENT + 128] = np.eye(128, dtype=np.float32)
    j = np.arange(128)[:, None]
    i = np.arange(128)[None, :]
    cf[:, CF_CAUS:CF_CAUS + 128] = (j <= i).astype(np.float32)
    mm = np.zeros((128, NM), np.float32)
    for a in range(NM):
        for b in range(NM):
            if a < 64 and b < 64:
                ok = (a // TS == b // TS) and a <= b
            elif a >= 64 and b >= 64:
                ok = a <= b
            else:
                ok = False
            mm[a, b] = 1.0 if ok else 0.0
    cf[:, CF_MASKM:CF_MASKM + NM] = mm
    rmp = np.ones(512, np.float32)
    rmp[0::128] = 0.0
    cf[:, CF_RMP:CF_RMP + 512] = rmp[None, :]
    rmm = np.ones((4, NM), np.float32)
    for t in range(NM):
        if (t < 64 and t % TS == 0) or t == 64:
            rmm[:, t] = 0.0
    cf[:, CF_RMM:CF_RMM + 4 * NM] = rmm.reshape(1, -1)
    inv = np.zeros((4, NMETA), np.float32)
    for g, w in enumerate((2, 4, 8, 16)):
        for p in range(NMETA):
            inv[g, p] = 1.0 / min(w, p + 1)
    cf[:, CF_INVCNT:CF_INVCNT + 64] = inv.reshape(1, -1)
    rm = np.zeros((128, NSS + 1), np.float32)
    for a in range(NM):
        if a < 64:
            rm[a, a // TS] = 1.0
        else:
            rm[a, NSS] = 1.0
    cf[:, CF_ROWMASK:CF_ROWMASK + NSS + 1] = rm
    return cf


def make_colmask():
    cm = np.zeros((NSS, NM), np.float32)
    for s in range(NSS):
        cm[s, TS * s:TS * s + TS] = 1.0
    return np.ascontiguousarray(np.broadcast_to(cm.reshape(1, -1), (128, NSS * NM)))


def build_nc():
    nc = bass.Bass("TRN2", target_bir_lowering=False)

    def din(name, shape, dt=F32):
        return nc.dram_tensor(name, list(shape), dt, kind="ExternalInput").ap()

    def dout(name, shape, dt=F32):
        return nc.dram_tensor(name, list(shape), dt, kind="ExternalOutput").ap()

    def dscr(name, shape, dt):
        return nc.dram_tensor(name, list(shape), dt, kind="Internal").ap()

    xp = din("xp", [SEQ, D])
    xs = din("xs", [NSS * TS, D])
    meta = din("meta", [NMETA, D])
    sgla = din("sgla", [NL, NSS, 4, 128, 256])
    spool = din("spool", [NL, NSS, 15, 512])
    w_in = din("w_in", [NL, D, IN_COLS])
    w_alpha = din("w_alpha", [NL, 16, 512])
    w_a = din("w_a", [NL, D, D])
    pool_w = din("pool_w", [NL, 4, 128, 128])
    w_b = din("w_b", [NL, 512, D])
    w_out = din("w_out", [NL, D, D])
    cols_d = din("cols", [128, NCOLS])
    fg_d = din("fgfull", [128, D])
    cf_d = din("cf", [128, NCF])
    cm_d = din("cm", [128, NSS * NM])

    y_p = dout("y_p", [SEQ, D])
    y_s = dout("y_s", [NSS * TS, D])
    nsg_p = dout("nsg_p", [NL, 4, 128, 256])
    nsp_p = dout("nsp_p", [NL, 15, 512])
    nsg_s = dout("nsg_s", [NL, NSS, 4, 128, 256])
    nsp_s = dout("nsp_s", [NL, NSS, 15, 512])

    H1 = dscr("H1", [NT * 128, D], F32)
    XN = dscr("XN", [NT, 128, 1024], BF16)
    OT = dscr("OT", [NT, 128, 1024], BF16)
    YP = dscr("YP", [NT, 128, 512], BF16)
    WS = {(0, 2): dscr("WS02", [128, WB_ELEMS], BF16), (1, 1): dscr("WS11", [128, WB_ELEMS], BF16),
          (1, 2): dscr("WS12", [128, WB_ELEMS], BF16)}

    S = Sched()
    with contextlib.ExitStack() as st:
        def sb(name, shape, dt):
            return st.enter_context(nc.sbuf_tensor("sb_" + name, list(shape), dt))

        WB = sb("WB", [128, WB_ELEMS], BF16)
        cf = sb("cf", [128, NCF], F32)
        cols = sb("cols", [128, NCOLS], F32)
        negc = sb("negc", [128, NCOLS], F32)
        walpha = sb("walpha", [16, NL * 512], F32)
        ident_bf = sb("ident_bf", [128, 128], BF16)
        mask4 = sb("mask4", [128, 512], BF16)
        maskm4 = sb("maskm4", [128, 4 * NM], BF16)
        colmask = sb("colmask", [128, NSS * NM], BF16)
        hb = [sb("hb%d" % i, [128, D], F32) for i in range(3)]
        xsbs = [sb("xsb%d" % i, [128, D], BF16) for i in range(2)]
        ss = sb("ss", [128, 16], F32)
        xnT = [sb("xnT%d" % i, [128, 1024], BF16) for i in range(3)]
        alow_sb = sb("alow_sb", [16, 128], F32)
        tmpA = sb("tmpA", [128, 512], F32)
        tmpB = sb("tmpB", [128, 512], F32)
        tmpC = sb("tmpC", [128, 512], F32)
        qT = sb("qT", [128, 512], BF16)
        kT = sb("kT", [128, 512], BF16)
        kdT = sb("kdT", [128, 512], BF16)
        ebc = sb("ebc", [128, 4], F32)
        ebcm = sb("ebcm", [128, 4 * 17], F32)
        v_tok = sb("v_tok", [128, D], BF16)
        kd_tok = sb("kd_tok", [128, 512], BF16)
        ATm = sb("ATm", [128, 512], BF16)
        Sst2 = [sb("Sst%d" % i, [128, 1024], F32) for i in range(2)]
        S_bf = sb("S_bf", [128, 1024], BF16)
        ss4 = sb("ss4", [128, 12], F32)
        on = sb("on", [128, D], BF16)
        eg = sb("eg", [128, 1024], F32)
        fgf = eg
        qT_m = sb("qT_m", [128, 4 * NM], BF16)
        kd_tok_m = sb("kd_tok_m", [128, 512], BF16)
        v_tok_m = sb("v_tok_m", [128, D], BF16)
        oTg = [sb("oTg%d" % i, [128, 1024], BF16) for i in range(2)]
        uext = sb("uext", [128, 4 * 143], F32)
        sA = sb("sA", [128, 4 * 143], F32)
        sB = sb("sB", [128, 4 * 143], F32)
        zT = sb("zT", [128, 512], BF16)
        eb = sb("eb", [128, 512], F32)
        ypTg = [sb("ypTg%d" % i, [128, 512], BF16) for i in range(2)]
        e2 = [sb("e2_%d" % i, [128, 256], F32) for i in range(2)]
        tbuf = [sb("tbuf%d" % i, [128, 256], F32) for i in range(2)]
        mergedT = [sb("mergedT%d" % i, [128, 1024], BF16) for i in range(2)]
        S_in_all = sb("S_in_all", [128, 4096], F32)
        S_in = [S_in_all[:, i * 1024:(i + 1) * 1024] for i in range(4)]
        S_bfs = [sb("S_bfs%d" % i, [128, 1024], BF16) for i in range(2)]
        qm = [sb("qm%d" % i, [128, 4 * NM], BF16) for i in range(2)]
        kdm = [sb("kdm%d" % i, [128, 512], BF16) for i in range(2)]
        sp_tok = sb("sp_tok", [128, 1024], F32)
        uextm = S_in_all[:, 0:1216]
        sAm = S_in_all[:, 1216:2432]
        sBm = S_in_all[:, 2432:3648]
        uextmeta = sb("uextmeta", [128, 4 * 31], F32)
        u_s_perm = sb("u_s_perm", [128, 256], F32)
        utok = sb("utok", [128, 512], F32)
        banks = [st.enter_context(nc.psum_tensor("bank%d" % i, [128, 512], F32)) for i in range(8)]

        free_banks = list(range(8))

        def nfree():
            return len(free_banks)

        def bank():
            i = free_banks.pop(0)
            return banks[i], "B%d" % i

        def free(key):
            i = int(key[1:])
            assert i not in free_banks
            free_banks.append(i)

        def fsz(ap):
            n_ = 1
            for d_ in list(ap.shape)[1:]:
                n_ *= int(d_)
            return n_

        phase_pe_ops = []

        def mm(out, lhsT, rhs, start, stop, r, w, sgc=False):
            N = fsz(rhs)
            c = max(N, 48) / 2.4 + 3.0
            if lhsT.dtype == F32:
                c *= 4.0
            i_ = S.add("pe", lambda e: e.matmul(out, lhsT=lhsT, rhs=rhs, start=start, stop=stop,
                                                skip_group_check=sgc), r=r, w=w, cost=c, lat=250.0, attach=True)
            phase_pe_ops.append(i_)
            return i_

        def tr(out, in_, ident, r, w):
            c = 64.0 if in_.dtype != F32 else 200.0
            return S.add("pe", lambda e: e.transpose(out, in_, ident), r=r, w=w, cost=c, lat=250.0, attach=True)

        def act(out, in_, func, r, w, bias=None, scale=None, accum_out=None):
            kw = {}
            if bias is not None:
                kw["bias"] = bias
            if scale is not None:
                kw["scale"] = scale
            if accum_out is not None:
                kw["accum_out"] = accum_out
            c = 200.0 + 0.75 * fsz(in_) + (200.0 if accum_out is not None else 0.0)
            return S.add("act", lambda e: e.activation(out=out, in_=in_, func=func, **kw), r=r, w=w, cost=c, lat=120.0)

        def ecost(eng, n_, f=1.0):
            if eng == "pool":
                return 220.0 + 1.0 * n_ * f
            return 70.0 + 1.05 * n_ * f

        def tt(eng, out, in0, in1, op, r, w):
            return S.add(eng, lambda e: e.tensor_tensor(out=out, in0=in0, in1=in1, op=op), r=r, w=w,
                         cost=ecost(eng, fsz(out)), lat=120.0)

        def ts(eng, out, in0, s1, r, w, s2=None, op0=ALU.mult, op1=None):
            c = ecost(eng, fsz(out))
            if op1 is None:
                return S.add(eng, lambda e: e.tensor_scalar(out=out, in0=in0, scalar1=s1, scalar2=None, op0=op0),
                             r=r, w=w, cost=c, lat=120.0)
            return S.add(eng, lambda e: e.tensor_scalar(out=out, in0=in0, scalar1=s1, scalar2=s2, op0=op0, op1=op1),
                         r=r, w=w, cost=c, lat=120.0)

        def stt(out, in0, scalar, in1, op0, op1, r, w):
            return S.add("dve", lambda e: e.scalar_tensor_tensor(out=out, in0=in0, scalar=scalar, in1=in1, op0=op0,
                                                                 op1=op1), r=r, w=w, cost=ecost("dve", fsz(out)),
                         lat=120.0)

        def cp(eng, out, in_, r, w):
            if eng == "act":
                return act(out, in_, AF.Copy, r, w)
            f = 3.3 if (eng == "pool" and out.dtype != in_.dtype) else 1.0
            return S.add(eng, lambda e: e.tensor_copy(out=out, in_=in_), r=r, w=w, cost=ecost(eng, fsz(out), f),
                         lat=120.0)

        def memset(eng, ap, val, w):
            return S.add(eng, lambda e: e.memset(ap, val), w=w, cost=ecost(eng, fsz(ap)))

        def dma(out, in_, r, w, eng="sp", after=(), **kw):
            nb = 1
            for d_ in list(out.shape):
                nb *= int(d_)
            nb *= 4 if out.dtype == F32 else 2
            if eng == "pool" and in_.dtype != out.dtype:
                nb *= 2
            return S.add(eng, lambda e: e.dma_start(out=out, in_=in_, **kw), r=r, w=w, dma=True, after=after,
                         cost=nb / 300.0)

        def V(ap2d, nchunk, n):
            return ap2d[:, 0:nchunk * n].rearrange("p (c m) -> p c m", m=n)

        def bfview(b):
            return b[:, 0:512].bitcast(BF16)

        def sigmoid_inplace(buf_ap, key):
            act(buf_ap, buf_ap, AF.Ln, r=[key], w=[key], bias=1.0, scale=1.0)
            act(buf_ap, buf_ap, AF.Exp, r=[key], w=[key], scale=-1.0)

        dma(cf[:, :], cf_d[:, :], r=[], w=["cf"])
        dma(cols[:, :], cols_d[:, :], r=[], w=["cols"])
        dma(walpha[:, :].rearrange("r (l c) -> r l c", l=NL), w_alpha.rearrange("l r c -> r l c"), r=[], w=["walpha"])
        ts("dve", negc[:, :], cols[:, :], -1.0, r=["cols"], w=["negc"])
        cp("dve", ident_bf[:, :], cf[:, CF_IDENT:CF_IDENT + 128], r=["cf"], w=["ident_bf"])
        for h in range(4):
            cp("dve", mask4[:, h * 128:(h + 1) * 128], cf[:, CF_CAUS:CF_CAUS + 128], r=["cf"], w=["mask4"])
            cp("dve", maskm4[:, h * NM:(h + 1) * NM], cf[:, CF_MASKM:CF_MASKM + NM], r=["cf"], w=["maskm4"])
        memset("dve", uextmeta[:, :], 0.0, w=["uextmeta"])
        memset("dve", xnT[2][:, :], 0.0, w=["xnT2"])
        for i in range(2):
            memset("dve", xnT[i][:, :], 0.0, w=["xnT%d" % i])
            memset("dve", oTg[i][:, :], 0.0, w=["oTg%d" % i])
            memset("dve", ypTg[i][:, :], 0.0, w=["ypTg%d" % i])

        ident_f = cf[:, CF_IDENT:CF_IDENT + 128]
        last_pe_of_phase = [None]

        def ntok(t):
            return NM if t == 0 else 128

        def w_groups(l, ph, dst):
            wv = w_in[l].rearrange("(kc p) n -> p kc n", p=128)
            out = []
            if ph == 1:
                W1d = dst[:, 0:8 * P1COLS].rearrange("p (kc n) -> p kc n", n=P1COLS)
                for name, c0, c1 in (("al", C_AL, C_AL + 16), ("qk", 0, 1024), ("v", 1024, 2048),
                                     ("ga", 2048, 3072), ("ugb", 3072, 4096)):
                    out.append(("W1%s_%d" % (name, l), W1d[:, :, c0:c1], wv[:, :, c0:c1]))
                out.append(("W1pw_%d" % l, dst[:, POOLW_OFF:POOLW_OFF + 512].rearrange("p (g d) -> p g d", g=4),
                            pool_w[l].rearrange("g c d -> c g d")))
            else:
                Wmgd = dst[:, 0:16384].rearrange("p (kc n) -> p kc n", n=2048)
                out.append(("W2mga_%d" % l, Wmgd[:, :, 0:1024], wv[:, :, C_MG:C_MG + 1024]))
                out.append(("W2mgb_%d" % l, Wmgd[:, :, 1024:2048], wv[:, :, C_MG + 1024:C_MG + 2048]))
                out.append(("W2a_%d" % l, dst[:, 16384:24576].rearrange("p (kc n) -> p kc n", n=1024),
                            w_a[l].rearrange("(kc p) n -> p kc n", p=128)))
                out.append(("W2b_%d" % l, dst[:, 24576:28672].rearrange("p (kc n) -> p kc n", n=1024),
                            w_b[l].rearrange("(kc p) n -> p kc n", p=128)))
                out.append(("W2o_%d" % l, dst[:, 28672:36864].rearrange("p (kc n) -> p kc n", n=1024),
                            w_out[l].rearrange("(kc p) n -> p kc n", p=128)))
            return out

        def stage_pieces():
            out = []
            for (l_, ph_) in ((0, 2), (1, 1), (1, 2)):
                for key, o_, i_ in w_groups(l_, ph_, WS[(l_, ph_)]):
                    shp = list(o_.shape)
                    if len(shp) == 3 and shp[1] in (4, 8) and not key.startswith("W1pw"):
                        nk = shp[1]
                        for kc in range(nk):
                            out.append((key, "%s_%d" % (key, kc), o_[:, kc, :], i_[:, kc, :]))
                    else:
                        out.append((key, key, o_, i_))
            return out

        stage_keys = {}

        def emit_stage(pieces, k):
            aft = (len(S.ops) - 1,)
            for _ in range(k):
                if not pieces:
                    return
                key, sub, o_, i_ = pieces.pop(0)
                stage_keys.setdefault(key, []).append("S" + sub)
                dma(o_, i_, r=[], w=["S" + sub], eng="pool", after=aft, max_dma_last_dim=4096)

        deferred_w = {}

        def load_w(l, ph, aft=None, only=None):
            if aft is None:
                aft = tuple(phase_pe_ops)
            late = ("W1ga", "W1ugb", "W1pw")
            if ph == 1 and only is None and DEFER_GA_AT >= 0:
                deferred_w[l] = aft
                only = "early"
            if (l, ph) not in WS:
                for key, o_, i_ in w_groups(l, ph, WB[:, :]):
                    il = key.startswith(late)
                    if (only == "early" and il) or (only == "late" and not il):
                        continue
                    dma(o_, i_, r=[], w=[key], eng="pool", after=aft, max_dma_last_dim=4096)
            else:
                src = w_groups(l, ph, WS[(l, ph)])
                for (key, o_, _), (_, so_, _) in zip(w_groups(l, ph, WB[:, :]), src):
                    il = key.startswith(late)
                    if (only == "early" and il) or (only == "late" and not il):
                        continue
                    dma(o_, so_, r=stage_keys.get(key, ["S" + key]), w=[key], after=aft, eng=STAGED_Q)

        def load_w_p1(l):
            load_w(l, 1)

        def load_w_p2(l):
            load_w(l, 2)

        def load_h(l, t):
            hs = t % 3
            n = ntok(t)
            key = "hb%d" % hs
            if l == 0:
                if t == 0:
                    dma(hb[hs][0:64, :], xs[:, :], r=[], w=[key])
                    dma(hb[hs][64:80, :], meta[:, :], r=[], w=[key + "m"])
                else:
                    dma(hb[hs][0:128, :], xp[(t - 1) * 128:t * 128, :], r=[], w=[key, key + "m"])
            else:
                dma(hb[hs][0:n, :], H1[t * 128:t * 128 + n, :], r=["H1_%d" % t], w=[key, key + "m"])

        def hkeys(l, t):
            hs = t % 3
            return ["hb%d" % hs, "hb%dm" % hs]

        def rms_stats(src_ap, n, rkeys, s2=0):
            xb, xbk, so, sk = xsbs[s2], "xsb%d" % s2, 4 * s2, "ss%d" % s2
            act(xb[0:n, :], src_ap, AF.Square, r=rkeys, w=[xbk, sk], accum_out=ss[0:n, so:so + 1])
            act(ss[0:n, so + 1:so + 2], ss[0:n, so:so + 1], AF.Ln, r=[sk], w=[sk], bias=EPS, scale=1.0 / D)
            act(ss[0:n, so + 2:so + 3], ss[0:n, so + 1:so + 2], AF.Exp, r=[sk], w=[sk], scale=-0.5)

        def front(l, t):
            n = ntok(t)
            s3, s2, hs, cb = t % 3, t % 2, t % 3, l * 40
            xk = "xnT%d" % s3
            xn3 = xnT[s3][:, :].rearrange("p (k m) -> p k m", m=128)
            hk = hkeys(l, t)
            xb, xbk, so, sk = xsbs[s2], "xsb%d" % s2, 4 * s2, "ss%d" % s2
            rms_stats(hb[hs][0:n, :], n, hk, s2)
            ts("dve", xb[0:n, :], hb[hs][0:n, :], ss[0:n, so + 2:so + 3], r=hk + [sk, xbk], w=[xbk])
            yield ("need", 1)
            bt, bk = bank()
            btb = bfview(bt)
            for kc in range(8):
                tr(btb[:, kc * n:(kc + 1) * n], xb[0:n, kc * 128:(kc + 1) * 128], ident_bf[0:n, 0:n],
                   r=[xbk, "ident_bf"], w=[bk])
            tt("dve", xn3[:, :, 0:n], V(btb, 8, n),
               cols[:, cb + CL_G:cb + CL_G + 8].unsqueeze(2).to_broadcast([128, 8, n]), ALU.mult,
               r=[bk, "cols"], w=[xk])
            free(bk)
            dma(XN[t], xnT[s3][:, :], r=[xk], w=["XN_%d" % t])

        def phase1_tile(l, t):
            n = ntok(t)
            sl = t % 2
            s3 = t % 3
            cb = l * 40
            W1 = WB[:, 0:8 * P1COLS].rearrange("p (kc n) -> p kc n", n=P1COLS)
            xk = "xnT%d" % s3
            xn3 = xnT[s3][:, :].rearrange("p (k m) -> p k m", m=128)
            if t + 2 < NT:
                load_h(l, t + 2)
            if t == 0:
                yield from front(l, 0)

            def proj(dst, c0, m, wkey, bkey):
                for kc in range(8):
                    mm(dst, W1[:, kc, c0:c0 + m], xn3[:, kc, 0:n], kc == 0, kc == 7, r=[wkey, xk], w=[bkey])

            yield ("need", 1)
            ba, bak = bank()
            proj(ba[0:16, 0:n], C_AL, 16, "W1al_%d" % l, bak)
            cp("act", alow_sb[0:16, 0:n], ba[0:16, 0:n], r=[bak], w=["alow"])
            free(bak)
            yield ("need", 1)
            bl, blk = bank()
            for h in range(4):
                mm(bl[:, h * n:(h + 1) * n], walpha[0:16, l * 512 + h * 128:l * 512 + (h + 1) * 128],
                   alow_sb[0:16, 0:n], True, True, r=["walpha", "alow"], w=[blk])
            for h in range(4):
                act(tmpA[:, h * n:(h + 1) * n], bl[:, h * n:(h + 1) * n], AF.Exp, r=[blk, "negc"], w=["tmpA"],
                    bias=negc[:, cb + CL_BA + h:cb + CL_BA + h + 1], scale=-1.0)
            free(blk)
            act(tmpA[:, 0:4 * n], tmpA[:, 0:4 * n], AF.Ln, r=["tmpA"], w=["tmpA"], bias=1.0, scale=1.0)
            rmc = CF_RMM if t == 0 else CF_RMP
            S.add("dve", lambda e: e.tensor_tensor_scan(out=tmpB[:, 0:4 * n], data0=cf[:, rmc:rmc + 4 * n],
                                                        data1=tmpA[:, 0:4 * n], initial=0.0, op0=ALU.mult,
                                                        op1=ALU.add),
                  r=["cf", "tmpA"], w=["tmpB"], cost=80.0 + 2.1 * 4 * n, lat=120.0)
            act(tmpA[:, 0:4 * n], tmpB[:, 0:4 * n], AF.Exp, r=["tmpB"], w=["tmpA"], scale=-1.0 / 16.0)
            act(tmpC[:, 0:4 * n], tmpB[:, 0:4 * n], AF.Exp, r=["tmpB"], w=["tmpC"], scale=1.0 / 16.0)
            if t == 0:
                tb3 = V(tmpB, 4, NM)
                act(V(ebcm, 4, 17)[:, :, 0:16].unsqueeze(3),
                    tb3[:, :, 0:64].rearrange("p h (s t) -> p h s t", t=TS)[:, :, :, TS - 1:TS],
                    AF.Exp, r=["tmpB"], w=["ebcm"], scale=-1.0 / 16.0)
                act(V(ebcm, 4, 17)[:, :, 16:17], tb3[:, :, NM - 1:NM], AF.Exp, r=["tmpB"], w=["ebcm"],
                    scale=-1.0 / 16.0)
            else:
                act(ebc[:, 0:4].unsqueeze(2), V(tmpB, 4, 128)[:, :, 127:128], AF.Exp, r=["tmpB"], w=["ebc"],
                    scale=-1.0 / 16.0)
            if t + 1 < NT:
                yield from front(l, t + 1)
            yield ("need", 1)
            bq, bqk = bank()
            for h in range(4):
                proj(bq[:, h * n:(h + 1) * n], C_Q + h * 128, 128, "W1qk_%d" % l, bqk)
            stt(qT[:, 0:4 * n], bq[:, 0:4 * n], 128.0 ** -0.5, tmpA[:, 0:4 * n], ALU.mult, ALU.mult,
                r=[bqk, "tmpA"], w=["qT"])
            free(bqk)
            yield ("need", 1)
            bkb, bkk = bank()
            for h in range(4):
                proj(bkb[:, h * n:(h + 1) * n], C_K + h * 128, 128, "W1qk_%d" % l, bkk)
            tt("dve", kT[:, 0:4 * n], bkb[:, 0:4 * n], tmpC[:, 0:4 * n], ALU.mult, r=[bkk, "tmpC"], w=["kT"])
            free(bkk)
            yield ("need", 2)
            bv = [bank(), bank()]
            for b in range(2):
                for kc in range(8):
                    mm(bv[b][0][0:n, 0:512], xn3[:, kc, 0:n], W1[:, kc, C_V + b * 512:C_V + (b + 1) * 512],
                       kc == 0, kc == 7, r=["W1v_%d" % l, xk], w=[bv[b][1]])
                cp("act", v_tok[0:n, b * 512:(b + 1) * 512], bv[b][0][0:n, 0:512], r=[bv[b][1]], w=["v_tok"])
                free(bv[b][1])
            if t == 0:
                k3 = V(kT, 4, NM)
                d3 = V(kdT, 4, NM)
                e3 = V(ebcm, 4, 17)
                tt("pool", d3[:, :, 0:64].rearrange("p h (s t) -> p h s t", t=TS),
                   k3[:, :, 0:64].rearrange("p h (s t) -> p h s t", t=TS),
                   e3[:, :, 0:16].unsqueeze(3).to_broadcast([128, 4, NSS, TS]), ALU.mult,
                   r=["kT", "ebcm"], w=["kdT"])
                tt("pool", d3[:, :, 64:NM], k3[:, :, 64:NM], e3[:, :, 16:17].to_broadcast([128, 4, NMETA]),
                   ALU.mult, r=["kT", "ebcm"], w=["kdT"])
            else:
                tt("pool", V(kdT, 4, 128), V(kT, 4, 128), ebc[:, 0:4].unsqueeze(2).to_broadcast([128, 4, 128]),
                   ALU.mult, r=["kT", "ebc"], w=["kdT"])
            yield ("need", 1)
            bt2, bt2k = bank()
            bt2b = bfview(bt2)
            for h in range(4):
                tr(bt2b[0:n, h * 128:(h + 1) * 128], kdT[:, h * n:(h + 1) * n], ident_bf[:, :],
                   r=["kdT", "ident_bf"], w=[bt2k])
            cp("act", kd_tok[0:n, 0:512], bt2b[0:n, 0:512], r=[bt2k], w=["kd_tok"])
            free(bt2k)
            yield ("need", 1)
            bA, bAk = bank()
            for h in range(4):
                mm(bA[0:n, h * n:(h + 1) * n], kT[:, h * n:(h + 1) * n], qT[:, h * n:(h + 1) * n], True, True,
                   r=["kT", "qT"], w=[bAk])
            mk = maskm4 if t == 0 else mask4
            tt("dve", ATm[0:n, 0:4 * n], bA[0:n, 0:4 * n], mk[0:n, 0:4 * n], ALU.mult,
               r=[bAk, "maskm4" if t == 0 else "mask4"], w=["ATm"])
            free(bAk)

            if t == 0:
                yield ("need", 2)
                bo = [bank(), bank()]
                for h in range(4):
                    b_, bk_ = bo[h // 2]
                    off = (h % 2) * 256
                    mm(b_[0:n, off:off + 256], ATm[0:n, h * n:(h + 1) * n], v_tok[0:n, h * 256:(h + 1) * 256],
                       h % 2 == 0, False, r=["ATm", "v_tok"], w=[bk_], sgc=True)
                cp("pool", qT_m[:, 0:4 * NM], qT[:, 0:4 * NM], r=["qT"], w=["qT_m"])
                cp("pool", kd_tok_m[0:n, :], kd_tok[0:n, :], r=["kd_tok"], w=["kd_tok_m"])
                cp("pool", v_tok_m[0:n, :], v_tok[0:n, :], r=["v_tok"], w=["v_tok_m"])
                ts("pool", kdm[0][0:n, :], kd_tok[0:n, :], cf[0:n, CF_ROWMASK + NSS:CF_ROWMASK + NSS + 1],
                   r=["kd_tok", "cf"], w=["kdm0"], s2=1.0, op1=ALU.mult)
                yield ("need", 2)
                bp = [bank(), bank()]
                for h in range(4):
                    b_, bk_ = bp[h // 2]
                    off = (h % 2) * 256
                    mm(b_[:, off:off + 256], kdm[0][0:n, h * 128:(h + 1) * 128],
                       v_tok[0:n, h * 256:(h + 1) * 256], True, True, r=["kdm0", "v_tok"], w=[bk_])
                for hp in range(2):
                    cp("dve", Sst2[0][:, hp * 512:(hp + 1) * 512], bp[hp][0][:, 0:512], r=[bp[hp][1]], w=["Sst0"])
                    free(bp[hp][1])
                yield "M"
                NSL = len(S_in)

                def ld(sn):
                    dma(S_in[sn % NSL][:, :].rearrange("p (h v) -> p h v", h=4),
                        sgla[l, sn].rearrange("h d v -> d h v"), r=[], w=["S_in%d" % (sn % NSL)])

                for s0 in range(NSL - 1):
                    ld(s0)
                for s in range(NSS):
                    i2 = s % 2
                    i3 = s % NSL
                    sk_ = "S_in%d" % i3
                    if s == DEFER_GA_AT and l in deferred_w:
                        load_w(l, 1, aft=deferred_w.pop(l) + (len(S.ops) - 1,), only="late")
                    if s + NSL - 1 < NSS:
                        ld(s + NSL - 1)
                    ts("pool", kdm[i2][0:n, :], kd_tok_m[0:n, :], cf[0:n, CF_ROWMASK + s:CF_ROWMASK + s + 1],
                       r=["kd_tok_m", "cf"], w=["kdm%d" % i2], s2=1.0, op1=ALU.mult)
                    cp("act", S_bfs[i2][:, :], S_in[i3][:, :], r=[sk_], w=["S_bfs%d" % i2])
                    tt("pool", V(qm[i2], 4, NM), V(qT_m, 4, NM),
                       colmask[:, s * NM:(s + 1) * NM].unsqueeze(1).to_broadcast([128, 4, NM]), ALU.mult,
                       r=["qT_m", "colmask"], w=["qm%d" % i2])
                    for h in range(4):
                        b_, bk_ = bo[h // 2]
                        off = (h % 2) * 256
                        mm(b_[0:n, off:off + 256], qm[i2][:, h * NM:(h + 1) * NM],
                           S_bfs[i2][:, h * 256:(h + 1) * 256], False, s == NSS - 1,
                           r=["qm%d" % i2, "S_bfs%d" % i2], w=[bk_], sgc=True)
                    for hp in range(2):
                        yield ("need", 1)
                        b_, bk_ = bank()
                        for h in (2 * hp, 2 * hp + 1):
                            off = (h % 2) * 256
                            mm(b_[:, off:off + 256], kdm[i2][0:n, h * 128:(h + 1) * 128],
                               v_tok_m[0:n, h * 256:(h + 1) * 256], True, True, r=["kdm%d" % i2, "v_tok_m"], w=[bk_])
                        for h in (2 * hp, 2 * hp + 1):
                            off = (h % 2) * 256
                            stt(S_in[i3][:, h * 256:(h + 1) * 256], S_in[i3][:, h * 256:(h + 1) * 256],
                                ebcm[:, h * 17 + s:h * 17 + s + 1], b_[:, off:off + 256], ALU.mult, ALU.add,
                                r=[sk_, "ebcm", bk_], w=[sk_])
                        free(bk_)
                    dma(nsg_s[l, s].rearrange("h d v -> d h v"),
                        S_in[i3][:, :].rearrange("p (h v) -> p h v", h=4), r=[sk_], w=[])
                if l in deferred_w:
                    load_w(l, 1, aft=deferred_w.pop(l) + (len(S.ops) - 1,), only="late")
                obanks = [(bo[h // 2][0][0:n, (h % 2) * 256:(h % 2) * 256 + 256], bo[h // 2][1]) for h in range(4)]
                obk = [bo[0][1], bo[1][1]]
            else:
                So, Sn = Sst2[(t - 1) % 2], Sst2[t % 2]
                Sok, Snk = "Sst%d" % ((t - 1) % 2), "Sst%d" % (t % 2)
                cp("pool", S_bf[:, :], So[:, :], r=[Sok], w=["S_bf"])
                yield ("need", 2)
                bo = [bank(), bank()]
                for h in range(4):
                    b_, bk_ = bo[h // 2]
                    off = (h % 2) * 256
                    mm(b_[0:n, off:off + 256], ATm[0:n, h * n:(h + 1) * n], v_tok[0:n, h * 256:(h + 1) * 256],
                       True, False, r=["ATm", "v_tok"], w=[bk_])
                    mm(b_[0:n, off:off + 256], qT[:, h * n:(h + 1) * n], S_bf[:, h * 256:(h + 1) * 256],
                       False, True, r=["qT", "S_bf"], w=[bk_])
                yield ("need", 2)
                bp = [bank(), bank()]
                for h in range(4):
                    b_, bk_ = bp[h // 2]
                    off = (h % 2) * 256
                    mm(b_[:, off:off + 256], kd_tok[0:n, h * 128:(h + 1) * 128], v_tok[0:n, h * 256:(h + 1) * 256],
                       True, True, r=["kd_tok", "v_tok"], w=[bk_])
                for h in range(4):
                    b_, bk_ = bp[h // 2]
                    off = (h % 2) * 256
                    stt(Sn[:, h * 256:(h + 1) * 256], So[:, h * 256:(h + 1) * 256], ebc[:, h:h + 1],
                        b_[:, off:off + 256], ALU.mult, ALU.add, r=[Sok, "ebc", bk_], w=[Snk])
                free(bp[0][1])
                free(bp[1][1])
                if t == NT - 1:
                    dma(nsg_p[l].rearrange("h d v -> d h v"), Sn[:, :].rearrange("p (h v) -> p h v", h=4),
                        r=[Snk], w=[])
                obanks = [(bo[h // 2][0][0:n, (h % 2) * 256:(h % 2) * 256 + 256], bo[h // 2][1]) for h in range(4)]
                obk = [bo[0][1], bo[1][1]]
            if t != 0:
                yield "M"

            for h in range(4):
                oap, ok_ = obanks[h]
                act(on[0:n, h * 256:(h + 1) * 256], oap, AF.Square, r=[ok_], w=["on", "ss4"],
                    accum_out=ss4[0:n, h:h + 1])
            act(ss4[0:n, 4:8], ss4[0:n, 0:4], AF.Ln, r=["ss4"], w=["ss4"], bias=EPS, scale=1.0 / 256.0)
            act(ss4[0:n, 8:12], ss4[0:n, 4:8], AF.Exp, r=["ss4"], w=["ss4"], scale=-0.5)
            yield
            for h in range(4):
                oap, ok_ = obanks[h]
                act(on[0:n, h * 256:(h + 1) * 256], oap, AF.Copy, r=[ok_, "ss4", "on"], w=["on"],
                    scale=ss4[0:n, 8 + h:9 + h])
            for k_ in obk:
                free(k_)
            gcols = cols[:, cb + CL_GAIN:cb + CL_GAIN + 8]
            for i in range(2):
                yield ("need", 1)
                b_, bk_ = bank()
                for c4 in range(4):
                    proj(b_[:, c4 * n:(c4 + 1) * n], C_GA + (4 * i + c4) * 128, 128, "W1ga_%d" % l, bk_)
                egi = eg[:, i * 4 * n:(i + 1) * 4 * n]
                act(egi, b_[:, 0:4 * n], AF.Exp, r=[bk_], w=["eg%d" % i], scale=-1.0)
                sigmoid_inplace(egi, "eg%d" % i)
                tt("pool", V(egi, 4, n), V(egi, 4, n),
                   gcols[:, 4 * i:4 * i + 4].unsqueeze(2).to_broadcast([128, 4, n]), ALU.mult,
                   r=["eg%d" % i, "cols"], w=["eg%d" % i])
                tt("dve", egi, b_[:, 0:4 * n], egi, ALU.mult, r=[bk_, "eg%d" % i], w=["eg%d" % i])
                free(bk_)
            yield ("need", 1)
            bT, bTk = bank()
            bTb = bfview(bT)
            for c in range(8):
                tr(bTb[:, c * n:(c + 1) * n], on[0:n, c * 128:(c + 1) * 128], ident_bf[0:n, 0:n],
                   r=["on", "ident_bf"], w=[bTk])
            ok2 = "oTg%d" % sl
            tt("dve", V(oTg[sl], 8, 128)[:, :, 0:n], V(bTb, 8, n), V(eg, 8, n), ALU.mult, r=[bTk, "eg0", "eg1"],
               w=[ok2])
            free(bTk)
            dma(OT[t], oTg[sl][:, :], r=[ok2], w=["OT_%d" % t])

            yield ("need", 1)
            bu, buk = bank()
            for g in range(4):
                proj(bu[:, g * n:(g + 1) * n], C_U + g * 128, 128, "W1ugb_%d" % l, buk)
            z3 = V(zT, 4, n)
            WIN = (2, 4, 8, 16)
            if t == 0:
                for rt in range(2):
                    dma(sp_tok[0:120, rt * 512:(rt + 1) * 512],
                        spool[l].rearrange("s p c -> (s p) c")[rt * 120:(rt + 1) * 120, :], r=[], w=["sp_tok%d" % rt])
                dma(nsp_s[l, :, 0:11, :], spool[l, :, 4:15, :], r=[], w=[])
                um = uextm[:, :].rearrange("p (g s e) -> p g s e", g=4, s=NSS)
                am = sAm[:, :].rearrange("p (g s e) -> p g s e", g=4, s=NSS)
                bm_ = sBm[:, :].rearrange("p (g s e) -> p g s e", g=4, s=NSS)
                for rt in range(2):
                    yield ("need", 1)
                    bs, bsk = bank()
                    for g in range(4):
                        tr(bs[:, g * 120:(g + 1) * 120], sp_tok[0:120, rt * 512 + g * 128:rt * 512 + (g + 1) * 128],
                           ident_f[0:120, 0:120], r=["sp_tok%d" % rt, "cf"], w=[bsk])
                    cp("act", um[:, :, rt * 8:(rt + 1) * 8, 0:15],
                       bs[:, 0:480].rearrange("p (g s e) -> p g s e", g=4, s=8), r=[bsk], w=["S_in0", "S_in1"])
                    free(bsk)
                u3 = V(bu, 4, NM)
                cp("act", um[:, :, :, 15:19], u3[:, :, 0:64].rearrange("p g (s t) -> p g s t", t=TS), r=[buk],
                   w=["S_in0", "S_in1"])
                ume = V(uextmeta, 4, 31)
                cp("act", ume[:, :, 15:31], u3[:, :, 64:NM], r=[buk], w=["uextmeta"])
                cp("act", u_s_perm[:, :].rearrange("p (g t s) -> p g s t", g=4, t=TS),
                   u3[:, :, 0:64].rearrange("p g (s t) -> p g s t", t=TS), r=[buk], w=["u_s_perm"])
                free(buk)
                yield
                tt("pool", am[:, :, :, 1:19], um[:, :, :, 1:19], um[:, :, :, 0:18], ALU.add, r=["S_in0", "S_in1"], w=["S_in1", "S_in2"])
                tt("pool", bm_[:, 1:4, :, 3:19], am[:, 1:4, :, 3:19], am[:, 1:4, :, 1:17], ALU.add, r=["S_in1", "S_in2"],
                   w=["S_in2", "S_in3"])
                tt("pool", am[:, 2:4, :, 7:19], bm_[:, 2:4, :, 7:19], bm_[:, 2:4, :, 3:15], ALU.add, r=["S_in2", "S_in3"],
                   w=["S_in1", "S_in2"])
                tt("pool", bm_[:, 3:4, :, 15:19], am[:, 3:4, :, 15:19], am[:, 3:4, :, 7:11], ALU.add, r=["S_in1", "S_in2"],
                   w=["S_in2", "S_in3"])
                for g in range(4):
                    src = am if g % 2 == 0 else bm_
                    stt(z3[:, g, 0:64].rearrange("p (s t) -> p s t", t=TS), src[:, g, :, 15:19], 1.0 / WIN[g],
                        um[:, g, :, 15:19], ALU.mult, ALU.subtract,
                        r=["S_in0", "S_in1", "S_in2", "S_in3"], w=["zT"])
                yield
                a3 = V(sA, 4, 143)
                b3 = V(sB, 4, 143)
                tt("pool", a3[:, :, 1:31], ume[:, :, 1:31], ume[:, :, 0:30], ALU.add, r=["uextmeta"], w=["sA"])
                tt("pool", b3[:, 1:4, 3:31], a3[:, 1:4, 3:31], a3[:, 1:4, 1:29], ALU.add, r=["sA"], w=["sB"])
                tt("pool", a3[:, 2:4, 7:31], b3[:, 2:4, 7:31], b3[:, 2:4, 3:27], ALU.add, r=["sB"], w=["sA"])
                tt("pool", b3[:, 3:4, 15:31], a3[:, 3:4, 15:31], a3[:, 3:4, 7:23], ALU.add, r=["sA"], w=["sB"])
                icn = cf[:, CF_INVCNT:CF_INVCNT + 64].rearrange("p (g e) -> p g e", g=4)
                for g in range(4):
                    src = a3 if g % 2 == 0 else b3
                    tt("pool", src[:, g, 15:31], src[:, g, 15:31], icn[:, g, :], ALU.mult, r=["sA", "sB", "cf"],
                       w=["sA", "sB"])
                    tt("dve", z3[:, g, 64:NM], src[:, g, 15:31], ume[:, g, 15:31], ALU.subtract,
                       r=["sA", "sB", "uextmeta"], w=["zT"])
                cp("pool", V(uext, 4, 143)[:, :, 0:15], ume[:, :, 16:31], r=["uextmeta"], w=["uext"])
                yield ("need", 1)
                bx, bxk = bank()
                for g in range(4):
                    tr(bx[0:64, g * 128:(g + 1) * 128], u_s_perm[:, g * 64:(g + 1) * 64], ident_f[:, :],
                       r=["u_s_perm", "cf"], w=[bxk])
                cp("act", utok[0:64, :], bx[0:64, :], r=[bxk], w=["utok"])
                free(bxk)
                for tt_ in range(TS):
                    dma(nsp_s[l, :, 11 + tt_, :], utok[tt_ * NSS:(tt_ + 1) * NSS, :], r=["utok"], w=[])
            else:
                ue = V(uext, 4, 143)
                a3 = V(sA, 4, 143)
                b3 = V(sB, 4, 143)
                cp("act", ue[:, :, 15:143], V(bu, 4, 128), r=[buk], w=["uext"])
                free(buk)
                yield
                tt("pool", a3[:, :, 1:143], ue[:, :, 1:143], ue[:, :, 0:142], ALU.add, r=["uext"], w=["sA"])
                tt("pool", b3[:, 1:4, 3:143], a3[:, 1:4, 3:143], a3[:, 1:4, 1:141], ALU.add, r=["sA"], w=["sB"])
                tt("pool", a3[:, 2:4, 7:143], b3[:, 2:4, 7:143], b3[:, 2:4, 3:139], ALU.add, r=["sB"], w=["sA"])
                tt("pool", b3[:, 3:4, 15:143], a3[:, 3:4, 15:143], a3[:, 3:4, 7:135], ALU.add, r=["sA"], w=["sB"])
                for g in range(4):
                    src = a3 if g % 2 == 0 else b3
                    stt(z3[:, g, :], src[:, g, 15:143], 1.0 / WIN[g], ue[:, g, 15:143], ALU.mult, ALU.subtract,
                        r=["sA", "sB", "uext"], w=["zT"])
                if t == NT - 1:
                    yield ("need", 1)
                    bx, bxk = bank()
                    for g in range(4):
                        tr(bx[0:15, g * 128:(g + 1) * 128], ue[:, g, 128:143], ident_f[:, :], r=["uext", "cf"],
                           w=[bxk])
                    cp("act", utok[0:15, :], bx[0:15, :], r=[bxk], w=["utok"])
                    free(bxk)
                    dma(nsp_p[l], utok[0:15, :], r=["utok"], w=[])
                else:
                    cp("pool", ue[:, :, 0:15], ue[:, :, 128:143], r=["uext"], w=["uext"])
            yield ("need", 1)
            bgb, bgbk = bank()
            for g in range(4):
                proj(bgb[:, g * n:(g + 1) * n], C_GB + g * 128, 128, "W1ugb_%d" % l, bgbk)
            act(eb[:, 0:4 * n], bgb[:, 0:4 * n], AF.Exp, r=[bgbk], w=["eb"], scale=-1.0)
            sigmoid_inplace(eb[:, 0:4 * n], "eb")
            tt("pool", V(eb, 4, n), V(eb, 4, n),
               cols[:, cb + CL_PS:cb + CL_PS + 4].unsqueeze(2).to_broadcast([128, 4, n]), ALU.mult,
               r=["eb", "cols"], w=["eb"])
            tt("dve", eb[:, 0:4 * n], bgb[:, 0:4 * n], eb[:, 0:4 * n], ALU.mult, r=[bgbk, "eb"], w=["eb"])
            free(bgbk)
            yield ("need", 1)
            by, byk = bank()
            pw = WB[:, POOLW_OFF:POOLW_OFF + 512].rearrange("p (g d) -> p g d", g=4)
            for g in range(4):
                mm(by[:, g * n:(g + 1) * n], pw[:, g, :], zT[:, g * n:(g + 1) * n], True, True,
                   r=["W1pw_%d" % l, "zT"], w=[byk])
            yk = "ypTg%d" % sl
            tt("dve", V(ypTg[sl], 4, 128)[:, :, 0:n], V(by, 4, n), V(eb, 4, n), ALU.mult, r=[byk, "eb"], w=[yk])
            free(byk)
            dma(YP[t], ypTg[sl][:, :], r=[yk], w=["YP_%d" % t])

        def p2_loads(l, t):
            sl = t % 2
            dma(xnT[sl][:, :], XN[t], r=["XN_%d" % t], w=["xnT%d" % sl])
            dma(oTg[sl][:, :], OT[t], r=["OT_%d" % t], w=["oTg%d" % sl])
            dma(ypTg[sl][:, :], YP[t], r=["YP_%d" % t], w=["ypTg%d" % sl])
            load_h(l, t)

        def phase2_tile(l, t):
            n = ntok(t)
            sl = t % 2
            hs = t % 3
            cb = l * 40
            hk = hkeys(l, t)
            xk, ok2, yk = "xnT%d" % sl, "oTg%d" % sl, "ypTg%d" % sl
            xn3 = V(xnT[sl], 8, 128)
            o3 = V(oTg[sl], 8, 128)
            y3 = V(ypTg[sl], 4, 128)
            Wmg = WB[:, 0:16384].rearrange("p (kc n) -> p kc n", n=2048)
            Wa = WB[:, 16384:24576].rearrange("p (kc n) -> p kc n", n=1024)
            Wb = WB[:, 24576:28672].rearrange("p (kc n) -> p kc n", n=1024)
            Wo = WB[:, 28672:36864].rearrange("p (kc n) -> p kc n", n=1024)
            if t + 1 < NT:
                p2_loads(l, t + 1)
            m3 = V(mergedT[sl], 8, 128)
            mk_ = "mergedT%d" % sl
            for c in range(8):
                yield ("need", 1)
                b_, bk_ = bank()
                es = c % 2
                for kc in range(8):
                    mm(b_[:, 0:n], Wmg[:, kc, c * 128:(c + 1) * 128], xn3[:, kc, 0:n], kc == 0, kc == 7,
                       r=["W2mga_%d" % l, xk], w=[bk_])
                for kc in range(8):
                    mm(b_[:, n:2 * n], Wmg[:, kc, 1024 + c * 128:1024 + (c + 1) * 128], xn3[:, kc, 0:n], kc == 0,
                       kc == 7, r=["W2mgb_%d" % l, xk], w=[bk_])
                for kc in range(8):
                    mm(b_[:, 2 * n:3 * n], Wa[:, kc, c * 128:(c + 1) * 128], o3[:, kc, 0:n], kc == 0, kc == 7,
                       r=["W2a_%d" % l, ok2], w=[bk_])
                for kc in range(4):
                    mm(b_[:, 3 * n:4 * n], Wb[:, kc, c * 128:(c + 1) * 128], y3[:, kc, 0:n], kc == 0, kc == 3,
                       r=["W2b_%d" % l, yk], w=[bk_])
                ek = "e2_%d" % es
                act(e2[es][:, 0:n], b_[:, 0:n], AF.Exp, r=[bk_, "negc"], w=[ek],
                    bias=negc[:, cb + CL_BM + c:cb + CL_BM + c + 1], scale=-1.0)
                act(e2[es][:, n:2 * n], b_[:, n:2 * n], AF.Exp, r=[bk_, "negc"], w=[ek],
                    bias=negc[:, cb + CL_BM + 8 + c:cb + CL_BM + 8 + c + 1], scale=-1.0)
                sigmoid_inplace(e2[es][:, 0:2 * n], ek)
                tk = "tbuf%d" % es
                tt("dve", tbuf[es][:, 0:2 * n], b_[:, 2 * n:4 * n], e2[es][:, 0:2 * n], ALU.mult, r=[bk_, ek], w=[tk])
                free(bk_)
                tt("pool", m3[:, c, 0:n], tbuf[es][:, 0:n], tbuf[es][:, n:2 * n], ALU.add, r=[tk], w=[mk_])
            yield "M"
            yield
            for b in range(2):
                yield ("need", 1)
                b_, bk_ = bank()
                for kc in range(8):
                    mm(b_[0:n, 0:512], m3[:, kc, 0:n], Wo[:, kc, b * 512:(b + 1) * 512], kc == 0, kc == 7,
                       r=["W2o_%d" % l, mk_], w=[bk_])
                tt("dve", hb[hs][0:n, b * 512:(b + 1) * 512], b_[0:n, 0:512], hb[hs][0:n, b * 512:(b + 1) * 512],
                   ALU.add, r=[bk_] + hk, w=hk)
                free(bk_)
            yield
            if l == 0:
                dma(H1[t * 128:t * 128 + n, :], hb[hs][0:n, :], r=hk, w=["H1_%d" % t])
            else:
                rms_stats(hb[hs][0:n, :], n, hk, t % 2)
                so_ = 4 * (t % 2)
                stt(hb[hs][0:n, :], hb[hs][0:n, :], ss[0:n, so_ + 2:so_ + 3], fgf[0:n, :], ALU.mult, ALU.mult,
                    r=hk + ["ss%d" % (t % 2), "eg0", "eg1"], w=hk)
                if t == 0:
                    dma(y_s[:, :], hb[hs][0:64, :], r=hk, w=[])
                else:
                    dma(y_p[(t - 1) * 128:t * 128, :], hb[hs][0:128, :], r=hk, w=[])

        def zipper(gens, tag=None):
            state = {}

            def step(g, must):
                pend = state.get(id(g))
                if pend is not None:
                    if nfree() < pend:
                        if must:
                            raise RuntimeError("PSUM banks exhausted")
                        return "blocked"
                    state[id(g)] = None
                while True:
                    try:
                        S.ctx = gctx.get(id(g))
                        v = next(g)
                    except StopIteration:
                        return "done"
                    if isinstance(v, tuple) and v[0] == "need":
                        if nfree() >= v[1]:
                            continue
                        state[id(g)] = v[1]
                        if must:
                            raise RuntimeError("PSUM banks exhausted (need %d, free %d)" % (v[1], nfree()))
                        return "blocked"
                    return "M" if v == "M" else "ok"

            cur = None
            gctx = {id(g): (tag, gi) for gi, g in enumerate(gens)}
            for g in gens:
                done_cur = cur is None
                while True:
                    if not done_cur:
                        done_cur = step(cur, True) == "done"
                    r = step(g, done_cur)
                    if r == "M":
                        break
                while not done_cur:
                    done_cur = step(cur, True) == "done"
                cur = g
            while step(cur, True) != "done":
                pass

        for l in range(NL):
            load_w_p1(l)
            del phase_pe_ops[:]
            if l == 0:
                dma(colmask[:, :], cm_d[:, :], r=[], w=["colmask"], eng="pool", max_dma_last_dim=4096)
            load_h(l, 0)
            load_h(l, 1)
            gens1 = [phase1_tile(l, t) for t in range(NT)]
            if l == 0:
                st_pieces = stage_pieces()

                def with_staging(g_, k_):
                    emit_stage(st_pieces, k_)
                    yield from g_

                for t_ in range(2, NT):
                    gens1[t_] = with_staging(gens1[t_], STAGE_PER_TILE + (2 if t_ < 12 else 0))
            zipper(gens1, tag="L%dP1" % l)
            for i in range(len(S.ops) - 1, -1, -1):
                if S.ops[i]["eng"] == "pe":
                    last_pe_of_phase[0] = i
                    break
            load_w_p2(l)
            del phase_pe_ops[:]
            if l == NL - 1:
                dma(fgf[:, :], fg_d[:, :], r=[], w=["eg0", "eg1"])
            p2_loads(l, 0)
            gens2 = [phase2_tile(l, t) for t in range(NT)]
            if l == 0:
                for t_ in range(1, NT):
                    gens2[t_] = with_staging(gens2[t_], STAGE_PER_TILE)
            zipper(gens2, tag="L%dP2" % l)
            if l == 0:
                emit_stage(st_pieces, 1000)
            for i in range(len(S.ops) - 1, -1, -1):
                if S.ops[i]["eng"] == "pe":
                    last_pe_of_phase[0] = i
                    break
        S.emit(nc)
    return nc


def _cols_layout(norm_g, b_alpha, gla_gain, pool_scale, b_merge):
    cols = np.zeros((128, NCOLS), np.float32)
    for l in range(NL):
        cb = l * 40
        cols[:, cb + CL_G:cb + CL_G + 8] = norm_g[l].reshape(8, 128).T
        cols[:, cb + CL_BA:cb + CL_BA + 4] = b_alpha[l].reshape(4, 128).T
        cols[:, cb + CL_GAIN:cb + CL_GAIN + 8] = gla_gain[l].reshape(8, 128).T
        cols[:, cb + CL_PS:cb + CL_PS + 4] = pool_scale[l].reshape(4, 128).T
        cols[:, cb + CL_BM:cb + CL_BM + 16] = b_merge[l].reshape(16, 128).T
    return cols


def kernel(x_prompt, x_sample, state_gla, state_pool, meta_tokens, norm_g, w_in, w_alpha, b_alpha, gla_gain,
           w_a, pool_w, pool_scale, w_b, b_merge, w_out, final_norm_g):
    f = lambda a: np.ascontiguousarray(np.asarray(a, dtype=np.float32))
    x_prompt, x_sample, state_gla, state_pool = f(x_prompt), f(x_sample), f(state_gla), f(state_pool)
    meta_tokens, w_in, w_alpha, w_a, pool_w, w_b, w_out = map(f, (meta_tokens, w_in, w_alpha, w_a, pool_w, w_b, w_out))
    cols = _cols_layout(f(norm_g), f(b_alpha), f(gla_gain), f(pool_scale), f(b_merge))
    fgfull = np.ascontiguousarray(np.broadcast_to(f(final_norm_g)[None, :], (128, D)))
    cf = make_consts()
    cm = make_colmask()
    nc = build_nc()
    in_maps = []
    for c in range(NCORE):
        in_maps.append({
            "xp": x_prompt[c],
            "xs": np.ascontiguousarray(x_sample[c * NSS:(c + 1) * NSS].reshape(NSS * TS, D)),
            "meta": meta_tokens,
            "sgla": np.ascontiguousarray(state_gla[:, c * NSS:(c + 1) * NSS]),
            "spool": np.ascontiguousarray(state_pool[:, c * NSS:(c + 1) * NSS]),
            "w_in": w_in, "w_alpha": w_alpha, "w_a": w_a, "pool_w": pool_w, "w_b": w_b, "w_out": w_out,
            "cols": cols, "fgfull": fgfull, "cf": cf, "cm": cm,
        })
    res = run_bass_kernel_spmd(nc, in_maps, core_ids=list(range(NCORE)))
    R = res.results
    y_prompt = np.stack([R[c]["y_p"] for c in range(NCORE)], axis=0)
    y_sample = np.concatenate([R[c]["y_s"].reshape(NSS, TS, D) for c in range(NCORE)], axis=0)
    nsg_p = np.stack([R[c]["nsg_p"] for c in range(NCORE)], axis=1)
    nsp_p = np.stack([R[c]["nsp_p"] for c in range(NCORE)], axis=1)
    nsg_s = np.concatenate([R[c]["nsg_s"] for c in range(NCORE)], axis=1)
    nsp_s = np.concatenate([R[c]["nsp_s"] for c in range(NCORE)], axis=1)
    return (y_prompt.astype(np.float32), y_sample.astype(np.float32), nsg_p.astype(np.float32),
            nsp_p.astype(np.float32), nsg_s.astype(np.float32), nsp_s.astype(np.float32))
```
